# Optimizing a Trainium2 kernel written in Bass

```python
import math
import jax, jax.numpy as jnp
from jax import lax
import numpy as np

D_MODEL = 2048
BATCH = 8
SEQ = 4096
DEPTH = 4
DEC_BATCH = 4
DEC_SEQ = 4096
PAST_LEN = 128

N_MIXERS = 2
MIX_WIDTH = 3 * D_MODEL // 4
MEM_WIDTH = D_MODEL // 4
HEAD_DIM = 128
N_Q_HEADS = MIX_WIDTH // HEAD_DIM
N_KV_HEADS = 4
GQA_GROUP = N_Q_HEADS // N_KV_HEADS
KV_WIDTH = N_KV_HEADS * HEAD_DIM
WINDOW = 128
BLOCK = 128
ROPE_THETA = 10000.0
N_MEM = 256
N_MEM_HEADS = 4
MEM_HEAD_DIM = MEM_WIDTH // N_MEM_HEADS
SSM_GROUP = 16
SSM_GROUPS = MIX_WIDTH // SSM_GROUP
SSM_STATE = 64
SSM_CHUNK = 128
D_FF = 4 * D_MODEL
ALPHA = (2 * DEPTH) ** 0.25
BETA = (8 * DEPTH) ** -0.25
N_SSM_LAYERS = (DEPTH + 1) // 2
N_ATTN_LAYERS = DEPTH // 2
LN_EPS = 1e-5

kernel_name = 'hybrid_s5_window_gqa_memory_encoder'


def layer_norm(x, g, b):
    xf = x.astype(jnp.float32)
    mu = xf.mean(-1, keepdims=True)
    var = jnp.mean(jnp.square(xf - mu), -1, keepdims=True)
    return ((xf - mu) * lax.rsqrt(var + LN_EPS) * g.astype(jnp.float32) + b.astype(jnp.float32)).astype(x.dtype)


def _ssm_combine(left, right):
    a_l, b_l = left
    a_r, b_r = right
    return a_r * a_l, a_r * b_l + b_r


def s5_scan(u, lam_re, lam_im, log_dt, b_re, b_im, c_re, c_im):
    bsz, L = u.shape[0], u.shape[1]
    f32 = jnp.float32
    lam = lax.complex(lam_re.astype(f32), lam_im.astype(f32))
    dt = jnp.exp(log_dt.astype(f32))[:, None]
    lam_bar = jnp.exp(lam * dt)
    b_bar = ((lam_bar - 1.0) / lam)[..., None] * lax.complex(b_re.astype(f32), b_im.astype(f32))
    cmat = lax.complex(c_re.astype(f32), c_im.astype(f32))
    n_chunks = L // SSM_CHUNK
    uc = u.astype(f32).reshape(bsz, n_chunks, SSM_CHUNK, SSM_GROUPS, SSM_GROUP).swapaxes(0, 1)

    def step(carry, u_blk):
        bu = jnp.einsum('gpc,btgc->btgp', b_bar, u_blk)
        a = jnp.broadcast_to(lam_bar, bu.shape)
        a_cum, h = lax.associative_scan(_ssm_combine, (a, bu), axis=1)
        h = h + a_cum * carry[:, None]
        y = jnp.einsum('gcp,btgp->btgc', cmat, h).real
        return h[:, -1], y

    carry0 = jnp.zeros((bsz, SSM_GROUPS, SSM_STATE), jnp.complex64)
    _, ys = lax.scan(step, carry0, uc)
    return ys.swapaxes(0, 1).reshape(bsz, L, SSM_GROUPS, SSM_GROUP)


def s5_mixer(u, lam_re, lam_im, log_dt, b_re, b_im, c_re, c_im, d_skip, w_glu):
    bsz, L, _ = u.shape
    ug = u.reshape(bsz, L, SSM_GROUPS, SSM_GROUP)
    y_f = s5_scan(ug, lam_re[0], lam_im[0], log_dt[0], b_re[0], b_im[0], c_re[0], c_im[0])
    y_b = jnp.flip(s5_scan(jnp.flip(ug, 1), lam_re[1], lam_im[1], log_dt[1],
                           b_re[1], b_im[1], c_re[1], c_im[1]), 1)
    y = (y_f + y_b).reshape(bsz, L, MIX_WIDTH) + d_skip.astype(jnp.float32) * u.astype(jnp.float32)
    z = jax.nn.gelu(y).astype(u.dtype)
    a, g = jnp.split(z @ w_glu, 2, axis=-1)
    return a * jax.nn.sigmoid(g)


def rope_tables(L):
    inv_freq = ROPE_THETA ** (-jnp.arange(0, HEAD_DIM, 2, dtype=jnp.float32) / HEAD_DIM)
    ang = jnp.arange(L, dtype=jnp.float32)[:, None] * inv_freq[None, :]
    return jnp.cos(ang), jnp.sin(ang)


def apply_rope(x, cos, sin):
    x1, x2 = jnp.split(x.astype(jnp.float32), 2, axis=-1)
    c = cos[None, :, None, :]
    s = sin[None, :, None, :]
    return jnp.concatenate([x1 * c - x2 * s, x2 * c + x1 * s], axis=-1).astype(x.dtype)


def windowed_gqa(q, k, v, sink):
    bsz, L = q.shape[0], q.shape[1]
    nb = L // BLOCK
    qb = q.reshape(bsz, nb, BLOCK, N_KV_HEADS, GQA_GROUP, HEAD_DIM)

    def band(t):
        tb = t.reshape(bsz, nb, BLOCK, N_KV_HEADS, HEAD_DIM)
        tp = jnp.pad(tb, ((0, 0), (1, 1), (0, 0), (0, 0), (0, 0)))
        return jnp.concatenate([tp[:, :-2], tp[:, 1:-1], tp[:, 2:]], axis=2)

    kb, vb = band(k), band(v)
    s = jnp.einsum('bnqhgd,bnkhd->bnhgqk', qb, kb, preferred_element_type=jnp.float32) * (HEAD_DIM ** -0.5)
    qpos = jnp.arange(L).reshape(nb, BLOCK)
    kpos = (jnp.arange(nb)[:, None] - 1) * BLOCK + jnp.arange(3 * BLOCK)[None, :]
    rel = kpos[:, None, :] - qpos[:, :, None]
    valid = (jnp.abs(rel) <= WINDOW) & (kpos[:, None, :] >= 0) & (kpos[:, None, :] < L)
    s = jnp.where(valid[None, :, None, None], s, -1e30)
    sk = sink.astype(jnp.float32).reshape(N_KV_HEADS, GQA_GROUP)[None, None, :, :, None, None]
    m = jnp.maximum(s.max(-1, keepdims=True), sk)
    p = jnp.exp(s - m)
    p = p / (p.sum(-1, keepdims=True) + jnp.exp(sk - m))
    o = jnp.einsum('bnhgqk,bnkhd->bnqhgd', p.astype(v.dtype), vb)
    return o.reshape(bsz, L, N_Q_HEADS * HEAD_DIM)


def memory_attention(qm, mem, w_mem_kv):
    bsz, L = qm.shape[0], qm.shape[1]
    q = qm.reshape(bsz, L, N_MEM_HEADS, MEM_HEAD_DIM)
    k, v = jnp.split(mem @ w_mem_kv, 2, axis=-1)
    k = k.reshape(bsz, N_MEM, N_MEM_HEADS, MEM_HEAD_DIM)
    v = v.reshape(bsz, N_MEM, N_MEM_HEADS, MEM_HEAD_DIM)
    s = jnp.einsum('bqhd,bkhd->bhqk', q, k, preferred_element_type=jnp.float32) * (MEM_HEAD_DIM ** -0.5)
    p = jax.nn.softmax(s, axis=-1)
    o = jnp.einsum('bhqk,bkhd->bqhd', p.astype(v.dtype), v)
    return o.reshape(bsz, L, MEM_WIDTH)


def trunk(x, mem, ssm_w_in, ssm_lam_re, ssm_lam_im, ssm_log_dt, ssm_b_re, ssm_b_im,
          ssm_c_re, ssm_c_im, ssm_d, ssm_w_glu, attn_w_in, attn_sink, w_mem_kv, w_out,
          ln1_g, ln1_b, w_ff1, w_ff2, ln2_g, ln2_b):
    L = x.shape[1]
    cos, sin = rope_tables(L)
    bsz = x.shape[0]
    for i in range(DEPTH):
        j = i // N_MIXERS
        if i % N_MIXERS == 0:
            proj = x @ ssm_w_in[j]
            u, qm = proj[..., :MIX_WIDTH], proj[..., MIX_WIDTH:]
            y_mix = s5_mixer(u, ssm_lam_re[j], ssm_lam_im[j], ssm_log_dt[j], ssm_b_re[j], ssm_b_im[j],
                             ssm_c_re[j], ssm_c_im[j], ssm_d[j], ssm_w_glu[j])
        else:
            proj = x @ attn_w_in[j]
            q = proj[..., :MIX_WIDTH].reshape(bsz, L, N_Q_HEADS, HEAD_DIM)
            k = proj[..., MIX_WIDTH:MIX_WIDTH + KV_WIDTH].reshape(bsz, L, N_KV_HEADS, HEAD_DIM)
            v = proj[..., MIX_WIDTH + KV_WIDTH:MIX_WIDTH + 2 * KV_WIDTH].reshape(bsz, L, N_KV_HEADS, HEAD_DIM)
            qm = proj[..., MIX_WIDTH + 2 * KV_WIDTH:]
            y_mix = windowed_gqa(apply_rope(q, cos, sin), apply_rope(k, cos, sin), v, attn_sink[j])
        y_mem = memory_attention(qm, mem, w_mem_kv[i])
        mix = jnp.concatenate([y_mix.astype(x.dtype), y_mem.astype(x.dtype)], axis=-1) @ w_out[i]
        x = layer_norm(ALPHA * x + mix, ln1_g[i], ln1_b[i])
        h = jnp.square(jax.nn.relu(x @ w_ff1[i]))
        x = layer_norm(ALPHA * x + h @ w_ff2[i], ln2_g[i], ln2_b[i])
    return x


def setup_inputs(seed: int = 0) -> dict:
    key = jax.random.key(seed)
    ks = iter(jax.random.split(key, 32))
    f32 = jnp.float32

    def nrm(shape, scale):
        return jax.random.normal(next(ks), shape, f32) * scale

    nS, nA, G, P, C = N_SSM_LAYERS, N_ATTN_LAYERS, SSM_GROUPS, SSM_STATE, SSM_GROUP
    inp = {}
    inp['x_prompt'] = nrm((BATCH, SEQ, D_MODEL), 1.0)
    inp['x_sample'] = nrm((DEC_BATCH, DEC_SEQ, D_MODEL), 1.0)
    inp['mem_prompt'] = nrm((BATCH, N_MEM, D_MODEL), 1.0)
    inp['mem_sample'] = nrm((DEC_BATCH, N_MEM, D_MODEL), 1.0)
    inp['ssm_w_in'] = nrm((nS, D_MODEL, MIX_WIDTH + MEM_WIDTH), D_MODEL ** -0.5)
    inp['ssm_lam_re'] = -0.5 + nrm((nS, 2, G, P), 0.01)
    inp['ssm_lam_im'] = jnp.broadcast_to(jnp.pi * jnp.arange(P, dtype=f32), (nS, 2, G, P)) + nrm((nS, 2, G, P), 0.01)
    inp['ssm_log_dt'] = jax.random.uniform(next(ks), (nS, 2, G), f32, math.log(1e-3), math.log(1e-1))
    inp['ssm_b_re'] = nrm((nS, 2, G, P, C), (2 * C) ** -0.5)
    inp['ssm_b_im'] = nrm((nS, 2, G, P, C), (2 * C) ** -0.5)
    inp['ssm_c_re'] = nrm((nS, 2, G, C, P), (2 * P) ** -0.5)
    inp['ssm_c_im'] = nrm((nS, 2, G, C, P), (2 * P) ** -0.5)
    inp['ssm_d'] = nrm((nS, MIX_WIDTH), 1.0)
    inp['ssm_w_glu'] = nrm((nS, MIX_WIDTH, 2 * MIX_WIDTH), MIX_WIDTH ** -0.5)
    inp['attn_w_in'] = nrm((nA, D_MODEL, MIX_WIDTH + 2 * KV_WIDTH + MEM_WIDTH), D_MODEL ** -0.5)
    inp['attn_sink'] = nrm((nA, N_Q_HEADS), 0.5)
    inp['w_mem_kv'] = nrm((DEPTH, D_MODEL, 2 * MEM_WIDTH), D_MODEL ** -0.5)
    inp['w_out'] = nrm((DEPTH, D_MODEL, D_MODEL), BETA * D_MODEL ** -0.5)
    inp['ln1_g'] = 1.0 + nrm((DEPTH, D_MODEL), 0.02)
    inp['ln1_b'] = nrm((DEPTH, D_MODEL), 0.02)
    inp['w_ff1'] = nrm((DEPTH, D_MODEL, D_FF), D_MODEL ** -0.5)
    inp['w_ff2'] = nrm((DEPTH, D_FF, D_MODEL), BETA * D_FF ** -0.5)
    inp['ln2_g'] = 1.0 + nrm((DEPTH, D_MODEL), 0.02)
    inp['ln2_b'] = nrm((DEPTH, D_MODEL), 0.02)
    return inp


def reference(x_prompt, x_sample, mem_prompt, mem_sample, ssm_w_in, ssm_lam_re, ssm_lam_im, ssm_log_dt,
              ssm_b_re, ssm_b_im, ssm_c_re, ssm_c_im, ssm_d, ssm_w_glu, attn_w_in, attn_sink, w_mem_kv,
              w_out, ln1_g, ln1_b, w_ff1, w_ff2, ln2_g, ln2_b):
    y_prompt = trunk(x_prompt, mem_prompt, ssm_w_in, ssm_lam_re, ssm_lam_im, ssm_log_dt, ssm_b_re, ssm_b_im,
                     ssm_c_re, ssm_c_im, ssm_d, ssm_w_glu, attn_w_in, attn_sink, w_mem_kv, w_out,
                     ln1_g, ln1_b, w_ff1, w_ff2, ln2_g, ln2_b)
    y_sample = trunk(x_sample, mem_sample, ssm_w_in, ssm_lam_re, ssm_lam_im, ssm_log_dt, ssm_b_re, ssm_b_im,
                     ssm_c_re, ssm_c_im, ssm_d, ssm_w_glu, attn_w_in, attn_sink, w_mem_kv, w_out,
                     ln1_g, ln1_b, w_ff1, w_ff2, ln2_g, ln2_b)
    return (y_prompt, y_sample)
```

```python
import contextlib
import numpy as np
import concourse.bass as bass
import concourse.mybir as mybir
from concourse.bass_utils import run_bass_kernel_spmd

F32 = mybir.dt.float32
BF16 = mybir.dt.bfloat16
I32 = mybir.dt.int32
AF = mybir.ActivationFunctionType
ALU = mybir.AluOpType

D = 2048
MIX = 1536
HD = 128
DFF = 8192
TC = 32
ALPHA = float(8 ** 0.25)
LN_EPS = 1e-5
EV = list(range(33)) + [64, 128, 256, 512, 1024, 2048]
NE = len(EV)
ENGINES = ("sync", "scalar", "vector", "gpsimd", "tensor")

WSHAPES = {
    "ssm_w_in": [2, 2048, 2048], "ssm_lam_re": [2, 2, 96, 64], "ssm_lam_im": [2, 2, 96, 64],
    "ssm_log_dt": [2, 2, 96], "ssm_b_re": [2, 2, 96, 64, 16], "ssm_b_im": [2, 2, 96, 64, 16],
    "ssm_c_re": [2, 2, 96, 16, 64], "ssm_c_im": [2, 2, 96, 16, 64], "ssm_d": [2, 1536],
    "ssm_w_glu": [2, 1536, 3072], "attn_w_in": [2, 2048, 3072], "attn_sink": [2, 12],
    "w_mem_kv": [4, 2048, 1024], "w_out": [4, 2048, 2048], "ln1_g": [4, 2048], "ln1_b": [4, 2048],
    "w_ff1": [4, 2048, 8192], "w_ff2": [4, 8192, 2048], "ln2_g": [4, 2048], "ln2_b": [4, 2048],
}
BIGW = ["ssm_w_in", "ssm_w_glu", "attn_w_in", "w_mem_kv", "w_out", "w_ff1", "w_ff2"]


class Prog:
    def __init__(self, nc):
        self.nc = nc
        self.ops = []

    def op(self, eng, fn, reads=(), writes=(), dma=None):
        self.ops.append((eng, fn, tuple(reads), tuple(writes), dma))

    def barrier(self):
        self.ops.append(("BAR", None, (), (), None))

    def emit(self):
        ops = self.ops
        n = len(ops)
        last_w, readers = {}, {}
        deps = [None] * n
        last_eng = {}
        dma_since = []
        pending_bar = {}
        for i, (eng, fn, rd, wr, dma) in enumerate(ops):
            if eng == "BAR":
                bd = set(last_eng.values()) | set(dma_since)
                dma_since = []
                pending_bar = {e: bd for e in ENGINES}
                deps[i] = set()
                continue
            d = set()
            for r in rd:
                if r in last_w:
                    d.add(last_w[r])
            for w in wr:
                if w in last_w:
                    d.add(last_w[w])
                for x in readers.get(w, ()):
                    d.add(x)
            if eng in pending_bar:
                d |= pending_bar.pop(eng)
            d.discard(i)
            deps[i] = d
            for w in wr:
                last_w[w] = i
                readers[w] = []
            for r in rd:
                if r not in wr:
                    readers.setdefault(r, []).append(i)
            if dma is None:
                if fn is not None:
                    last_eng[eng] = i
            else:
                dma_since.append(i)
        need_sig = [False] * n
        fdeps = [()] * n
        for i, (eng, fn, rd, wr, dma) in enumerate(ops):
            if eng == "BAR":
                continue
            keep = []
            srd = set(rd)
            for j in deps[i]:
                ej, fj, rdj, wrj, dmaj = ops[j]
                if fj is None:
                    continue
                if dmaj is None and dma is None and ej == eng:
                    if eng == "tensor":
                        continue
                    if not (set(wrj) & srd):
                        continue
                keep.append(j)
            fdeps[i] = keep
            for j in keep:
                need_sig[j] = True
        counts, sig, semkeys = {}, [None] * n, {}
        for i, (eng, fn, rd, wr, dma) in enumerate(ops):
            if not need_sig[i]:
                continue
            key = ("dma", dma) if dma is not None else ("eng", eng)
            inc = 16 if dma is not None else 1
            counts[key] = counts.get(key, 0) + inc
            sig[i] = (key, counts[key], inc)
            semkeys[key] = None
        waited = {e: {} for e in ENGINES}
        streams = {e: [] for e in ENGINES}
        for i, (eng, fn, rd, wr, dma) in enumerate(ops):
            if eng == "BAR":
                continue
            ws = {}
            for j in fdeps[i]:
                key, cnt, _ = sig[j]
                if waited[eng].get(key, 0) >= cnt:
                    continue
                ws[key] = max(ws.get(key, 0), cnt)
            for key, cnt in ws.items():
                waited[eng][key] = cnt
            streams[eng].append((tuple(ws.items()), fn, sig[i]))
        self.semkeys = list(semkeys.keys())
        self.counts = counts
        return streams

    def run_block(self):
        nc = self.nc
        streams = self.emit()
        with contextlib.ExitStack() as st:
            sems = {}
            for n_, k in enumerate(self.semkeys):
                sems[k] = st.enter_context(nc.semaphore("s%d" % n_))
            block = st.enter_context(nc.Block())

            def mk(ename):
                def body(eng):
                    for ws, fn, sg in streams[ename]:
                        for key, cnt in ws:
                            eng.wait_ge(sems[key], cnt)
                        if fn is None:
                            continue
                        ins = fn(eng)
                        if sg is not None:
                            ins.then_inc(sems[sg[0]], sg[2])
                return body

            for ename in ENGINES:
                if streams[ename]:
                    getattr(block, ename)(mk(ename))


class Arena:
    def __init__(self, ap, nbytes):
        self.ap, self.n, self.off = ap, nbytes, 0

    def reset(self, off=0):
        self.off = off

    def alloc(self, shape, dt, parts=128):
        ne = int(np.prod(shape))
        nb = ne * (2 if dt == BF16 else 4)
        a = self.ap[0:parts, self.off // 2:(self.off + nb) // 2]
        self.off += (nb + 63) // 64 * 64
        assert self.off <= self.n, (self.off, self.n)
        if dt != BF16:
            a = a.bitcast(dt)
        if len(shape) == 2:
            return a.rearrange("p (a b) -> p a b", a=shape[0])
        if len(shape) == 3:
            return a.rearrange("p (a b c) -> p a b c", a=shape[0], b=shape[1])
        if len(shape) == 4:
            return a.rearrange("p (a b c d) -> p a b c d", a=shape[0], b=shape[1], c=shape[2])
        if len(shape) == 5:
            return a.rearrange("p (a b c d e) -> p a b c d e", a=shape[0], b=shape[1], c=shape[2], d=shape[3])
        return a


def build(L, NS, DEPTH, dbg=False, stop=None):
    nc = bass.Bass("TRN2", target_bir_lowering=False)
    NT = L // 512
    NB = L // 128
    NCH = L // TC
    NSTEP = int(np.log2(NCH))
    assert 2 ** NSTEP == NCH

    def din(name, shape, dt=F32):
        return nc.dram_tensor(name, list(shape), dt, kind="ExternalInput").ap()

    def scr(name, shape, dt):
        return nc.dram_tensor(name, list(shape), dt, kind="Internal").ap()

    x_in = din("x", [NS, L, D])
    mem_in = din("mem", [NS, 256, D])
    w = {k: din(k, v) for k, v in WSHAPES.items()}
    c_ident = din("c_ident", [128, 128])
    c_cos = din("c_cos", [L, 64])
    c_sin = din("c_sin", [L, 64])
    c_mask = din("c_mask", [128, 384])
    c_msk2 = din("c_msk2", [128, 2])
    c_ev = din("c_ev", [128, NE])
    y_out = nc.dram_tensor("y", [NS, L, D], F32, kind="ExternalOutput").ap()

    wb = {k: scr("wb_" + k, WSHAPES[k], BF16) for k in BIGW}
    xbuf = scr("xbuf", [NS, L, D], F32)
    U_d = scr("U_d", [NS, 16, 96, L], BF16)
    Z_d = scr("Z_d", [NS, 16, 96, L], BF16)
    QM_d = scr("QM_d", [NS, 4, 128, L], BF16)
    QT_d = scr("QT_d", [NS, 12, 128, L], BF16)
    KT_d = scr("KT_d", [NS, 4, 128, L], BF16)
    V_d = scr("V_d", [NS, L, 512], BF16)
    Kd = scr("Kd", [2, 16, 96, 2 * 32 * 96], BF16)
    Qd = scr("Qd", [2, 16, 96, 2 * 32 * 2 * 128], BF16)
    Pd = scr("Pd", [2, 16, 128, 3 * 2 * 32 * 2 * 32], BF16)
    SC_d = scr("SC_d", [2, 2, 128, 2, 7, 48], F32)

    st = contextlib.ExitStack()
    ARENA_BYTES = 206 * 1024
    arena_t = st.enter_context(nc.sbuf_tensor("arena", [128, ARENA_BYTES // 2], BF16))
    A = Arena(arena_t, ARENA_BYTES)
    ps = [st.enter_context(nc.psum_tensor("ps%d" % i, [128, 512], F32)) for i in range(8)]
    P = Prog(nc)
    TWO_PI = float(2 * np.pi)

    def finish():
        P.barrier()
        P.op("sync", None, reads=["out"])
        P.run_block()
        st.close()
        return nc

    def dma(eng, out, in_, reads, writes, key, slow=False):
        if slow:
            P.op(eng, lambda e: e.dma_start(out=out, in_=in_, allow_slow_non_contiguous=True), reads, writes, dma=key)
        else:
            P.op(eng, lambda e: e.dma_start(out=out, in_=in_), reads, writes, dma=key)

    psi = [0]

    def nextps(nb=6, base=0):
        i = base + psi[0] % nb
        psi[0] += 1
        return i

    PERS_OFF = 192 * 1024
    o = PERS_OFF // 2
    ident = arena_t[:, o:o + 256].bitcast(F32); o += 256
    identb = arena_t[:, o:o + 128]; o += 128
    maskt = arena_t[:, o:o + 768].bitcast(F32); o += 768
    msk2 = arena_t[:, o:o + 32].bitcast(F32)[:, 0:2]; o += 32
    kmemT = arena_t[:, o:o + 1024].rearrange("p (h t) -> p h t", h=4); o += 1024
    vmem = arena_t[:, o:o + 1024].rearrange("p (m f) -> p m f", m=2); o += 1024
    sinkt = arena_t[:, o:o + 32].bitcast(F32)[:, 0:12]; o += 32
    cst = arena_t[:, o:o + 1024].bitcast(F32).rearrange("p (a s f) -> p a s f", a=2, s=4); o += 1024
    stats = arena_t[:, o:o + 128].bitcast(F32); o += 128
    smalls = arena_t[:, o:o + 512].bitcast(F32); o += 512
    rtmp0 = arena_t[:, o:o + 512]; o += 512
    rtmp1 = arena_t[:, o:o + 512]; o += 512
    assert o * 2 <= ARENA_BYTES

    dma("sync", ident, c_ident, [], ["ident"], "c0")
    dma("sync", maskt, c_mask, [], ["mask"], "c1")
    dma("sync", msk2, c_msk2, [], ["msk2"], "c2")
    P.op("vector", lambda e: e.tensor_copy(out=identb, in_=ident), ["ident"], ["identb"])

    for name in BIGW:
        shp = WSHAPES[name]
        rows_per = max(1, (1 << 20) // shp[2])
        for l in range(shp[0]):
            for r0 in range(0, shp[1], rows_per):
                r1 = min(shp[1], r0 + rows_per)
                dma("gpsimd", wb[name][l, r0:r1, :], w[name][l, r0:r1, :], [], ["wb"], "cast")
    P.barrier()
    if stop == 'cast':
        return finish()

    def rr_sin(out, arg, tmp_r, tmp_i, tmp_f, shift, rd, wrn):
        P.op("vector", lambda e: e.tensor_scalar(out=tmp_r, in0=arg, scalar1=float(1.0 / TWO_PI), scalar2=float(shift),
                                                 op0=ALU.mult, op1=ALU.add), rd, [wrn + "r"])
        P.op("vector", lambda e: e.tensor_copy(out=tmp_i, in_=tmp_r), [wrn + "r"], [wrn + "i"])
        P.op("vector", lambda e: e.tensor_copy(out=tmp_f, in_=tmp_i), [wrn + "i"], [wrn + "f"])
        P.op("vector", lambda e: e.tensor_tensor(out=tmp_r, in0=tmp_r, in1=tmp_f, op=ALU.subtract),
             [wrn + "r", wrn + "f"], [wrn + "r"])
        P.op("scalar", lambda e: e.activation(out=out, in_=tmp_r, func=AF.Sin, scale=TWO_PI), [wrn + "r"], [wrn])

    n_ssm = (DEPTH + 1) // 2
    for j in range(n_ssm):
        for d in range(2):
            A.reset(0)

            def a2(n, dt=F32):
                t = A.alloc([1, n], dt)
                return t[:, 0, :]
            lamre = a2(48); lamim = a2(48); dtt = a2(48); evt = a2(NE)
            aa = a2(48); th = a2(48)
            BR = A.alloc([48, 16], F32); BI = A.alloc([48, 16], F32)
            CR = A.alloc([48, 16], F32); CI = A.alloc([48, 16], F32)
            cn = A.alloc([1, 128], F32)[:, 0, :]
            LR = A.alloc([NE, 48], F32); LI = A.alloc([NE, 48], F32); MG = A.alloc([NE, 48], F32)
            T1 = A.alloc([NE, 48], F32); T2i = A.alloc([NE, 48], I32); T3 = A.alloc([NE, 48], F32)
            crr = a2(48); cii = a2(48); s1 = a2(48); s2 = a2(48); s3 = a2(48)
            BBR = A.alloc([48, 16], F32); BBI = A.alloc([48, 16], F32)
            dcol = a2(16)
            pre = "pc%d%d" % (j, d)
            flat = lambda ap_: ap_.rearrange("a b -> (a b)")
            dma("sync", lamre, flat(w["ssm_lam_re"][j, d]).rearrange("(g q) -> q g", q=128), [], [pre + "lamre"], "p0", slow=True)
            dma("sync", lamim, flat(w["ssm_lam_im"][j, d]).rearrange("(g q) -> q g", q=128), [], [pre + "lamim"], "p1", slow=True)
            ldt2 = w["ssm_log_dt"][j, d].rearrange("(g t) -> t g", t=2)
            for g2 in range(2):
                dma("sync", dtt[64 * g2:64 * g2 + 64, :], ldt2[g2:g2 + 1, :].to_broadcast([64, 48]), [], [pre + "dt%d" % g2], "p2%d" % g2, slow=True)
            dma("sync", evt, c_ev, [], [pre + "ev"], "p3")
            dma("sync", BR, w["ssm_b_re"][j, d].rearrange("g p c -> (g p c)").rearrange("(g q c) -> q g c", q=128, c=16), [], [pre + "BR"], "p4", slow=True)
            dma("sync", BI, w["ssm_b_im"][j, d].rearrange("g p c -> (g p c)").rearrange("(g q c) -> q g c", q=128, c=16), [], [pre + "BI"], "p5", slow=True)
            dma("sync", dcol[0:96, :], w["ssm_d"][j].rearrange("(f p) -> p f", p=96), [], [pre + "dcol"], "p6", slow=True)
            for ri, (src, dst) in enumerate(((w["ssm_c_re"], CR), (w["ssm_c_im"], CI))):
                cflat = src[j, d].rearrange("g c p -> (g c) p")
                for i in range(12):
                    dma("sync", cn[:, 0:64], cflat[128 * i:128 * i + 128, :], [], [pre + "cn"], "p7")
                    dma("sync", cn[:, 64:128], cflat[128 * i:128 * i + 128, :], [], [pre + "cn"], "p7")
                    b = nextps()
                    P.op("tensor", lambda e, b=b: e.transpose(out=ps[b][:, 0:128], in_=cn, identity=ident), [pre + "cn", "ident"], ["ps%d" % b])
                    tv = ps[b][:, 0:128].rearrange("q (l t c) -> q l t c", l=4, t=2)
                    P.op("vector", lambda e, tv=tv, dst=dst, i=i: e.tensor_copy(out=dst[0:64, 4 * i:4 * i + 4, :], in_=tv[0:64, :, 0, :]), ["ps%d" % b], [pre + "C%d" % ri])
                    P.op("scalar", lambda e, tv=tv, dst=dst, i=i: e.copy(out=dst[64:128, 4 * i:4 * i + 4, :], in_=tv[64:128, :, 1, :]), ["ps%d" % b], [pre + "C%d" % ri])
            P.op("scalar", lambda e: e.activation(out=dtt, in_=dtt, func=AF.Exp), [pre + "dt0", pre + "dt1"], [pre + "dt"])
            P.op("vector", lambda e: e.tensor_tensor(out=aa, in0=dtt, in1=lamre, op=ALU.mult), [pre + "dt", pre + "lamre"], [pre + "aa"])
            P.op("vector", lambda e: e.tensor_tensor(out=th, in0=dtt, in1=lamim, op=ALU.mult), [pre + "dt", pre + "lamim"], [pre + "th"])
            bc_g = lambda t: t.unsqueeze(1).to_broadcast([128, NE, 48])
            bc_e = lambda t: t.unsqueeze(2).to_broadcast([128, NE, 48])
            P.op("vector", lambda e: e.tensor_tensor(out=T1, in0=bc_g(th), in1=bc_e(evt), op=ALU.mult), [pre + "th", pre + "ev"], [pre + "ARG"])
            rr_sin(LI, T1, T3, T2i, MG, 0.0, [pre + "ARG"], pre + "LI")
            rr_sin(LR, T1, T3, T2i, MG, 0.25, [pre + "ARG", pre + "LI"], pre + "LR")
            P.op("vector", lambda e: e.tensor_tensor(out=T1, in0=bc_g(aa), in1=bc_e(evt), op=ALU.mult), [pre + "aa", pre + "ev", pre + "LR", pre + "LI", pre + "ARG"], [pre + "ARG"])
            P.op("scalar", lambda e: e.activation(out=MG, in_=T1, func=AF.Exp), [pre + "ARG", pre + "LRf", pre + "LIf", pre + "LR", pre + "LI"], [pre + "MG"])
            P.op("vector", lambda e: e.tensor_tensor(out=LR, in0=LR, in1=MG, op=ALU.mult), [pre + "LR", pre + "MG"], [pre + "LR"])
            P.op("vector", lambda e: e.tensor_tensor(out=LI, in0=LI, in1=MG, op=ALU.mult), [pre + "LI", pre + "MG"], [pre + "LI"])
            L1R = LR[:, 1, :]; L1I = LI[:, 1, :]
            V = "vector"
            P.op(V, lambda e: e.tensor_scalar(out=s1, in0=L1R, scalar1=-1.0, scalar2=None, op0=ALU.add), [pre + "LR"], [pre + "s1"])
            P.op(V, lambda e: e.tensor_tensor(out=s2, in0=lamre, in1=lamre, op=ALU.mult), [pre + "lamre"], [pre + "s2"])
            P.op(V, lambda e: e.tensor_tensor(out=s3, in0=lamim, in1=lamim, op=ALU.mult), [pre + "lamim"], [pre + "s3"])
            P.op(V, lambda e: e.tensor_tensor(out=s2, in0=s2, in1=s3, op=ALU.add), [pre + "s2", pre + "s3"], [pre + "s2"])
            P.op(V, lambda e: e.reciprocal(out=s2, in_=s2), [pre + "s2"], [pre + "s2"])
            P.op(V, lambda e: e.tensor_tensor(out=crr, in0=s1, in1=lamre, op=ALU.mult), [pre + "s1", pre + "lamre"], [pre + "crr"])
            P.op(V, lambda e: e.tensor_tensor(out=s3, in0=L1I, in1=lamim, op=ALU.mult), [pre + "LI", pre + "lamim", pre + "s2"], [pre + "s3"])
            P.op(V, lambda e: e.tensor_tensor(out=crr, in0=crr, in1=s3, op=ALU.add), [pre + "crr", pre + "s3"], [pre + "crr"])
            P.op(V, lambda e: e.tensor_tensor(out=crr, in0=crr, in1=s2, op=ALU.mult), [pre + "crr", pre + "s2"], [pre + "crr"])
            P.op(V, lambda e: e.tensor_tensor(out=cii, in0=L1I, in1=lamre, op=ALU.mult), [pre + "LI", pre + "lamre"], [pre + "cii"])
            P.op(V, lambda e: e.tensor_tensor(out=s3, in0=s1, in1=lamim, op=ALU.mult), [pre + "s1", pre + "lamim", pre + "crr"], [pre + "s3"])
            P.op(V, lambda e: e.tensor_tensor(out=cii, in0=cii, in1=s3, op=ALU.subtract), [pre + "cii", pre + "s3"], [pre + "cii"])
            P.op(V, lambda e: e.tensor_tensor(out=cii, in0=cii, in1=s2, op=ALU.mult), [pre + "cii", pre + "s2"], [pre + "cii"])
            bc_c = lambda t: t.unsqueeze(2).to_broadcast([128, 48, 16])
            TB = A.alloc([48, 16], F32)
            P.op(V, lambda e: e.tensor_tensor(out=BBR, in0=BR, in1=bc_c(crr), op=ALU.mult), [pre + "BR", pre + "crr"], [pre + "BBR"])
            P.op(V, lambda e: e.tensor_tensor(out=TB, in0=BI, in1=bc_c(cii), op=ALU.mult), [pre + "BI", pre + "cii"], [pre + "TB"])
            P.op(V, lambda e: e.tensor_tensor(out=BBR, in0=BBR, in1=TB, op=ALU.subtract), [pre + "BBR", pre + "TB"], [pre + "BBR"])
            P.op(V, lambda e: e.tensor_tensor(out=BBI, in0=BI, in1=bc_c(crr), op=ALU.mult), [pre + "BI", pre + "crr"], [pre + "BBI"])
            P.op(V, lambda e: e.tensor_tensor(out=TB, in0=BR, in1=bc_c(cii), op=ALU.mult), [pre + "BR", pre + "cii", pre + "BBR"], [pre + "TB"])
            P.op(V, lambda e: e.tensor_tensor(out=BBI, in0=BBI, in1=TB, op=ALU.add), [pre + "BBI", pre + "TB"], [pre + "BBI"])
            base_off = A.off
            for ft in range(16):
                A.reset(base_off)
                g0 = 3 * ft
                XR = A.alloc([32, 3, 16], F32); XI = A.alloc([32, 3, 16], F32); XT = A.alloc([32, 3, 16], F32)
                DQ = [A.alloc([32, 96], F32), A.alloc([32, 96], F32)]
                VR = A.alloc([33, 3, 16], F32); VI = A.alloc([33, 3, 16], F32); VT = A.alloc([33, 3, 16], F32)
                EE = [A.alloc([33, 3, 32], F32), A.alloc([33, 3, 32], F32)]
                Qt = A.alloc([32, 2, 128], BF16)
                Pt = A.alloc([3, 32, 2, 32], BF16)
                Kt = A.alloc([32, 96], BF16)
                K0 = A.alloc([1, 96], F32)[:, 0, :]
                fp = pre + "f"
                lrb = lambda t, ne, g0=g0: t[:, 0:ne, g0:g0 + 3].unsqueeze(3).to_broadcast([128, ne, 3, 16])
                bb = lambda t, ne, g0=g0: t[:, g0:g0 + 3, :].unsqueeze(1).to_broadcast([128, ne, 3, 16])
                P.op(V, lambda e, XR=XR, lrb=lrb, bb=bb: e.tensor_tensor(out=XR, in0=lrb(LR, 32), in1=bb(BBR, 32), op=ALU.mult), [pre + "LR", pre + "BBR"], [fp + "XR"])
                P.op("gpsimd", lambda e, XT=XT, lrb=lrb, bb=bb: e.tensor_tensor(out=XT, in0=lrb(LI, 32), in1=bb(BBI, 32), op=ALU.mult), [pre + "LI", pre + "BBI"], [fp + "XT"])
                P.op(V, lambda e, XR=XR, XT=XT: e.tensor_tensor(out=XR, in0=XR, in1=XT, op=ALU.subtract), [fp + "XR", fp + "XT"], [fp + "XR"])
                P.op(V, lambda e, XI=XI, lrb=lrb, bb=bb: e.tensor_tensor(out=XI, in0=lrb(LR, 32), in1=bb(BBI, 32), op=ALU.mult), [pre + "LR", pre + "BBI"], [fp + "XI"])
                P.op("gpsimd", lambda e, XT=XT, lrb=lrb, bb=bb: e.tensor_tensor(out=XT, in0=lrb(LI, 32), in1=bb(BBR, 32), op=ALU.mult), [pre + "LI", pre + "BBR", fp + "XR"], [fp + "XT"])
                P.op(V, lambda e, XI=XI, XT=XT: e.tensor_tensor(out=XI, in0=XI, in1=XT, op=ALU.add), [fp + "XI", fp + "XT"], [fp + "XI"])
                m2 = lambda n: msk2.unsqueeze(1).unsqueeze(3).to_broadcast([128, n, 2, 16])
                for c, Xc in enumerate((XR, XI)):
                    xin = Xc.rearrange("q e g c -> q (e g) c").unsqueeze(2).to_broadcast([128, 96, 2, 16])
                    dqo = DQ[c].rearrange("q e (g t c) -> q (e g) t c", g=3, t=2)
                    P.op(V if c == 0 else "gpsimd", lambda e, dqo=dqo, xin=xin, m2=m2: e.tensor_tensor(out=dqo, in0=xin, in1=m2(96), op=ALU.mult),
                         [fp + ("XR" if c == 0 else "XI"), "msk2"], [fp + "DQ%d" % c])
                for e2 in range(0, 32, 2):
                    b = nextps()
                    for ee in range(2):
                        for c in range(2):
                            sl = (ee * 2 + c) * 128
                            P.op("tensor", lambda e, b=b, sl=sl, c=c, e_=e2 + ee, DQ=DQ: e.transpose(out=ps[b][0:96, sl:sl + 128], in_=DQ[c][:, e_, :], identity=ident),
                                 [fp + "DQ%d" % c, "ident"], ["ps%d" % b])
                    pv = ps[b][0:96, :].rearrange("q (a c n) -> q a c n", a=2, c=2)
                    P.op(V if (e2 // 2) % 2 == 0 else "scalar",
                         (lambda e, pv=pv, e2=e2, Qt=Qt: e.tensor_copy(out=Qt[0:96, e2:e2 + 2, :, :], in_=pv)) if (e2 // 2) % 2 == 0 else
                         (lambda e, pv=pv, e2=e2, Qt=Qt: e.copy(out=Qt[0:96, e2:e2 + 2, :, :], in_=pv)),
                         ["ps%d" % b], [fp + "Qt"])
                dma("gpsimd", Qd[j, ft, :, d * 8192:(d + 1) * 8192], Qt[0:96].rearrange("q e c n -> q (e c n)"), [fp + "Qt"], ["Qd"], "pq")
                cb = lambda t, ne, g0=g0: t[:, g0:g0 + 3, :].unsqueeze(1).to_broadcast([128, ne, 3, 16])
                P.op(V, lambda e, VR=VR, lrb=lrb, cb=cb: e.tensor_tensor(out=VR, in0=lrb(LR, 33), in1=cb(CR, 33), op=ALU.mult), [pre + "LR", pre + "C0"], [fp + "VR"])
                P.op("gpsimd", lambda e, VT=VT, lrb=lrb, cb=cb: e.tensor_tensor(out=VT, in0=lrb(LI, 33), in1=cb(CI, 33), op=ALU.mult), [pre + "LI", pre + "C1"], [fp + "VT"])
                P.op(V, lambda e, VR=VR, VT=VT: e.tensor_tensor(out=VR, in0=VR, in1=VT, op=ALU.subtract), [fp + "VR", fp + "VT"], [fp + "VR"])
                P.op(V, lambda e, VI=VI, lrb=lrb, cb=cb: e.tensor_tensor(out=VI, in0=lrb(LR, 33), in1=cb(CI, 33), op=ALU.mult), [pre + "LR", pre + "C1"], [fp + "VI"])
                P.op("gpsimd", lambda e, VT=VT, lrb=lrb, cb=cb: e.tensor_tensor(out=VT, in0=lrb(LI, 33), in1=cb(CR, 33), op=ALU.mult), [pre + "LI", pre + "C0", fp + "VR"], [fp + "VT"])
                P.op(V, lambda e, VI=VI, VT=VT: e.scalar_tensor_tensor(out=VI, in0=VI, scalar=-1.0, in1=VT, op0=ALU.mult, op1=ALU.subtract), [fp + "VI", fp + "VT"], [fp + "VI"])
                for c, Vc in enumerate((VR, VI)):
                    vin = Vc.rearrange("q e g c -> q (e g) c").unsqueeze(2).to_broadcast([128, 99, 2, 16])
                    eo = EE[c].rearrange("q e g (t c) -> q (e g) t c", t=2)
                    P.op(V if c == 0 else "gpsimd", lambda e, eo=eo, vin=vin, m2=m2: e.tensor_tensor(out=eo, in0=vin, in1=m2(99), op=ALU.mult),
                         [fp + ("VR" if c == 0 else "VI"), "msk2"], [fp + "EE%d" % c])
                    for gl in range(3):
                        P.op("scalar", lambda e, gl=gl, c=c, Pt=Pt, EE=EE: e.copy(out=Pt[:, gl, :, c, :], in_=EE[c][:, 1:33, gl, :]), [fp + "EE%d" % c], [fp + "Pt"])
                dma("gpsimd", Pd[j, ft].rearrange("q (g d r) -> q g d r", g=3, d=2)[:, :, d, :], Pt.rearrange("q g e c n -> q g (e c n)"), [fp + "Pt"], ["Pd"], "pp")
                for bk in range(8):
                    b = nextps()
                    P.op(V, lambda e, b=b: e.memset(ps[b][0:96, 0:384], 0.0), [], ["ps%d" % b])
                    pk = ps[b][0:96, 0:384].rearrange("q (l n) -> q l n", l=4)
                    for gl in range(3):
                        for c in range(2):
                            P.op("tensor", lambda e, gl=gl, c=c, bk=bk, pk=pk, DQ=DQ, EE=EE: e.matmul(
                                pk[32 * gl:32 * gl + 32, :, 32 * gl:32 * gl + 32], lhsT=DQ[c][:, 0, 32 * gl:32 * gl + 32],
                                rhs=EE[c][:, 4 * bk:4 * bk + 4, gl, :], start=False, stop=(gl == 2 and c == 1), skip_group_check=True),
                                [fp + "DQ%d" % c, fp + "EE%d" % c, "ps%d" % b], ["ps%d" % b])
                    P.op(V, lambda e, pk=pk, bk=bk, Kt=Kt: e.tensor_copy(out=Kt[0:96, 4 * bk:4 * bk + 4, :], in_=pk), ["ps%d" % b], [fp + "Kt"])
                    if bk == 0 and d == 0:
                        P.op(V, lambda e, pk=pk, K0=K0, ft=ft: e.scalar_tensor_tensor(out=K0[0:96, :], in0=ident[0:96, 0:96], scalar=dcol[0:96, ft:ft + 1], in1=pk[:, 0, :],
                                                                                op0=ALU.mult, op1=ALU.add), ["ps%d" % b, "ident", pre + "dcol"], [fp + "K0"])
                        P.op(V, lambda e, K0=K0, Kt=Kt: e.tensor_copy(out=Kt[0:96, 0, :], in_=K0[0:96, :]), [fp + "K0", fp + "Kt"], [fp + "Kt"])
                dma("gpsimd", Kd[j, ft, :, d * 3072:(d + 1) * 3072], Kt[0:96].rearrange("q l n -> q (l n)"), [fp + "Kt"], ["Kd"], "pk")
            dma("gpsimd", SC_d[j, d, :, 0, :, :], LR[:, 32:32 + 7, :], [pre + "LR"], ["SC_d"], "psc0")
            dma("gpsimd", SC_d[j, d, :, 1, :, :], LI[:, 32:32 + 7, :], [pre + "LI"], ["SC_d"], "psc1")
            P.barrier()
    if stop == 'pre':
        return finish()
    RING = 4
    ring_i = [0]

    def wget(src3, rows, nkt, ncols):
        slot = ring_i[0] % RING
        ring_i[0] += 1
        dst = arena_t[0:rows, slot * 8192: slot * 8192 + nkt * ncols].rearrange("p (k n) -> p k n", k=nkt)
        dma("sync", dst, src3, ["wb"], ["ring%d" % slot], "ring%d" % slot)
        return dst, "ring%d" % slot

    def wchunk(name, l, r0, nrows_p, nkt, c0, ncols):
        v = wb[name][l, r0:r0 + nrows_p * nkt, c0:c0 + ncols].rearrange("(k p) n -> p k n", p=nrows_p)
        return wget(v, nrows_p, nkt, ncols)

    H_OFF = 64 * 1024
    XR_OFF = 128 * 1024
    XT_OFF = 160 * 1024
    LNP_OFF = 176 * 1024
    xr = arena_t[:, XR_OFF // 2:(XR_OFF + 32768) // 2].bitcast(F32).rearrange("p (s d) -> p s d", s=4)
    xT = arena_t[:, XT_OFF // 2:(XT_OFF + 16384) // 2].rearrange("p (k t) -> p k t", k=16)
    lnp = arena_t[:, LNP_OFF // 2:(LNP_OFF + 16384) // 2].bitcast(F32).rearrange("p (a d) -> p a d", a=2)
    hT = arena_t[:, H_OFF // 2:(H_OFF + 65536) // 2].rearrange("p (k t) -> p k t", k=64)
    HA = Arena(arena_t, 192 * 1024)
    evc = [0]

    def evac_copy(out, in_, rd, wr):
        evc[0] += 1
        if evc[0] % 2:
            P.op("vector", lambda e: e.tensor_copy(out=out, in_=in_), rd, wr)
        else:
            P.op("scalar", lambda e: e.copy(out=out, in_=in_), rd, wr)

    def transposes_to_xT(xb, rd):
        for kp in range(8):
            b = 6 + kp % 2
            pb = ps[b][:].bitcast(BF16)
            for kk in range(2):
                kt = 2 * kp + kk
                for sub in range(4):
                    P.op("tensor", lambda e, pb=pb, kk=kk, sub=sub, kt=kt: e.transpose(out=pb[:, kk * 512 + sub * 128: kk * 512 + sub * 128 + 128],
                                                                                  in_=xb[:, sub, kt * 128:(kt + 1) * 128], identity=identb), rd + ["identb"], ["ps%d" % b])
            evac_copy(xT[:, 2 * kp:2 * kp + 2, :], pb.rearrange("p (k t) -> p k t", k=2), ["ps%d" % b], ["xT"])

    def layer_norm(sub, gname, bname, li):
        xs = xr[:, sub, :]
        so = 16 * sub
        for c in range(4):
            P.op("vector", lambda e, c=c: e.bn_stats(out=stats[:, so * 0 + 6 * c: 6 * c + 6], in_=xs[:, 512 * c:512 * c + 512]), ["xr%d" % sub], ["bst"])
        P.op("vector", lambda e: e.bn_aggr(out=stats[:, 24:26], in_=stats[:, 0:24]), ["bst"], ["bag"])
        P.op("scalar", lambda e: e.activation(out=stats[:, 26:27], in_=stats[:, 25:26], func=AF.Sqrt, bias=epst, scale=1.0), ["bag", "eps"], ["bsd"])
        P.op("vector", lambda e: e.reciprocal(out=stats[:, 27:28], in_=stats[:, 26:27]), ["bsd"], ["brs"])
        P.op("vector", lambda e: e.scalar_tensor_tensor(out=stats[:, 28:29], in0=stats[:, 24:25], scalar=-1.0, in1=stats[:, 27:28], op0=ALU.mult, op1=ALU.mult), ["bag", "brs"], ["bnb"])
        P.op("scalar", lambda e: e.activation(out=xs, in_=xs, func=AF.Identity, bias=stats[:, 28:29], scale=stats[:, 27:28]), ["xr%d" % sub, "brs", "bnb"], ["xr%d" % sub])
        P.op("gpsimd", lambda e: e.tensor_tensor(out=xs, in0=xs, in1=lnp[:, 0, :], op=ALU.mult), ["xr%d" % sub, "lnp"], ["xr%d" % sub])
        P.op("gpsimd", lambda e: e.tensor_tensor(out=xs, in0=xs, in1=lnp[:, 1, :], op=ALU.add), ["xr%d" % sub, "lnp"], ["xr%d" % sub])

    epst = smalls[:, 0:1]
    P.op("vector", lambda e: e.memset(epst, LN_EPS), [], ["eps"])

    for s in range(NS):
        for li in range(DEPTH):
            j = li // 2
            is_ssm = (li % 2 == 0)
            src = x_in if li == 0 else xbuf
            dst = y_out if li == DEPTH - 1 else xbuf
            lp = "s%dl%d" % (s, li)
            P.barrier()
            HA.reset(H_OFF)
            memb = HA.alloc([2, 2048], BF16)
            memT = HA.alloc([16, 256], BF16)
            dma("sync", xr[:, 0:2, :], mem_in[s].rearrange("(m p) d -> p m d", p=128), [], ["xr0", "xr1"], "xr")
            P.op("vector", lambda e: e.tensor_copy(out=memb[:, 0, :], in_=xr[:, 0, :]), ["xr0"], ["memb"])
            P.op("scalar", lambda e: e.copy(out=memb[:, 1, :], in_=xr[:, 1, :]), ["xr1"], ["memb"])
            for kp in range(4):
                b = 6 + kp % 2
                pb = ps[b][:].bitcast(BF16)
                for kk in range(4):
                    kt = 4 * kp + kk
                    for m in range(2):
                        P.op("tensor", lambda e, pb=pb, kk=kk, m=m, kt=kt: e.transpose(out=pb[:, kk * 256 + m * 128: kk * 256 + m * 128 + 128], in_=memb[:, m, kt * 128:(kt + 1) * 128], identity=identb),
                             ["memb", "identb"], ["ps%d" % b])
                evac_copy(memT[:, 4 * kp:4 * kp + 4, :], pb.rearrange("p (k t) -> p k t", k=4), ["ps%d" % b], ["memT"])
            wc, wr_ = wchunk("w_mem_kv", li, 0, 128, 16, 0, 512)
            for h in range(4):
                b = nextps()
                for kt in range(16):
                    P.op("tensor", lambda e, b=b, h=h, kt=kt, wc=wc: e.matmul(ps[b][:, 0:256], lhsT=wc[:, kt, h * 128:(h + 1) * 128], rhs=memT[:, kt, :], start=(kt == 0), stop=(kt == 15)),
                         ["memT", wr_], ["ps%d" % b])
                evac_copy(kmemT[:, h, :], ps[b][:, 0:256], ["ps%d" % b], ["kmemT"])
            wc, wr_ = wchunk("w_mem_kv", li, 0, 128, 16, 512, 512)
            for m in range(2):
                b = nextps()
                for kt in range(16):
                    P.op("tensor", lambda e, b=b, m=m, kt=kt, wc=wc: e.matmul(ps[b][:], lhsT=memT[:, kt, m * 128:(m + 1) * 128], rhs=wc[:, kt, :], start=(kt == 0), stop=(kt == 15)),
                         ["memT", wr_], ["ps%d" % b])
                evac_copy(vmem[:, m, :], ps[b][:], ["ps%d" % b], ["vmem"])
            if not is_ssm:
                dma("sync", sinkt, w["attn_sink"][j].partition_broadcast(128), [], ["sink"], "sink")
            if stop == 'mem':
                return finish()
            P.barrier()
            HA.reset(H_OFF)
            xb = HA.alloc([4, 2048], BF16)
            if is_ssm:
                ust = HA.alloc([16, 512], BF16)
                qst = HA.alloc([4, 512], BF16)
            else:
                qTst = HA.alloc([16, 512], BF16)
                qst = HA.alloc([4, 512], BF16)
                vst = HA.alloc([4, 512], BF16)
                rk = HA.alloc([4, 4, 128], BF16)
                rt = [HA.alloc([4, 64], F32) for _ in range(4)]
            for t in range(NT):
                tk = slice(t * 512, (t + 1) * 512)
                dma("sync", xr, src[s, tk, :].rearrange("(u p) d -> p u d", p=128), [], ["xr0", "xr1", "xr2", "xr3"], "xr")
                for sub in range(4):
                    eng = ("vector", "scalar", "gpsimd", "vector")[sub]
                    if eng == "scalar":
                        P.op(eng, lambda e, sub=sub, xb=xb: e.copy(out=xb[:, sub, :], in_=xr[:, sub, :]), ["xr%d" % sub], ["xb%d" % sub])
                    else:
                        P.op(eng, lambda e, sub=sub, xb=xb: e.tensor_copy(out=xb[:, sub, :], in_=xr[:, sub, :]), ["xr%d" % sub], ["xb%d" % sub])
                transposes_to_xT(xb, ["xb0", "xb1", "xb2", "xb3"])
                if is_ssm:
                    for c in range(4):
                        wc, wr_ = wchunk("ssm_w_in", j, 0, 128, 16, 384 * c, 384)
                        for fi in range(4):
                            b = nextps()
                            for kt in range(16):
                                P.op("tensor", lambda e, b=b, fi=fi, kt=kt, wc=wc: e.matmul(ps[b][0:96, :], lhsT=wc[:, kt, fi * 96:(fi + 1) * 96], rhs=xT[:, kt, :], start=(kt == 0), stop=(kt == 15)),
                                     ["xT", wr_], ["ps%d" % b])
                            evac_copy(ust[0:96, 4 * c + fi, :], ps[b][0:96, :], ["ps%d" % b], ["ust"])
                    dma("gpsimd", U_d[s, :, :, tk].rearrange("f p t -> p f t"), ust[0:96], ["ust"], ["U_d"], "ust")
                    wc, wr_ = wchunk("ssm_w_in", j, 0, 128, 16, 1536, 512)
                else:
                    dma("sync", cst[:, 0], c_cos[tk, :].rearrange("(u p) f -> p u f", p=128), [], ["cst"], "cst")
                    dma("sync", cst[:, 1], c_sin[tk, :].rearrange("(u p) f -> p u f", p=128), [], ["cst"], "cst")
                    for c in range(4):
                        wc, wr_ = wchunk("attn_w_in", j, 0, 128, 16, 512 * c, 512)
                        for sub in range(4):
                            b = nextps()
                            for kt in range(16):
                                P.op("tensor", lambda e, b=b, sub=sub, kt=kt, wc=wc: e.matmul(ps[b][:], lhsT=xT[:, kt, sub * 128:(sub + 1) * 128], rhs=wc[:, kt, :], start=(kt == 0), stop=(kt == 15)),
                                     ["xT", wr_], ["ps%d" % b])
                            pv = ps[b][:].rearrange("p (h two f) -> p h two f", h=4, two=2)
                            cosb = cst[:, 0, sub, :].unsqueeze(1).to_broadcast([128, 4, 64])
                            sinb = cst[:, 1, sub, :].unsqueeze(1).to_broadcast([128, 4, 64])
                            rn = "rt%d" % sub
                            P.op("vector", lambda e, pv=pv, cosb=cosb: e.tensor_tensor(out=rt[0], in0=pv[:, :, 0, :], in1=cosb, op=ALU.mult), ["ps%d" % b, "cst"], ["rt0"])
                            P.op("vector", lambda e, pv=pv, sinb=sinb: e.tensor_tensor(out=rt[1], in0=pv[:, :, 1, :], in1=sinb, op=ALU.mult), ["ps%d" % b, "cst"], ["rt1"])
                            P.op("vector", lambda e, pv=pv, cosb=cosb: e.tensor_tensor(out=rt[2], in0=pv[:, :, 1, :], in1=cosb, op=ALU.mult), ["ps%d" % b, "cst"], ["rt2"])
                            P.op("vector", lambda e, pv=pv, sinb=sinb: e.tensor_tensor(out=rt[3], in0=pv[:, :, 0, :], in1=sinb, op=ALU.mult), ["ps%d" % b, "cst"], ["rt3"])
                            P.op("gpsimd", lambda e, sub=sub: e.tensor_tensor(out=rk[:, sub, :, 0:64], in0=rt[0], in1=rt[1], op=ALU.subtract), ["rt0", "rt1"], ["rk%d" % sub])
                            P.op("gpsimd", lambda e, sub=sub: e.tensor_tensor(out=rk[:, sub, :, 64:128], in0=rt[2], in1=rt[3], op=ALU.add), ["rt2", "rt3"], ["rk%d" % sub])
                        for hp in range(2):
                            b = 6 + hp % 2
                            pb = ps[b][:].bitcast(BF16)
                            for hh in range(2):
                                for sub in range(4):
                                    P.op("tensor", lambda e, pb=pb, hh=hh, sub=sub, hp=hp: e.transpose(out=pb[:, hh * 512 + sub * 128: hh * 512 + sub * 128 + 128], in_=rk[:, sub, 2 * hp + hh, :], identity=identb),
                                         ["rk%d" % sub, "identb"], ["ps%d" % b])
                            evac_copy(qTst[:, 4 * c + 2 * hp: 4 * c + 2 * hp + 2, :], pb.rearrange("p (k t) -> p k t", k=2), ["ps%d" % b], ["qTst"])
                    dma("gpsimd", QT_d[s, :, :, tk].rearrange("f p t -> p f t"), qTst[:, 0:12, :], ["qTst"], ["QT_d"], "qTst")
                    dma("gpsimd", KT_d[s, :, :, tk].rearrange("f p t -> p f t"), qTst[:, 12:16, :], ["qTst"], ["KT_d"], "kTst")
                    wc, wr_ = wchunk("attn_w_in", j, 0, 128, 16, 2048, 512)
                    for sub in range(4):
                        b = nextps()
                        for kt in range(16):
                            P.op("tensor", lambda e, b=b, sub=sub, kt=kt, wc=wc: e.matmul(ps[b][:], lhsT=xT[:, kt, sub * 128:(sub + 1) * 128], rhs=wc[:, kt, :], start=(kt == 0), stop=(kt == 15)),
                                 ["xT", wr_], ["ps%d" % b])
                        evac_copy(vst[:, sub, :], ps[b][:], ["ps%d" % b], ["vst"])
                    dma("gpsimd", V_d[s, tk, :].rearrange("(u p) f -> p u f", p=128), vst, ["vst"], ["V_d"], "vst")
                    wc, wr_ = wchunk("attn_w_in", j, 0, 128, 16, 2560, 512)
                for m in range(4):
                    b = nextps()
                    for kt in range(16):
                        P.op("tensor", lambda e, b=b, m=m, kt=kt, wc=wc: e.matmul(ps[b][:], lhsT=wc[:, kt, m * 128:(m + 1) * 128], rhs=xT[:, kt, :], start=(kt == 0), stop=(kt == 15)),
                             ["xT", wr_], ["ps%d" % b])
                    evac_copy(qst[:, m, :], ps[b][:], ["ps%d" % b], ["qst"])
                dma("gpsimd", QM_d[s, :, :, tk].rearrange("f p t -> p f t"), qst, ["qst"], ["QM_d"], "qst")
            if stop == 'A':
                return finish()
            if is_ssm:
                P.barrier()
                HA.reset(0)
                unat = HA.alloc([1, L], BF16)[:, 0, :]
                uperm = HA.alloc([TC, NCH], BF16)
                znat = HA.alloc([1, L], BF16)[:, 0, :]
                Qt = HA.alloc([2, 32, 2, 128], BF16)
                Pt = HA.alloc([3, 2, 32, 2, 32], BF16)
                Kt = HA.alloc([2, 32, 96], BF16)
                SC = HA.alloc([2, 2, 7, 48], F32)
                X = [HA.alloc([3, 2, 2, NCH], F32) for _ in range(2)]
                TM = [HA.alloc([3, 2, 2, NCH], F32) for _ in range(2)]
                Hb = HA.alloc([3, 2, 2, NCH], BF16)
                g1 = HA.alloc([1, 2048], F32)[:, 0, :]
                g2t = HA.alloc([1, 2048], F32)[:, 0, :]
                dma("sync", SC[:, 0], SC_d[j, 0], [], ["SC"], "SC")
                dma("sync", SC[:, 1], SC_d[j, 1], [], ["SC"], "SC")
                NBK = L // 512
                TPB = 512 // NCH
                for ft in range(16):
                    g0 = 3 * ft
                    dma("sync", unat[0:96, :], U_d[s, ft], ["U_d"], ["unat"], "unat")
                    dma("sync", Qt[0:96].rearrange("q a b c d -> q (a b c d)"), Qd[j, ft], ["Qd"], ["Qt"], "Qt")
                    dma("sync", Pt.rearrange("q a b c d e -> q (a b c d e)"), Pd[j, ft], ["Pd"], ["Pt"], "Pt")
                    dma("sync", Kt[0:96].rearrange("q a b c -> q (a b c)"), Kd[j, ft], ["Kd"], ["Kt"], "Kt")
                    P.op("vector", lambda e: e.tensor_copy(out=uperm[0:96], in_=unat[0:96, :].rearrange("p (k t) -> p t k", t=TC)), ["unat"], ["uperm"])
                    if stop == 'B0':
                        return finish()
                    for gl in range(3):
                        for d in range(2):
                            for c in range(2):
                                b = gl
                                col = (d * 2 + c) * NCH
                                for sg in range(TC):
                                    e_ = (31 - sg) if d == 0 else sg
                                    P.op("tensor", lambda e, b=b, col=col, gl=gl, d=d, c=c, sg=sg, e_=e_: e.matmul(
                                        ps[b][:, col:col + NCH], lhsT=Qt[32 * gl:32 * gl + 32, d, e_, c, :], rhs=uperm[32 * gl:32 * gl + 32, sg, :],
                                        start=(sg == 0), stop=(sg == TC - 1), skip_group_check=True), ["Qt", "uperm"], ["ps%d" % b])
                    for gl in range(3):
                        xo = X[0][:, gl].rearrange("q d c k -> q (d c) k")
                        evac_copy(xo, ps[gl][:, 0:4 * NCH].rearrange("q (a k) -> q a k", k=NCH), ["ps%d" % gl], ["X0"])
                    if stop == 'B1':
                        return finish()
                    cur = 0
                    for s_ in range(NSTEP):
                        sh = 2 ** s_
                        n_ = NCH - sh
                        Xi, Xo = X[cur], X[1 - cur]
                        ci, co_ = "X%d" % cur, "X%d" % (1 - cur)
                        ar = lambda d, c, s_=s_, g0=g0, n_=n_: SC[:, d, c, s_, g0:g0 + 3].unsqueeze(2).to_broadcast([128, 3, n_])
                        for d in range(2):
                            if d == 0:
                                dsl, ssl, ksl = slice(sh, NCH), slice(0, n_), slice(0, sh)
                            else:
                                dsl, ssl, ksl = slice(0, n_), slice(sh, NCH), slice(n_, NCH)
                            eng = "vector" if d == 0 else "gpsimd"
                            t0_, t1_ = TM[0][:, :, d, 0, 0:n_], TM[1][:, :, d, 0, 0:n_]
                            tn = "TM%d" % d
                            xre, xim = Xi[:, :, d, 0, ssl], Xi[:, :, d, 1, ssl]
                            P.op(eng, lambda e, t0_=t0_, xre=xre, d=d, ar=ar: e.tensor_tensor(out=t0_, in0=xre, in1=ar(d, 0), op=ALU.mult), [ci, "SC"], [tn + "a"])
                            P.op(eng, lambda e, t1_=t1_, xim=xim, d=d, ar=ar: e.tensor_tensor(out=t1_, in0=xim, in1=ar(d, 1), op=ALU.mult), [ci, "SC"], [tn + "b"])
                            P.op(eng, lambda e, t0_=t0_, t1_=t1_: e.tensor_tensor(out=t0_, in0=t0_, in1=t1_, op=ALU.subtract), [tn + "a", tn + "b"], [tn + "a"])
                            P.op(eng, lambda e, Xo=Xo, Xi=Xi, d=d, dsl=dsl, t0_=t0_: e.tensor_tensor(out=Xo[:, :, d, 0, dsl], in0=Xi[:, :, d, 0, dsl], in1=t0_, op=ALU.add), [ci, tn + "a"], [co_])
                            P.op(eng, lambda e, t0_=t0_, xim=xim, d=d, ar=ar: e.tensor_tensor(out=t0_, in0=xim, in1=ar(d, 0), op=ALU.mult), [ci, "SC", co_], [tn + "a"])
                            P.op(eng, lambda e, t1_=t1_, xre=xre, d=d, ar=ar: e.tensor_tensor(out=t1_, in0=xre, in1=ar(d, 1), op=ALU.mult), [ci, "SC", tn + "a"], [tn + "b"])
                            P.op(eng, lambda e, t0_=t0_, t1_=t1_: e.tensor_tensor(out=t0_, in0=t0_, in1=t1_, op=ALU.add), [tn + "a", tn + "b"], [tn + "a"])
                            P.op(eng, lambda e, Xo=Xo, Xi=Xi, d=d, dsl=dsl, t0_=t0_: e.tensor_tensor(out=Xo[:, :, d, 1, dsl], in0=Xi[:, :, d, 1, dsl], in1=t0_, op=ALU.add), [ci, tn + "a"], [co_])
                            P.op("scalar", lambda e, Xo=Xo, Xi=Xi, d=d, ksl=ksl: e.copy(out=Xo[:, :, d, :, ksl], in_=Xi[:, :, d, :, ksl]), [ci], [co_])
                        cur = 1 - cur
                    P.op("gpsimd", lambda e: e.memset(Hb, 0.0), [], ["Hb"])
                    P.op("vector", lambda e, cur=cur: e.tensor_copy(out=Hb[:, :, 0, :, 1:NCH], in_=X[cur][:, :, 0, :, 0:NCH - 1]), ["X%d" % cur, "Hb"], ["Hb"])
                    P.op("vector", lambda e, cur=cur: e.tensor_copy(out=Hb[:, :, 1, :, 0:NCH - 1], in_=X[cur][:, :, 1, :, 1:NCH]), ["X%d" % cur, "Hb"], ["Hb"])
                    if stop == 'B2':
                        return finish()
                    for b in range(NBK):
                        t_lo = b * TPB
                        first = True
                        yb = ps[b][0:96, :].rearrange("q (t k) -> q t k", k=NCH)
                        for d in range(2):
                            for l in range(min(TC, t_lo + TPB)):
                                if d == 0:
                                    ta = max(l, t_lo); tb_ = t_lo + TPB
                                    if ta >= tb_:
                                        continue
                                    out_ = yb[:, ta - t_lo:tb_ - t_lo, :]
                                    rhs_ = uperm[0:96, ta - l:tb_ - l, :]
                                else:
                                    continue
                                P.op("tensor", lambda e, out_=out_, rhs_=rhs_, d=d, l=l, first=first: e.matmul(out_, lhsT=Kt[0:96, d, l, :], rhs=rhs_, start=first, stop=False, skip_group_check=True),
                                     ["Kt", "uperm"], ["ps%d" % b])
                                first = False
                        for l in range(TC):
                            ta = t_lo; tb_ = min(t_lo + TPB, TC - l)
                            if ta >= tb_:
                                continue
                            out_ = yb[:, ta - t_lo:tb_ - t_lo, :]
                            rhs_ = uperm[0:96, ta + l:tb_ + l, :]
                            P.op("tensor", lambda e, out_=out_, rhs_=rhs_, l=l: e.matmul(out_, lhsT=Kt[0:96, 1, l, :], rhs=rhs_, start=False, stop=False, skip_group_check=True),
                                 ["Kt", "uperm"], ["ps%d" % b])
                        if stop == 'B3':
                            return finish()
                        for tau in range(t_lo, t_lo + TPB):
                            for gl in range(3):
                                for d in range(2):
                                    e_ = tau if d == 0 else 31 - tau
                                    for c in range(2):
                                        out_ = yb[32 * gl:32 * gl + 32, tau - t_lo, :]
                                        rhs_ = Hb[:, gl, d, c, :]
                                        last = (tau == t_lo + TPB - 1 and gl == 2 and d == 1 and c == 1)
                                        P.op("tensor", lambda e, out_=out_, rhs_=rhs_, gl=gl, d=d, e_=e_, c=c, last=last: e.matmul(out_, lhsT=Pt[:, gl, d, e_, c, :], rhs=rhs_, start=False, stop=last, skip_group_check=True),
                                             ["Pt", "Hb"], ["ps%d" % b])
                        if stop == 'B4':
                            return finish()
                        ysb = g1[0:96, 0:512]
                        tt = g2t[0:96, 0:512]
                        P.op("vector", lambda e, b=b, ysb=ysb: e.tensor_copy(out=ysb, in_=ps[b][0:96, :]), ["ps%d" % b], ["g1"])
                        P.op("gpsimd", lambda e, ysb=ysb, tt=tt: e.tensor_tensor(out=tt, in0=ysb, in1=ysb, op=ALU.mult), ["g1"], ["g2"])
                        P.op("gpsimd", lambda e, tt=tt: e.tensor_scalar(out=tt, in0=tt, scalar1=0.044715, scalar2=1.0, op0=ALU.mult, op1=ALU.add), ["g2"], ["g2"])
                        P.op("gpsimd", lambda e, ysb=ysb, tt=tt: e.tensor_tensor(out=tt, in0=tt, in1=ysb, op=ALU.mult), ["g1", "g2"], ["g2"])
                        P.op("scalar", lambda e, tt=tt: e.activation(out=tt, in_=tt, func=AF.Sigmoid, scale=float(2.0 * np.sqrt(2.0 / np.pi))), ["g2"], ["g2"])
                        zv = znat[0:96, :].rearrange("p (k t) -> p t k", t=TC)[:, t_lo:t_lo + TPB, :]
                        P.op("vector", lambda e, zv=zv, ysb=ysb, tt=tt: e.tensor_tensor(out=zv, in0=ysb.rearrange("p (t k) -> p t k", k=NCH), in1=tt.rearrange("p (t k) -> p t k", k=NCH), op=ALU.mult), ["g1", "g2"], ["znat"])
                    dma("gpsimd", Z_d[s, ft], znat[0:96, :], ["znat"], ["Z_d"], "znat")
            if stop == 'B':
                return finish()
            P.barrier()
            for t in range(NT):
                tk = slice(t * 512, (t + 1) * 512)
                HA.reset(H_OFF)
                ymixT = HA.alloc([12, 512], BF16)
                ymemT = HA.alloc([4, 512], BF16)
                qmT = HA.alloc([4, 512], BF16)
                smm = [HA.alloc([1, 256], F32)[:, 0, :] for _ in range(4)]
                pm = [HA.alloc([1, 256], BF16)[:, 0, :] for _ in range(4)]
                pmT = [HA.alloc([2, 128], BF16) for _ in range(4)]
                dma("sync", xr, src[s, tk, :].rearrange("(u p) d -> p u d", p=128), [], ["xr0", "xr1", "xr2", "xr3"], "xr")
                dma("sync", qmT, QM_d[s, :, :, tk].rearrange("f p t -> p f t"), ["QM_d"], ["qmT"], "qmT")
                dma("sync", lnp[:, 0, :], w["ln1_g"][li].partition_broadcast(128), [], ["lnp"], "lnp")
                dma("sync", lnp[:, 1, :], w["ln1_b"][li].partition_broadcast(128), [], ["lnp"], "lnp")
                if is_ssm:
                    zT = HA.alloc([16, 512], BF16)
                    sg_ = HA.alloc([1, 512], F32)[:, 0, :]
                    dma("sync", zT[0:96], Z_d[s, :, :, tk].rearrange("f p t -> p f t"), ["Z_d"], ["zT"], "zT")
                    for c in range(3):
                        wa, wra = wchunk("ssm_w_glu", j, 0, 96, 16, 512 * c, 512)
                        wg, wrg = wchunk("ssm_w_glu", j, 0, 96, 16, 1536 + 512 * c, 512)
                        for mi in range(4):
                            ba = nextps(); bg = nextps()
                            for kt in range(16):
                                P.op("tensor", lambda e, ba=ba, mi=mi, kt=kt, wa=wa: e.matmul(ps[ba][:], lhsT=wa[0:96, kt, mi * 128:(mi + 1) * 128], rhs=zT[0:96, kt, :], start=(kt == 0), stop=(kt == 15)),
                                     ["zT", wra], ["ps%d" % ba])
                            for kt in range(16):
                                P.op("tensor", lambda e, bg=bg, mi=mi, kt=kt, wg=wg: e.matmul(ps[bg][:], lhsT=wg[0:96, kt, mi * 128:(mi + 1) * 128], rhs=zT[0:96, kt, :], start=(kt == 0), stop=(kt == 15)),
                                     ["zT", wrg], ["ps%d" % bg])
                            P.op("scalar", lambda e, bg=bg: e.activation(out=sg_, in_=ps[bg][:], func=AF.Sigmoid), ["ps%d" % bg], ["sg"])
                            P.op("vector", lambda e, ba=ba, c=c, mi=mi: e.tensor_tensor(out=ymixT[:, 4 * c + mi, :], in0=ps[ba][:], in1=sg_, op=ALU.mult), ["ps%d" % ba, "sg"], ["ymixT"])
                else:
                    qT = HA.alloc([12, 512], BF16)
                    kT = HA.alloc([4, 768], BF16)
                    vv = HA.alloc([6, 512], BF16)
                    k0 = max(0, t * 512 - 128); k1 = min(L, t * 512 + 640)
                    ko = 128 - (t * 512 - k0)
                    dma("sync", qT, QT_d[s, :, :, tk].rearrange("f p t -> p f t"), ["QT_d"], ["qT"], "qT")
                    dma("sync", kT[:, :, ko:ko + (k1 - k0)], KT_d[s, :, :, k0:k1].rearrange("f p t -> p f t"), ["KT_d"], ["kT"], "kT")
                    dma("sync", vv[:, ko // 128: ko // 128 + (k1 - k0) // 128, :], V_d[s, k0:k1, :].rearrange("(u p) f -> p u f", p=128), ["V_d"], ["vv"], "vv")
                    sm = [HA.alloc([1, 384], F32)[:, 0, :] for _ in range(3)]
                    pp = [HA.alloc([1, 384], BF16)[:, 0, :] for _ in range(3)]
                    pT = [HA.alloc([3, 128], BF16) for _ in range(3)]
                    for kvh in range(4):
                        ybk = [nextps(3, 3) for _ in range(3)]
                        for qb in range(4):
                            gq = t * 4 + qb
                            kb0 = 0 if gq > 0 else 1
                            kb1 = 3 if gq < NB - 1 else 2
                            nk = (kb1 - kb0) * 128
                            kcol = qb * 128 + kb0 * 128
                            for hh in range(3):
                                h = kvh * 3 + hh
                                b = hh
                                so = 32 + 8 * hh
                                P.op("tensor", lambda e, b=b, h=h, qb=qb, kcol=kcol, nk=nk, kvh=kvh: e.matmul(ps[b][:, 0:nk], lhsT=qT[:, h, qb * 128:(qb + 1) * 128], rhs=kT[:, kvh, kcol:kcol + nk], start=True, stop=True),
                                     ["qT", "kT"], ["ps%d" % b])
                                P.op("vector", lambda e, b=b, hh=hh, nk=nk, kb0=kb0: e.tensor_tensor(out=sm[hh][:, 0:nk], in0=ps[b][:, 0:nk], in1=maskt[:, kb0 * 128:kb0 * 128 + nk], op=ALU.add), ["ps%d" % b, "mask"], ["sm%d" % hh])
                                P.op("vector", lambda e, hh=hh, nk=nk, so=so: e.reduce_max(out=stats[:, so:so + 1], in_=sm[hh][:, 0:nk], axis=mybir.AxisListType.X), ["sm%d" % hh], ["st%da" % hh])
                                P.op("vector", lambda e, so=so, h=h: e.scalar_tensor_tensor(out=stats[:, so + 1:so + 2], in0=stats[:, so:so + 1], scalar=float(HD ** -0.5), in1=sinkt[:, h:h + 1], op0=ALU.mult, op1=ALU.max), ["st%da" % hh, "sink"], ["st%db" % hh])
                                P.op("vector", lambda e, so=so: e.tensor_scalar(out=stats[:, so + 2:so + 3], in0=stats[:, so + 1:so + 2], scalar1=-1.0, scalar2=None, op0=ALU.mult), ["st%db" % hh], ["st%dc" % hh])
                                P.op("scalar", lambda e, hh=hh, nk=nk, so=so: e.activation(out=pp[hh][:, 0:nk], in_=sm[hh][:, 0:nk], func=AF.Exp, bias=stats[:, so + 2:so + 3], scale=float(HD ** -0.5), accum_out=stats[:, so + 3:so + 4]),
                                     ["sm%d" % hh, "st%dc" % hh], ["pp%d" % hh, "st%dd" % hh])
                                P.op("scalar", lambda e, so=so, h=h: e.activation(out=stats[:, so + 4:so + 5], in_=stats[:, so + 2:so + 3], func=AF.Exp, bias=sinkt[:, h:h + 1], scale=1.0), ["st%dc" % hh, "sink"], ["st%de" % hh])
                                P.op("vector", lambda e, so=so: e.tensor_tensor(out=stats[:, so + 5:so + 6], in0=stats[:, so + 3:so + 4], in1=stats[:, so + 4:so + 5], op=ALU.add), ["st%dd" % hh, "st%de" % hh], ["st%df" % hh])
                                P.op("vector", lambda e, so=so: e.reciprocal(out=stats[:, so + 6:so + 7], in_=stats[:, so + 5:so + 6]), ["st%df" % hh], ["st%dg" % hh])
                                P.op("vector", lambda e, hh=hh, nk=nk, so=so: e.tensor_scalar(out=pp[hh][:, 0:nk], in0=pp[hh][:, 0:nk], scalar1=stats[:, so + 6:so + 7], scalar2=None, op0=ALU.mult), ["pp%d" % hh, "st%dg" % hh], ["pp%d" % hh])
                                bt = 6 + hh % 2
                                pb = ps[bt][:].bitcast(BF16)
                                for kb in range(kb1 - kb0):
                                    P.op("tensor", lambda e, pb=pb, hh=hh, kb=kb: e.transpose(out=pb[:, (hh // 2) * 512 + kb * 128:(hh // 2) * 512 + kb * 128 + 128], in_=pp[hh][:, kb * 128:(kb + 1) * 128], identity=identb),
                                         ["pp%d" % hh, "identb"], ["ps%d" % bt])
                                nkb = kb1 - kb0
                                evac_copy(pT[hh][:, 0:nkb, :], pb[:, (hh // 2) * 512:(hh // 2) * 512 + nkb * 128].rearrange("p (k t) -> p k t", k=nkb), ["ps%d" % bt], ["pT%d" % hh])
                                yb_ = ybk[hh]
                                for kb in range(nkb):
                                    vb = qb + kb0 + kb
                                    P.op("tensor", lambda e, yb_=yb_, qb=qb, kb=kb, vb=vb, hh=hh, kvh=kvh, nkb=nkb: e.matmul(ps[yb_][:, qb * 128:(qb + 1) * 128], lhsT=vv[:, vb, kvh * 128:(kvh + 1) * 128], rhs=pT[hh][:, kb, :],
                                                                                                                        start=(kb == 0), stop=(kb == nkb - 1), skip_group_check=True), ["vv", "pT%d" % hh], ["ps%d" % yb_])
                        for hh in range(3):
                            evac_copy(ymixT[:, kvh * 3 + hh, :], ps[ybk[hh]][:], ["ps%d" % ybk[hh]], ["ymixT"])
                for qb in range(4):
                    for h in range(4):
                        b = h % 2
                        so = 32 + 8 * h
                        hn = "m%d" % h
                        P.op("tensor", lambda e, b=b, h=h, qb=qb: e.matmul(ps[b][:, 0:256], lhsT=qmT[:, h, qb * 128:(qb + 1) * 128], rhs=kmemT[:, h, :], start=True, stop=True), ["qmT", "kmemT"], ["ps%d" % b])
                        P.op("vector", lambda e, b=b, h=h: e.tensor_copy(out=smm[h], in_=ps[b][:, 0:256]), ["ps%d" % b], [hn + "s"])
                        P.op("vector", lambda e, h=h, so=so: e.reduce_max(out=stats[:, so:so + 1], in_=smm[h], axis=mybir.AxisListType.X), [hn + "s"], [hn + "a"])
                        P.op("vector", lambda e, so=so: e.tensor_scalar(out=stats[:, so + 2:so + 3], in0=stats[:, so:so + 1], scalar1=float(-(HD ** -0.5)), scalar2=None, op0=ALU.mult), [hn + "a"], [hn + "c"])
                        P.op("scalar", lambda e, h=h, so=so: e.activation(out=pm[h], in_=smm[h], func=AF.Exp, bias=stats[:, so + 2:so + 3], scale=float(HD ** -0.5), accum_out=stats[:, so + 3:so + 4]), [hn + "s", hn + "c"], [hn + "p", hn + "d"])
                        P.op("vector", lambda e, so=so: e.reciprocal(out=stats[:, so + 6:so + 7], in_=stats[:, so + 3:so + 4]), [hn + "d"], [hn + "g"])
                        P.op("vector", lambda e, h=h, so=so: e.tensor_scalar(out=pm[h], in0=pm[h], scalar1=stats[:, so + 6:so + 7], scalar2=None, op0=ALU.mult), [hn + "p", hn + "g"], [hn + "p"])
                        bt = 6 + h % 2
                        pb = ps[bt][:].bitcast(BF16)
                        for kb in range(2):
                            P.op("tensor", lambda e, pb=pb, h=h, kb=kb: e.transpose(out=pb[:, (h // 2) * 256 + kb * 128:(h // 2) * 256 + kb * 128 + 128], in_=pm[h][:, kb * 128:(kb + 1) * 128], identity=identb), [hn + "p", "identb"], ["ps%d" % bt])
                        evac_copy(pmT[h], pb[:, (h // 2) * 256:(h // 2) * 256 + 256].rearrange("p (k t) -> p k t", k=2), ["ps%d" % bt], [hn + "T"])
                        for kb in range(2):
                            P.op("tensor", lambda e, h=h, kb=kb, qb=qb: e.matmul(ps[2 + h][:, qb * 128:(qb + 1) * 128], lhsT=vmem[:, kb, h * 128:(h + 1) * 128], rhs=pmT[h][:, kb, :],
                                                                              start=(kb == 0), stop=(kb == 1), skip_group_check=True), [hn + "T", "vmem"], ["ps%d" % (2 + h)])
                for h in range(4):
                    evac_copy(ymemT[:, h, :], ps[2 + h][:], ["ps%d" % (2 + h)], ["ymemT"])
                for fc in range(4):
                    wc, wr_ = wchunk("w_out", li, 0, 128, 16, 512 * fc, 512)
                    for sub in range(4):
                        b = nextps(3, 0)
                        for kt in range(16):
                            lhs = ymixT[:, kt, sub * 128:(sub + 1) * 128] if kt < 12 else ymemT[:, kt - 12, sub * 128:(sub + 1) * 128]
                            P.op("tensor", lambda e, b=b, lhs=lhs, kt=kt, wc=wc: e.matmul(ps[b][:], lhsT=lhs, rhs=wc[:, kt, :], start=(kt == 0), stop=(kt == 15)), ["ymixT", "ymemT", wr_], ["ps%d" % b])
                        P.op("vector", lambda e, b=b, sub=sub, fc=fc: e.scalar_tensor_tensor(out=xr[:, sub, 512 * fc:512 * fc + 512], in0=xr[:, sub, 512 * fc:512 * fc + 512], scalar=ALPHA, in1=ps[b][:], op0=ALU.mult, op1=ALU.add),
                             ["ps%d" % b, "xr%d" % sub], ["xr%d" % sub])
                for sub in range(4):
                    layer_norm(sub, "ln1_g", "ln1_b", li)
                HA.reset(H_OFF + 49152)
                xb2 = HA.alloc([4, 2048], BF16)
                P.barrier()
                for sub in range(4):
                    eng = ("vector", "scalar", "gpsimd", "vector")[sub]
                    if eng == "scalar":
                        P.op(eng, lambda e, sub=sub, xb2=xb2: e.copy(out=xb2[:, sub, :], in_=xr[:, sub, :]), ["xr%d" % sub], ["xb%d" % sub])
                    else:
                        P.op(eng, lambda e, sub=sub, xb2=xb2: e.tensor_copy(out=xb2[:, sub, :], in_=xr[:, sub, :]), ["xr%d" % sub], ["xb%d" % sub])
                transposes_to_xT(xb2, ["xb0", "xb1", "xb2", "xb3"])
                dma("sync", lnp[:, 0, :], w["ln2_g"][li].partition_broadcast(128), [], ["lnp"], "lnp")
                dma("sync", lnp[:, 1, :], w["ln2_b"][li].partition_broadcast(128), [], ["lnp"], "lnp")
                P.barrier()
                rtmp = [rtmp0, rtmp1]
                for c in range(16):
                    wc, wr_ = wchunk("w_ff1", li, 0, 128, 16, 512 * c, 512)
                    for fi in range(4):
                        f = 4 * c + fi
                        b = 4 + f % 2
                        for kt in range(16):
                            P.op("tensor", lambda e, b=b, fi=fi, kt=kt, wc=wc: e.matmul(ps[b][:], lhsT=wc[:, kt, fi * 128:(fi + 1) * 128], rhs=xT[:, kt, :], start=(kt == 0), stop=(kt == 15)), ["xT", wr_], ["ps%d" % b])
                        r_ = rtmp[f % 2]
                        P.op("scalar", lambda e, b=b, r_=r_: e.activation(out=r_, in_=ps[b][:], func=AF.Relu), ["ps%d" % b], ["rt%d" % (f % 2)])
                        P.op("gpsimd", lambda e, f=f, r_=r_: e.tensor_tensor(out=hT[:, f, :], in0=r_, in1=r_, op=ALU.mult), ["rt%d" % (f % 2)], ["hT%d" % (f // 16)])
                for fc in range(4):
                    for c4 in range(4):
                        v = wb["w_ff2"][li, c4 * 2048:(c4 + 1) * 2048, 512 * fc:512 * fc + 512].rearrange("(k p) n -> p k n", p=128)
                        wc, wr_ = wget(v, 128, 16, 512)
                        for sub in range(4):
                            for kt in range(16):
                                P.op("tensor", lambda e, sub=sub, kt=kt, c4=c4, wc=wc: e.matmul(ps[sub][:], lhsT=hT[:, c4 * 16 + kt, sub * 128:(sub + 1) * 128], rhs=wc[:, kt, :],
                                                                                          start=(c4 == 0 and kt == 0), stop=(c4 == 3 and kt == 15), skip_group_check=True),
                                     ["hT%d" % c4, wr_], ["ps%d" % sub])
                    for sub in range(4):
                        P.op("vector", lambda e, sub=sub, fc=fc: e.scalar_tensor_tensor(out=xr[:, sub, 512 * fc:512 * fc + 512], in0=xr[:, sub, 512 * fc:512 * fc + 512], scalar=ALPHA, in1=ps[sub][:], op0=ALU.mult, op1=ALU.add),
                             ["ps%d" % sub, "xr%d" % sub], ["xr%d" % sub])
                for sub in range(4):
                    layer_norm(sub, "ln2_g", "ln2_b", li)
                dma("gpsimd", dst[s, tk, :].rearrange("(u p) d -> p u d", p=128), xr, ["xr0", "xr1", "xr2", "xr3"], ["out"], "xout")
                P.barrier()
    P.barrier()
    P.op("sync", None, reads=["out"])
    P.run_block()
    st.close()
    return nc


def _consts(L):
    inv = (10000.0 ** (-np.arange(0, 128, 2, dtype=np.float32) / np.float32(128))).astype(np.float32)
    ang = (np.arange(L, dtype=np.float32)[:, None] * inv[None, :]).astype(np.float32)
    i = np.arange(128)[:, None]
    jj = np.arange(384)[None, :]
    mask = np.where(np.abs(jj - 128 - i) <= 128, 0.0, -1e30).astype(np.float32)
    msk2 = np.zeros((128, 2), np.float32)
    msk2[:64, 0] = 1
    msk2[64:, 1] = 1
    return {"c_ident": np.eye(128, dtype=np.float32), "c_cos": np.cos(ang).astype(np.float32), "c_sin": np.sin(ang).astype(np.float32),
            "c_mask": mask, "c_msk2": msk2, "c_ev": np.tile(np.array(EV, np.float32)[None, :], (128, 1))}


def kernel(**inputs):
    L, NS, DEPTH = 4096, 2, 4
    xp = np.asarray(inputs["x_prompt"], np.float32)
    xs = np.asarray(inputs["x_sample"], np.float32)
    mp = np.asarray(inputs["mem_prompt"], np.float32)
    ms = np.asarray(inputs["mem_sample"], np.float32)
    nc = build(L, NS, DEPTH)
    cs = _consts(L)
    wts = {k: np.ascontiguousarray(np.asarray(inputs[k], np.float32)) for k in WSHAPES}
    in_maps = []
    for c in range(8):
        x1 = xs[c] if c < 4 else xp[c]
        m1 = ms[c] if c < 4 else mp[c]
        m = {"x": np.ascontiguousarray(np.stack([xp[c], x1])), "mem": np.ascontiguousarray(np.stack([mp[c], m1]))}
        m.update(wts)
        m.update(cs)
        in_maps.append(m)
    res = run_bass_kernel_spmd(nc, in_maps, core_ids=list(range(8)))
    yp = np.stack([np.asarray(res.results[c]["y"][0], np.float32) for c in range(8)])
    ysm = np.stack([np.asarray(res.results[c]["y"][1], np.float32) for c in range(4)])
    return (yp, ysm)
```

```python
import contextlib
import numpy as np
import concourse.bass as bass
import concourse.mybir as mybir
from concourse.bass_utils import run_bass_kernel_spmd

F32 = mybir.dt.float32
BF16 = mybir.dt.bfloat16
I32 = mybir.dt.int32
AF = mybir.ActivationFunctionType
ALU = mybir.AluOpType

D = 2048
MIX = 1536
HD = 128
DFF = 8192
TC = 32
ALPHA = float(8 ** 0.25)
LN_EPS = 1e-5
EV = list(range(33)) + [64, 128, 256, 512, 1024, 2048]
NE = len(EV)
ENGINES = ("sync", "scalar", "vector", "gpsimd", "tensor")

WSHAPES = {
    "ssm_w_in": [2, 2048, 2048], "ssm_lam_re": [2, 2, 96, 64], "ssm_lam_im": [2, 2, 96, 64],
    "ssm_log_dt": [2, 2, 96], "ssm_b_re": [2, 2, 96, 64, 16], "ssm_b_im": [2, 2, 96, 64, 16],
    "ssm_c_re": [2, 2, 96, 16, 64], "ssm_c_im": [2, 2, 96, 16, 64], "ssm_d": [2, 1536],
    "ssm_w_glu": [2, 1536, 3072], "attn_w_in": [2, 2048, 3072], "attn_sink": [2, 12],
    "w_mem_kv": [4, 2048, 1024], "w_out": [4, 2048, 2048], "ln1_g": [4, 2048], "ln1_b": [4, 2048],
    "w_ff1": [4, 2048, 8192], "w_ff2": [4, 8192, 2048], "ln2_g": [4, 2048], "ln2_b": [4, 2048],
}
BIGW = ["ssm_w_in", "ssm_w_glu", "attn_w_in", "w_mem_kv", "w_out", "w_ff1", "w_ff2"]


class Prog:
    def __init__(self, nc):
        self.nc = nc
        self.ops = []

    def op(self, eng, fn, reads=(), writes=(), dma=None):
        self.ops.append((eng, fn, tuple(reads), tuple(writes), dma))

    def barrier(self, skip=()):
        self.ops.append(("BAR", None, tuple(skip), (), None))

    def emit(self):
        ops = self.ops
        n = len(ops)
        last_w, readers = {}, {}
        deps = [None] * n
        last_eng = {}
        dma_since = []
        pending_bar = {}
        for i, (eng, fn, rd, wr, dma) in enumerate(ops):
            if eng == "BAR":
                keepd = [x for x in dma_since if ops[x][4] in rd]
                bd = set(last_eng.values()) | set(x for x in dma_since if ops[x][4] not in rd)
                dma_since = keepd
                pending_bar = {e: bd for e in ENGINES}
                deps[i] = set()
                continue
            d = set()
            for r in rd:
                if r in last_w:
                    d.add(last_w[r])
            for w in wr:
                if w in last_w:
                    d.add(last_w[w])
                for x in readers.get(w, ()):
                    d.add(x)
            if eng in pending_bar:
                d |= pending_bar.pop(eng)
            d.discard(i)
            deps[i] = d
            for w in wr:
                last_w[w] = i
                readers[w] = []
            for r in rd:
                if r not in wr:
                    readers.setdefault(r, []).append(i)
            if dma is None:
                if fn is not None:
                    last_eng[eng] = i
            else:
                dma_since.append(i)
        need_sig = [False] * n
        fdeps = [()] * n
        for i, (eng, fn, rd, wr, dma) in enumerate(ops):
            if eng == "BAR":
                continue
            keep = []
            srd = set(rd)
            for j in deps[i]:
                ej, fj, rdj, wrj, dmaj = ops[j]
                if fj is None:
                    continue
                if dmaj is None and dma is None and ej == eng:
                    if eng == "tensor":
                        continue
                    if not (set(wrj) & srd):
                        continue
                keep.append(j)
            fdeps[i] = keep
            for j in keep:
                need_sig[j] = True
        counts, sig, semkeys = {}, [None] * n, {}
        for i, (eng, fn, rd, wr, dma) in enumerate(ops):
            if not need_sig[i]:
                continue
            key = ("dma", dma) if dma is not None else ("eng", eng)
            inc = 16 if dma is not None else 1
            counts[key] = counts.get(key, 0) + inc
            sig[i] = (key, counts[key], inc)
            semkeys[key] = None
        waited = {e: {} for e in ENGINES}
        streams = {e: [] for e in ENGINES}
        for i, (eng, fn, rd, wr, dma) in enumerate(ops):
            if eng == "BAR":
                continue
            ws = {}
            for j in fdeps[i]:
                key, cnt, _ = sig[j]
                if waited[eng].get(key, 0) >= cnt:
                    continue
                ws[key] = max(ws.get(key, 0), cnt)
            for key, cnt in ws.items():
                waited[eng][key] = cnt
            streams[eng].append((tuple(ws.items()), fn, sig[i]))
        self.semkeys = list(semkeys.keys())
        self.counts = counts
        return streams

    def run_block(self):
        nc = self.nc
        streams = self.emit()
        with contextlib.ExitStack() as st:
            sems = {}
            for n_, k in enumerate(self.semkeys):
                sems[k] = st.enter_context(nc.semaphore("s%d" % n_))
            block = st.enter_context(nc.Block())

            def mk(ename):
                def body(eng):
                    for ws, fn, sg in streams[ename]:
                        for key, cnt in ws:
                            eng.wait_ge(sems[key], cnt)
                        if fn is None:
                            continue
                        ins = fn(eng)
                        if sg is not None:
                            ins.then_inc(sems[sg[0]], sg[2])
                return body

            for ename in ENGINES:
                if streams[ename]:
                    getattr(block, ename)(mk(ename))


class Arena:
    def __init__(self, ap, nbytes):
        self.ap, self.n, self.off = ap, nbytes, 0

    def reset(self, off=0):
        self.off = off

    def alloc(self, shape, dt, parts=128):
        ne = int(np.prod(shape))
        nb = ne * (2 if dt == BF16 else 4)
        a = self.ap[0:parts, self.off // 2:(self.off + nb) // 2]
        self.off += (nb + 63) // 64 * 64
        assert self.off <= self.n, (self.off, self.n)
        if dt != BF16:
            a = a.bitcast(dt)
        if len(shape) == 2:
            return a.rearrange("p (a b) -> p a b", a=shape[0])
        if len(shape) == 3:
            return a.rearrange("p (a b c) -> p a b c", a=shape[0], b=shape[1])
        if len(shape) == 4:
            return a.rearrange("p (a b c d) -> p a b c d", a=shape[0], b=shape[1], c=shape[2])
        if len(shape) == 5:
            return a.rearrange("p (a b c d e) -> p a b c d e", a=shape[0], b=shape[1], c=shape[2], d=shape[3])
        return a


def build(L, NS, DEPTH, dbg=False, stop=None):
    nc = bass.Bass("TRN2", target_bir_lowering=False)
    NT = L // 512
    NB = L // 128
    NCH = L // TC
    NSTEP = int(np.log2(NCH))
    assert 2 ** NSTEP == NCH

    def din(name, shape, dt=F32):
        return nc.dram_tensor(name, list(shape), dt, kind="ExternalInput").ap()

    def scr(name, shape, dt):
        return nc.dram_tensor(name, list(shape), dt, kind="Internal").ap()

    x_in = din("x", [NS, L, D])
    mem_in = din("mem", [NS, 256, D])
    w = {k: din(k, v) for k, v in WSHAPES.items()}
    c_ident = din("c_ident", [128, 128])
    c_cos = din("c_cos", [L, 64])
    c_sin = din("c_sin", [L, 64])
    c_mask = din("c_mask", [128, 384])
    c_msk2 = din("c_msk2", [128, 2])
    c_ev = din("c_ev", [128, NE])
    y_out = nc.dram_tensor("y", [NS, L, D], F32, kind="ExternalOutput").ap()

    wb = {k: scr("wb_" + k, WSHAPES[k], BF16) for k in BIGW}
    xbuf = scr("xbuf", [NS, L, D], F32)
    U_d = scr("U_d", [NS, 16, 96, L], BF16)
    Z_d = scr("Z_d", [NS, 16, 96, L], BF16)
    QM_d = scr("QM_d", [NS, 4, 128, L], BF16)
    QT_d = scr("QT_d", [NS, 12, 128, L], BF16)
    KT_d = scr("KT_d", [NS, 4, 128, L], BF16)
    V_d = scr("V_d", [NS, L, 512], BF16)
    Kd = scr("Kd", [2, 16, 96, 2 * 32 * 96], BF16)
    Qd = scr("Qd", [2, 16, 96, 2 * 32 * 2 * 128], BF16)
    Pd = scr("Pd", [2, 16, 128, 3 * 2 * 32 * 2 * 32], BF16)
    SC_d = scr("SC_d", [2, 2, 128, 2, 7, 48], F32)

    st = contextlib.ExitStack()
    ARENA_BYTES = 206 * 1024
    arena_t = st.enter_context(nc.sbuf_tensor("arena", [128, ARENA_BYTES // 2], BF16))
    A = Arena(arena_t, ARENA_BYTES)
    ps = [st.enter_context(nc.psum_tensor("ps%d" % i, [128, 512], F32)) for i in range(8)]
    P = Prog(nc)
    TWO_PI = float(2 * np.pi)

    def finish():
        P.barrier()
        P.op("sync", None, reads=["out"])
        P.run_block()
        st.close()
        return nc

    def dma(eng, out, in_, reads, writes, key, slow=False):
        if slow:
            P.op(eng, lambda e: e.dma_start(out=out, in_=in_, allow_slow_non_contiguous=True), reads, writes, dma=key)
        else:
            P.op(eng, lambda e: e.dma_start(out=out, in_=in_), reads, writes, dma=key)

    psi = [0]

    def nextps(nb=6, base=0):
        i = base + psi[0] % nb
        psi[0] += 1
        return i

    PERS_OFF = 192 * 1024
    o = PERS_OFF // 2
    ident = arena_t[:, o:o + 256].bitcast(F32); o += 256
    identb = arena_t[:, o:o + 128]; o += 128
    maskt = arena_t[:, o:o + 768].bitcast(F32); o += 768
    msk2 = arena_t[:, o:o + 32].bitcast(F32)[:, 0:2]; o += 32
    kmemT = arena_t[:, o:o + 1024].rearrange("p (h t) -> p h t", h=4); o += 1024
    vmem = arena_t[:, o:o + 1024].rearrange("p (m f) -> p m f", m=2); o += 1024
    sinkt = arena_t[:, o:o + 32].bitcast(F32)[:, 0:12]; o += 32
    cst = arena_t[:, o:o + 1024].bitcast(F32).rearrange("p (a s f) -> p a s f", a=2, s=4); o += 1024
    stats = arena_t[:, o:o + 128].bitcast(F32); o += 128
    smalls = arena_t[:, o:o + 512].bitcast(F32); o += 512
    rtmp0 = arena_t[:, o:o + 512]; o += 512
    rtmp1 = arena_t[:, o:o + 512]; o += 512
    assert o * 2 <= ARENA_BYTES

    dma("sync", ident, c_ident, [], ["ident"], "c0")
    dma("sync", maskt, c_mask, [], ["mask"], "c1")
    dma("sync", msk2, c_msk2, [], ["msk2"], "c2")
    P.op("vector", lambda e: e.tensor_copy(out=identb, in_=ident), ["ident"], ["identb"])

    for name in BIGW:
        shp = WSHAPES[name]
        rows_per = max(1, (1 << 20) // shp[2])
        for l in range(shp[0]):
            for r0 in range(0, shp[1], rows_per):
                r1 = min(shp[1], r0 + rows_per)
                dma("gpsimd", wb[name][l, r0:r1, :], w[name][l, r0:r1, :], [], ["wb"], "cast")
    if stop == 'cast':
        return finish()

    def rr_sin(out, arg, tmp_r, tmp_i, tmp_f, shift, rd, wrn):
        P.op("vector", lambda e: e.tensor_scalar(out=tmp_r, in0=arg, scalar1=float(1.0 / TWO_PI), scalar2=float(shift),
                                                 op0=ALU.mult, op1=ALU.add), rd, [wrn + "r"])
        P.op("vector", lambda e: e.tensor_copy(out=tmp_i, in_=tmp_r), [wrn + "r"], [wrn + "i"])
        P.op("vector", lambda e: e.tensor_copy(out=tmp_f, in_=tmp_i), [wrn + "i"], [wrn + "f"])
        P.op("vector", lambda e: e.tensor_tensor(out=tmp_r, in0=tmp_r, in1=tmp_f, op=ALU.subtract),
             [wrn + "r", wrn + "f"], [wrn + "r"])
        P.op("scalar", lambda e: e.activation(out=out, in_=tmp_r, func=AF.Sin, scale=TWO_PI), [wrn + "r"], [wrn])

    n_ssm = (DEPTH + 1) // 2
    for j in range(n_ssm):
        for d in range(2):
            A.reset(0)

            def a2(n, dt=F32):
                t = A.alloc([1, n], dt)
                return t[:, 0, :]
            lamre = a2(48); lamim = a2(48); dtt = a2(48); evt = a2(NE)
            aa = a2(48); th = a2(48)
            BR = A.alloc([48, 16], F32); BI = A.alloc([48, 16], F32)
            CR = A.alloc([48, 16], F32); CI = A.alloc([48, 16], F32)
            cn = A.alloc([1, 128], F32)[:, 0, :]
            LR = A.alloc([NE, 48], F32); LI = A.alloc([NE, 48], F32); MG = A.alloc([NE, 48], F32)
            T1 = A.alloc([NE, 48], F32); T2i = A.alloc([NE, 48], I32); T3 = A.alloc([NE, 48], F32)
            crr = a2(48); cii = a2(48); s1 = a2(48); s2 = a2(48); s3 = a2(48)
            BBR = A.alloc([48, 16], F32); BBI = A.alloc([48, 16], F32)
            dcol = a2(16)
            pre = "pc%d%d" % (j, d)
            flat = lambda ap_: ap_.rearrange("a b -> (a b)")
            dma("sync", lamre, flat(w["ssm_lam_re"][j, d]).rearrange("(g q) -> q g", q=128), [], [pre + "lamre"], "p0", slow=True)
            dma("sync", lamim, flat(w["ssm_lam_im"][j, d]).rearrange("(g q) -> q g", q=128), [], [pre + "lamim"], "p1", slow=True)
            ldt2 = w["ssm_log_dt"][j, d].rearrange("(g t) -> t g", t=2)
            for g2 in range(2):
                dma("sync", dtt[64 * g2:64 * g2 + 64, :], ldt2[g2:g2 + 1, :].to_broadcast([64, 48]), [], [pre + "dt%d" % g2], "p2%d" % g2, slow=True)
            dma("sync", evt, c_ev, [], [pre + "ev"], "p3")
            dma("sync", BR, w["ssm_b_re"][j, d].rearrange("g p c -> (g p c)").rearrange("(g q c) -> q g c", q=128, c=16), [], [pre + "BR"], "p4", slow=True)
            dma("sync", BI, w["ssm_b_im"][j, d].rearrange("g p c -> (g p c)").rearrange("(g q c) -> q g c", q=128, c=16), [], [pre + "BI"], "p5", slow=True)
            dma("sync", dcol[0:96, :], w["ssm_d"][j].rearrange("(f p) -> p f", p=96), [], [pre + "dcol"], "p6", slow=True)
            for ri, (src, dst) in enumerate(((w["ssm_c_re"], CR), (w["ssm_c_im"], CI))):
                cflat = src[j, d].rearrange("g c p -> (g c) p")
                for i in range(12):
                    dma("sync", cn[:, 0:64], cflat[128 * i:128 * i + 128, :], [], [pre + "cn"], "p7")
                    dma("sync", cn[:, 64:128], cflat[128 * i:128 * i + 128, :], [], [pre + "cn"], "p7")
                    b = nextps()
                    P.op("tensor", lambda e, b=b: e.transpose(out=ps[b][:, 0:128], in_=cn, identity=ident), [pre + "cn", "ident"], ["ps%d" % b])
                    tv = ps[b][:, 0:128].rearrange("q (l t c) -> q l t c", l=4, t=2)
                    P.op("vector", lambda e, tv=tv, dst=dst, i=i: e.tensor_copy(out=dst[0:64, 4 * i:4 * i + 4, :], in_=tv[0:64, :, 0, :]), ["ps%d" % b], [pre + "C%d" % ri])
                    P.op("scalar", lambda e, tv=tv, dst=dst, i=i: e.copy(out=dst[64:128, 4 * i:4 * i + 4, :], in_=tv[64:128, :, 1, :]), ["ps%d" % b], [pre + "C%d" % ri])
            P.op("scalar", lambda e: e.activation(out=dtt, in_=dtt, func=AF.Exp), [pre + "dt0", pre + "dt1"], [pre + "dt"])
            P.op("vector", lambda e: e.tensor_tensor(out=aa, in0=dtt, in1=lamre, op=ALU.mult), [pre + "dt", pre + "lamre"], [pre + "aa"])
            P.op("vector", lambda e: e.tensor_tensor(out=th, in0=dtt, in1=lamim, op=ALU.mult), [pre + "dt", pre + "lamim"], [pre + "th"])
            bc_g = lambda t: t.unsqueeze(1).to_broadcast([128, NE, 48])
            bc_e = lambda t: t.unsqueeze(2).to_broadcast([128, NE, 48])
            P.op("vector", lambda e: e.tensor_tensor(out=T1, in0=bc_g(th), in1=bc_e(evt), op=ALU.mult), [pre + "th", pre + "ev"], [pre + "ARG"])
            rr_sin(LI, T1, T3, T2i, MG, 0.0, [pre + "ARG"], pre + "LI")
            rr_sin(LR, T1, T3, T2i, MG, 0.25, [pre + "ARG", pre + "LI"], pre + "LR")
            P.op("vector", lambda e: e.tensor_tensor(out=T1, in0=bc_g(aa), in1=bc_e(evt), op=ALU.mult), [pre + "aa", pre + "ev", pre + "LR", pre + "LI", pre + "ARG"], [pre + "ARG"])
            P.op("scalar", lambda e: e.activation(out=MG, in_=T1, func=AF.Exp), [pre + "ARG", pre + "LRf", pre + "LIf", pre + "LR", pre + "LI"], [pre + "MG"])
            P.op("vector", lambda e: e.tensor_tensor(out=LR, in0=LR, in1=MG, op=ALU.mult), [pre + "LR", pre + "MG"], [pre + "LR"])
            P.op("vector", lambda e: e.tensor_tensor(out=LI, in0=LI, in1=MG, op=ALU.mult), [pre + "LI", pre + "MG"], [pre + "LI"])
            L1R = LR[:, 1, :]; L1I = LI[:, 1, :]
            V = "vector"
            P.op(V, lambda e: e.tensor_scalar(out=s1, in0=L1R, scalar1=-1.0, scalar2=None, op0=ALU.add), [pre + "LR"], [pre + "s1"])
            P.op(V, lambda e: e.tensor_tensor(out=s2, in0=lamre, in1=lamre, op=ALU.mult), [pre + "lamre"], [pre + "s2"])
            P.op(V, lambda e: e.tensor_tensor(out=s3, in0=lamim, in1=lamim, op=ALU.mult), [pre + "lamim"], [pre + "s3"])
            P.op(V, lambda e: e.tensor_tensor(out=s2, in0=s2, in1=s3, op=ALU.add), [pre + "s2", pre + "s3"], [pre + "s2"])
            P.op(V, lambda e: e.reciprocal(out=s2, in_=s2), [pre + "s2"], [pre + "s2"])
            P.op(V, lambda e: e.tensor_tensor(out=crr, in0=s1, in1=lamre, op=ALU.mult), [pre + "s1", pre + "lamre"], [pre + "crr"])
            P.op(V, lambda e: e.tensor_tensor(out=s3, in0=L1I, in1=lamim, op=ALU.mult), [pre + "LI", pre + "lamim", pre + "s2"], [pre + "s3"])
            P.op(V, lambda e: e.tensor_tensor(out=crr, in0=crr, in1=s3, op=ALU.add), [pre + "crr", pre + "s3"], [pre + "crr"])
            P.op(V, lambda e: e.tensor_tensor(out=crr, in0=crr, in1=s2, op=ALU.mult), [pre + "crr", pre + "s2"], [pre + "crr"])
            P.op(V, lambda e: e.tensor_tensor(out=cii, in0=L1I, in1=lamre, op=ALU.mult), [pre + "LI", pre + "lamre"], [pre + "cii"])
            P.op(V, lambda e: e.tensor_tensor(out=s3, in0=s1, in1=lamim, op=ALU.mult), [pre + "s1", pre + "lamim", pre + "crr"], [pre + "s3"])
            P.op(V, lambda e: e.tensor_tensor(out=cii, in0=cii, in1=s3, op=ALU.subtract), [pre + "cii", pre + "s3"], [pre + "cii"])
            P.op(V, lambda e: e.tensor_tensor(out=cii, in0=cii, in1=s2, op=ALU.mult), [pre + "cii", pre + "s2"], [pre + "cii"])
            bc_c = lambda t: t.unsqueeze(2).to_broadcast([128, 48, 16])
            TB = A.alloc([48, 16], F32)
            P.op(V, lambda e: e.tensor_tensor(out=BBR, in0=BR, in1=bc_c(crr), op=ALU.mult), [pre + "BR", pre + "crr"], [pre + "BBR"])
            P.op(V, lambda e: e.tensor_tensor(out=TB, in0=BI, in1=bc_c(cii), op=ALU.mult), [pre + "BI", pre + "cii"], [pre + "TB"])
            P.op(V, lambda e: e.tensor_tensor(out=BBR, in0=BBR, in1=TB, op=ALU.subtract), [pre + "BBR", pre + "TB"], [pre + "BBR"])
            P.op(V, lambda e: e.tensor_tensor(out=BBI, in0=BI, in1=bc_c(crr), op=ALU.mult), [pre + "BI", pre + "crr"], [pre + "BBI"])
            P.op(V, lambda e: e.tensor_tensor(out=TB, in0=BR, in1=bc_c(cii), op=ALU.mult), [pre + "BR", pre + "cii", pre + "BBR"], [pre + "TB"])
            P.op(V, lambda e: e.tensor_tensor(out=BBI, in0=BBI, in1=TB, op=ALU.add), [pre + "BBI", pre + "TB"], [pre + "BBI"])
            base_off = A.off
            for ft in range(16):
                A.reset(base_off)
                g0 = 3 * ft
                XR = A.alloc([32, 3, 16], F32); XI = A.alloc([32, 3, 16], F32); XT = A.alloc([32, 3, 16], F32)
                DQ = [A.alloc([32, 96], F32), A.alloc([32, 96], F32)]
                VR = A.alloc([33, 3, 16], F32); VI = A.alloc([33, 3, 16], F32); VT = A.alloc([33, 3, 16], F32)
                EE = [A.alloc([33, 3, 32], F32), A.alloc([33, 3, 32], F32)]
                Qt = A.alloc([32, 2, 128], BF16)
                Pt = A.alloc([3, 32, 2, 32], BF16)
                Kt = A.alloc([32, 96], BF16)
                K0 = A.alloc([1, 96], F32)[:, 0, :]
                fp = pre + "f"
                lrb = lambda t, ne, g0=g0: t[:, 0:ne, g0:g0 + 3].unsqueeze(3).to_broadcast([128, ne, 3, 16])
                bb = lambda t, ne, g0=g0: t[:, g0:g0 + 3, :].unsqueeze(1).to_broadcast([128, ne, 3, 16])
                P.op(V, lambda e, XR=XR, lrb=lrb, bb=bb: e.tensor_tensor(out=XR, in0=lrb(LR, 32), in1=bb(BBR, 32), op=ALU.mult), [pre + "LR", pre + "BBR"], [fp + "XR"])
                P.op("gpsimd", lambda e, XT=XT, lrb=lrb, bb=bb: e.tensor_tensor(out=XT, in0=lrb(LI, 32), in1=bb(BBI, 32), op=ALU.mult), [pre + "LI", pre + "BBI"], [fp + "XT"])
                P.op(V, lambda e, XR=XR, XT=XT: e.tensor_tensor(out=XR, in0=XR, in1=XT, op=ALU.subtract), [fp + "XR", fp + "XT"], [fp + "XR"])
                P.op(V, lambda e, XI=XI, lrb=lrb, bb=bb: e.tensor_tensor(out=XI, in0=lrb(LR, 32), in1=bb(BBI, 32), op=ALU.mult), [pre + "LR", pre + "BBI"], [fp + "XI"])
                P.op("gpsimd", lambda e, XT=XT, lrb=lrb, bb=bb: e.tensor_tensor(out=XT, in0=lrb(LI, 32), in1=bb(BBR, 32), op=ALU.mult), [pre + "LI", pre + "BBR", fp + "XR"], [fp + "XT"])
                P.op(V, lambda e, XI=XI, XT=XT: e.tensor_tensor(out=XI, in0=XI, in1=XT, op=ALU.add), [fp + "XI", fp + "XT"], [fp + "XI"])
                m2 = lambda n: msk2.unsqueeze(1).unsqueeze(3).to_broadcast([128, n, 2, 16])
                for c, Xc in enumerate((XR, XI)):
                    xin = Xc.rearrange("q e g c -> q (e g) c").unsqueeze(2).to_broadcast([128, 96, 2, 16])
                    dqo = DQ[c].rearrange("q e (g t c) -> q (e g) t c", g=3, t=2)
                    P.op(V if c == 0 else "gpsimd", lambda e, dqo=dqo, xin=xin, m2=m2: e.tensor_tensor(out=dqo, in0=xin, in1=m2(96), op=ALU.mult),
                         [fp + ("XR" if c == 0 else "XI"), "msk2"], [fp + "DQ%d" % c])
                for e2 in range(0, 32, 2):
                    b = nextps()
                    for ee in range(2):
                        for c in range(2):
                            sl = (ee * 2 + c) * 128
                            P.op("tensor", lambda e, b=b, sl=sl, c=c, e_=e2 + ee, DQ=DQ: e.transpose(out=ps[b][0:96, sl:sl + 128], in_=DQ[c][:, e_, :], identity=ident),
                                 [fp + "DQ%d" % c, "ident"], ["ps%d" % b])
                    pv = ps[b][0:96, :].rearrange("q (a c n) -> q a c n", a=2, c=2)
                    P.op(V if (e2 // 2) % 2 == 0 else "scalar",
                         (lambda e, pv=pv, e2=e2, Qt=Qt: e.tensor_copy(out=Qt[0:96, e2:e2 + 2, :, :], in_=pv)) if (e2 // 2) % 2 == 0 else
                         (lambda e, pv=pv, e2=e2, Qt=Qt: e.copy(out=Qt[0:96, e2:e2 + 2, :, :], in_=pv)),
                         ["ps%d" % b], [fp + "Qt"])
                dma("gpsimd", Qd[j, ft, :, d * 8192:(d + 1) * 8192], Qt[0:96].rearrange("q e c n -> q (e c n)"), [fp + "Qt"], ["Qd"], "pq")
                cb = lambda t, ne, g0=g0: t[:, g0:g0 + 3, :].unsqueeze(1).to_broadcast([128, ne, 3, 16])
                P.op(V, lambda e, VR=VR, lrb=lrb, cb=cb: e.tensor_tensor(out=VR, in0=lrb(LR, 33), in1=cb(CR, 33), op=ALU.mult), [pre + "LR", pre + "C0"], [fp + "VR"])
                P.op("gpsimd", lambda e, VT=VT, lrb=lrb, cb=cb: e.tensor_tensor(out=VT, in0=lrb(LI, 33), in1=cb(CI, 33), op=ALU.mult), [pre + "LI", pre + "C1"], [fp + "VT"])
                P.op(V, lambda e, VR=VR, VT=VT: e.tensor_tensor(out=VR, in0=VR, in1=VT, op=ALU.subtract), [fp + "VR", fp + "VT"], [fp + "VR"])
                P.op(V, lambda e, VI=VI, lrb=lrb, cb=cb: e.tensor_tensor(out=VI, in0=lrb(LR, 33), in1=cb(CI, 33), op=ALU.mult), [pre + "LR", pre + "C1"], [fp + "VI"])
                P.op("gpsimd", lambda e, VT=VT, lrb=lrb, cb=cb: e.tensor_tensor(out=VT, in0=lrb(LI, 33), in1=cb(CR, 33), op=ALU.mult), [pre + "LI", pre + "C0", fp + "VR"], [fp + "VT"])
                P.op(V, lambda e, VI=VI, VT=VT: e.scalar_tensor_tensor(out=VI, in0=VI, scalar=-1.0, in1=VT, op0=ALU.mult, op1=ALU.subtract), [fp + "VI", fp + "VT"], [fp + "VI"])
                for c, Vc in enumerate((VR, VI)):
                    vin = Vc.rearrange("q e g c -> q (e g) c").unsqueeze(2).to_broadcast([128, 99, 2, 16])
                    eo = EE[c].rearrange("q e g (t c) -> q (e g) t c", t=2)
                    P.op(V if c == 0 else "gpsimd", lambda e, eo=eo, vin=vin, m2=m2: e.tensor_tensor(out=eo, in0=vin, in1=m2(99), op=ALU.mult),
                         [fp + ("VR" if c == 0 else "VI"), "msk2"], [fp + "EE%d" % c])
                    for gl in range(3):
                        P.op("scalar", lambda e, gl=gl, c=c, Pt=Pt, EE=EE: e.copy(out=Pt[:, gl, :, c, :], in_=EE[c][:, 1:33, gl, :]), [fp + "EE%d" % c], [fp + "Pt"])
                dma("gpsimd", Pd[j, ft].rearrange("q (g d r) -> q g d r", g=3, d=2)[:, :, d, :], Pt.rearrange("q g e c n -> q g (e c n)"), [fp + "Pt"], ["Pd"], "pp")
                for bk in range(8):
                    b = nextps()
                    P.op(V, lambda e, b=b: e.memset(ps[b][0:96, 0:384], 0.0), [], ["ps%d" % b])
                    pk = ps[b][0:96, 0:384].rearrange("q (l n) -> q l n", l=4)
                    for gl in range(3):
                        for c in range(2):
                            P.op("tensor", lambda e, gl=gl, c=c, bk=bk, pk=pk, DQ=DQ, EE=EE: e.matmul(
                                pk[32 * gl:32 * gl + 32, :, 32 * gl:32 * gl + 32], lhsT=DQ[c][:, 0, 32 * gl:32 * gl + 32],
                                rhs=EE[c][:, 4 * bk:4 * bk + 4, gl, :], start=False, stop=(gl == 2 and c == 1), skip_group_check=True),
                                [fp + "DQ%d" % c, fp + "EE%d" % c, "ps%d" % b], ["ps%d" % b])
                    P.op(V, lambda e, pk=pk, bk=bk, Kt=Kt: e.tensor_copy(out=Kt[0:96, 4 * bk:4 * bk + 4, :], in_=pk), ["ps%d" % b], [fp + "Kt"])
                    if bk == 0 and d == 0:
                        P.op(V, lambda e, pk=pk, K0=K0, ft=ft: e.scalar_tensor_tensor(out=K0[0:96, :], in0=ident[0:96, 0:96], scalar=dcol[0:96, ft:ft + 1], in1=pk[:, 0, :],
                                                                                op0=ALU.mult, op1=ALU.add), ["ps%d" % b, "ident", pre + "dcol"], [fp + "K0"])
                        P.op(V, lambda e, K0=K0, Kt=Kt: e.tensor_copy(out=Kt[0:96, 0, :], in_=K0[0:96, :]), [fp + "K0", fp + "Kt"], [fp + "Kt"])
                dma("gpsimd", Kd[j, ft, :, d * 3072:(d + 1) * 3072], Kt[0:96].rearrange("q l n -> q (l n)"), [fp + "Kt"], ["Kd"], "pk")
            dma("gpsimd", SC_d[j, d, :, 0, :, :], LR[:, 32:32 + 7, :], [pre + "LR"], ["SC_d"], "psc0")
            dma("gpsimd", SC_d[j, d, :, 1, :, :], LI[:, 32:32 + 7, :], [pre + "LI"], ["SC_d"], "psc1")
            P.barrier(skip=("cast",))
    if stop == 'pre':
        return finish()
    RING = 4
    ring_i = [0]

    def wget(src3, rows, nkt, ncols):
        slot = ring_i[0] % RING
        ring_i[0] += 1
        dst = arena_t[0:rows, slot * 8192: slot * 8192 + nkt * ncols].rearrange("p (k n) -> p k n", k=nkt)
        dma("sync", dst, src3, ["wb"], ["ring%d" % slot], "ring%d" % slot)
        return dst, "ring%d" % slot

    def wchunk(name, l, r0, nrows_p, nkt, c0, ncols):
        v = wb[name][l, r0:r0 + nrows_p * nkt, c0:c0 + ncols].rearrange("(k p) n -> p k n", p=nrows_p)
        return wget(v, nrows_p, nkt, ncols)

    H_OFF = 64 * 1024
    XR_OFF = 128 * 1024
    XT_OFF = 160 * 1024
    LNP_OFF = 176 * 1024
    xr = arena_t[:, XR_OFF // 2:(XR_OFF + 32768) // 2].bitcast(F32).rearrange("p (s d) -> p s d", s=4)
    xT = arena_t[:, XT_OFF // 2:(XT_OFF + 16384) // 2].rearrange("p (k t) -> p k t", k=16)
    lnp = arena_t[:, LNP_OFF // 2:(LNP_OFF + 16384) // 2].bitcast(F32).rearrange("p (a d) -> p a d", a=2)
    hT = arena_t[:, H_OFF // 2:(H_OFF + 65536) // 2].rearrange("p (k t) -> p k t", k=64)
    HA = Arena(arena_t, 192 * 1024)
    evc = [0]

    def evac_copy(out, in_, rd, wr):
        evc[0] += 1
        if evc[0] % 2:
            P.op("vector", lambda e: e.tensor_copy(out=out, in_=in_), rd, wr)
        else:
            P.op("scalar", lambda e: e.copy(out=out, in_=in_), rd, wr)

    def transposes_to_xT(xb, rd):
        for kp in range(8):
            b = 6 + kp % 2
            pb = ps[b][:].bitcast(BF16)
            for kk in range(2):
                kt = 2 * kp + kk
                for sub in range(4):
                    P.op("tensor", lambda e, pb=pb, kk=kk, sub=sub, kt=kt: e.transpose(out=pb[:, kk * 512 + sub * 128: kk * 512 + sub * 128 + 128],
                                                                                  in_=xb[:, sub, kt * 128:(kt + 1) * 128], identity=identb), rd + ["identb"], ["ps%d" % b])
            evac_copy(xT[:, 2 * kp:2 * kp + 2, :], pb.rearrange("p (k t) -> p k t", k=2), ["ps%d" % b], ["xT"])

    def layer_norm_all():
        V = "vector"
        sc = lambda sub, c0, c1: smalls[:, 8 + 32 * sub + c0: 8 + 32 * sub + c1]
        for c in range(4):
            for sub in range(4):
                P.op(V, lambda e, c=c, sub=sub: e.bn_stats(out=sc(sub, 6 * c, 6 * c + 6), in_=xr[:, sub, 512 * c:512 * c + 512]), ["xr%d" % sub], ["bst%d" % sub])
        for sub in range(4):
            P.op(V, lambda e, sub=sub: e.bn_aggr(out=sc(sub, 24, 26), in_=sc(sub, 0, 24)), ["bst%d" % sub], ["bag%d" % sub])
        for sub in range(4):
            P.op("scalar", lambda e, sub=sub: e.activation(out=sc(sub, 26, 27), in_=sc(sub, 25, 26), func=AF.Sqrt, bias=epst, scale=1.0), ["bag%d" % sub, "eps"], ["bsd%d" % sub])
        for sub in range(4):
            P.op(V, lambda e, sub=sub: e.reciprocal(out=sc(sub, 27, 28), in_=sc(sub, 26, 27)), ["bsd%d" % sub], ["brs%d" % sub])
        for sub in range(4):
            P.op(V, lambda e, sub=sub: e.scalar_tensor_tensor(out=sc(sub, 28, 29), in0=sc(sub, 24, 25), scalar=-1.0, in1=sc(sub, 27, 28), op0=ALU.mult, op1=ALU.mult), ["bag%d" % sub, "brs%d" % sub], ["bnb%d" % sub])
        for sub in range(4):
            P.op("scalar", lambda e, sub=sub: e.activation(out=xr[:, sub, :], in_=xr[:, sub, :], func=AF.Identity, bias=sc(sub, 28, 29), scale=sc(sub, 27, 28)), ["xr%d" % sub, "brs%d" % sub, "bnb%d" % sub], ["xr%d" % sub])
        for sub in range(4):
            eng = V if sub % 2 == 0 else "gpsimd"
            P.op(eng, lambda e, sub=sub: e.tensor_tensor(out=xr[:, sub, :], in0=xr[:, sub, :], in1=lnp[:, 0, :], op=ALU.mult), ["xr%d" % sub, "lnp"], ["xr%d" % sub])
        for sub in range(4):
            eng = "gpsimd" if sub % 2 == 0 else V
            P.op(eng, lambda e, sub=sub: e.tensor_tensor(out=xr[:, sub, :], in0=xr[:, sub, :], in1=lnp[:, 1, :], op=ALU.add), ["xr%d" % sub, "lnp"], ["xr%d" % sub])

    epst = smalls[:, 0:1]
    P.op("vector", lambda e: e.memset(epst, LN_EPS), [], ["eps"])

    for s in range(NS):
        for li in range(DEPTH):
            j = li // 2
            is_ssm = (li % 2 == 0)
            src = x_in if li == 0 else xbuf
            dst = y_out if li == DEPTH - 1 else xbuf
            lp = "s%dl%d" % (s, li)
            P.barrier()
            HA.reset(H_OFF)
            memb = HA.alloc([2, 2048], BF16)
            memT = HA.alloc([16, 256], BF16)
            dma("sync", xr[:, 0:2, :], mem_in[s].rearrange("(m p) d -> p m d", p=128), [], ["xr0", "xr1"], "xr")
            P.op("vector", lambda e: e.tensor_copy(out=memb[:, 0, :], in_=xr[:, 0, :]), ["xr0"], ["memb"])
            P.op("scalar", lambda e: e.copy(out=memb[:, 1, :], in_=xr[:, 1, :]), ["xr1"], ["memb"])
            for kp in range(4):
                b = 6 + kp % 2
                pb = ps[b][:].bitcast(BF16)
                for kk in range(4):
                    kt = 4 * kp + kk
                    for m in range(2):
                        P.op("tensor", lambda e, pb=pb, kk=kk, m=m, kt=kt: e.transpose(out=pb[:, kk * 256 + m * 128: kk * 256 + m * 128 + 128], in_=memb[:, m, kt * 128:(kt + 1) * 128], identity=identb),
                             ["memb", "identb"], ["ps%d" % b])
                evac_copy(memT[:, 4 * kp:4 * kp + 4, :], pb.rearrange("p (k t) -> p k t", k=4), ["ps%d" % b], ["memT"])
            wc, wr_ = wchunk("w_mem_kv", li, 0, 128, 16, 0, 512)
            for h in range(4):
                b = nextps()
                for kt in range(16):
                    P.op("tensor", lambda e, b=b, h=h, kt=kt, wc=wc: e.matmul(ps[b][:, 0:256], lhsT=wc[:, kt, h * 128:(h + 1) * 128], rhs=memT[:, kt, :], start=(kt == 0), stop=(kt == 15)),
                         ["memT", wr_], ["ps%d" % b])
                evac_copy(kmemT[:, h, :], ps[b][:, 0:256], ["ps%d" % b], ["kmemT"])
            wc, wr_ = wchunk("w_mem_kv", li, 0, 128, 16, 512, 512)
            for m in range(2):
                b = nextps()
                for kt in range(16):
                    P.op("tensor", lambda e, b=b, m=m, kt=kt, wc=wc: e.matmul(ps[b][:], lhsT=memT[:, kt, m * 128:(m + 1) * 128], rhs=wc[:, kt, :], start=(kt == 0), stop=(kt == 15)),
                         ["memT", wr_], ["ps%d" % b])
                evac_copy(vmem[:, m, :], ps[b][:], ["ps%d" % b], ["vmem"])
            if not is_ssm:
                dma("sync", sinkt, w["attn_sink"][j].partition_broadcast(128), [], ["sink"], "sink")
            if stop == 'mem':
                return finish()
            P.barrier()
            HA.reset(H_OFF)
            xb = HA.alloc([4, 2048], BF16)
            if is_ssm:
                ust = HA.alloc([16, 512], BF16)
                qst = HA.alloc([4, 512], BF16)
            else:
                qTst = HA.alloc([16, 512], BF16)
                qst = HA.alloc([4, 512], BF16)
                vst = HA.alloc([4, 512], BF16)
                rk = HA.alloc([4, 4, 128], BF16)
                rt = [HA.alloc([4, 64], F32) for _ in range(4)]
            for t in range(NT):
                tk = slice(t * 512, (t + 1) * 512)
                dma("sync", xr, src[s, tk, :].rearrange("(u p) d -> p u d", p=128), [], ["xr0", "xr1", "xr2", "xr3"], "xr")
                for sub in range(4):
                    eng = ("vector", "scalar", "gpsimd", "vector")[sub]
                    if eng == "scalar":
                        P.op(eng, lambda e, sub=sub, xb=xb: e.copy(out=xb[:, sub, :], in_=xr[:, sub, :]), ["xr%d" % sub], ["xb%d" % sub])
                    else:
                        P.op(eng, lambda e, sub=sub, xb=xb: e.tensor_copy(out=xb[:, sub, :], in_=xr[:, sub, :]), ["xr%d" % sub], ["xb%d" % sub])
                transposes_to_xT(xb, ["xb0", "xb1", "xb2", "xb3"])
                if is_ssm:
                    for c in range(4):
                        wc, wr_ = wchunk("ssm_w_in", j, 0, 128, 16, 384 * c, 384)
                        for fi in range(4):
                            b = nextps()
                            for kt in range(16):
                                P.op("tensor", lambda e, b=b, fi=fi, kt=kt, wc=wc: e.matmul(ps[b][0:96, :], lhsT=wc[:, kt, fi * 96:(fi + 1) * 96], rhs=xT[:, kt, :], start=(kt == 0), stop=(kt == 15)),
                                     ["xT", wr_], ["ps%d" % b])
                            evac_copy(ust[0:96, 4 * c + fi, :], ps[b][0:96, :], ["ps%d" % b], ["ust"])
                    dma("gpsimd", U_d[s, :, :, tk].rearrange("f p t -> p f t"), ust[0:96], ["ust"], ["U_d"], "ust")
                    wc, wr_ = wchunk("ssm_w_in", j, 0, 128, 16, 1536, 512)
                else:
                    dma("sync", cst[:, 0], c_cos[tk, :].rearrange("(u p) f -> p u f", p=128), [], ["cst"], "cst")
                    dma("sync", cst[:, 1], c_sin[tk, :].rearrange("(u p) f -> p u f", p=128), [], ["cst"], "cst")
                    for c in range(4):
                        wc, wr_ = wchunk("attn_w_in", j, 0, 128, 16, 512 * c, 512)
                        for sub in range(4):
                            b = nextps()
                            for kt in range(16):
                                P.op("tensor", lambda e, b=b, sub=sub, kt=kt, wc=wc: e.matmul(ps[b][:], lhsT=xT[:, kt, sub * 128:(sub + 1) * 128], rhs=wc[:, kt, :], start=(kt == 0), stop=(kt == 15)),
                                     ["xT", wr_], ["ps%d" % b])
                            pv = ps[b][:].rearrange("p (h two f) -> p h two f", h=4, two=2)
                            cosb = cst[:, 0, sub, :].unsqueeze(1).to_broadcast([128, 4, 64])
                            sinb = cst[:, 1, sub, :].unsqueeze(1).to_broadcast([128, 4, 64])
                            rn = "rt%d" % sub
                            P.op("vector", lambda e, pv=pv, cosb=cosb: e.tensor_tensor(out=rt[0], in0=pv[:, :, 0, :], in1=cosb, op=ALU.mult), ["ps%d" % b, "cst"], ["rt0"])
                            P.op("vector", lambda e, pv=pv, sinb=sinb: e.tensor_tensor(out=rt[1], in0=pv[:, :, 1, :], in1=sinb, op=ALU.mult), ["ps%d" % b, "cst"], ["rt1"])
                            P.op("vector", lambda e, pv=pv, cosb=cosb: e.tensor_tensor(out=rt[2], in0=pv[:, :, 1, :], in1=cosb, op=ALU.mult), ["ps%d" % b, "cst"], ["rt2"])
                            P.op("vector", lambda e, pv=pv, sinb=sinb: e.tensor_tensor(out=rt[3], in0=pv[:, :, 0, :], in1=sinb, op=ALU.mult), ["ps%d" % b, "cst"], ["rt3"])
                            P.op("gpsimd", lambda e, sub=sub: e.tensor_tensor(out=rk[:, sub, :, 0:64], in0=rt[0], in1=rt[1], op=ALU.subtract), ["rt0", "rt1"], ["rk%d" % sub])
                            P.op("gpsimd", lambda e, sub=sub: e.tensor_tensor(out=rk[:, sub, :, 64:128], in0=rt[2], in1=rt[3], op=ALU.add), ["rt2", "rt3"], ["rk%d" % sub])
                        for hp in range(2):
                            b = 6 + hp % 2
                            pb = ps[b][:].bitcast(BF16)
                            for hh in range(2):
                                for sub in range(4):
                                    P.op("tensor", lambda e, pb=pb, hh=hh, sub=sub, hp=hp: e.transpose(out=pb[:, hh * 512 + sub * 128: hh * 512 + sub * 128 + 128], in_=rk[:, sub, 2 * hp + hh, :], identity=identb),
                                         ["rk%d" % sub, "identb"], ["ps%d" % b])
                            evac_copy(qTst[:, 4 * c + 2 * hp: 4 * c + 2 * hp + 2, :], pb.rearrange("p (k t) -> p k t", k=2), ["ps%d" % b], ["qTst"])
                    dma("gpsimd", QT_d[s, :, :, tk].rearrange("f p t -> p f t"), qTst[:, 0:12, :], ["qTst"], ["QT_d"], "qTst")
                    dma("gpsimd", KT_d[s, :, :, tk].rearrange("f p t -> p f t"), qTst[:, 12:16, :], ["qTst"], ["KT_d"], "kTst")
                    wc, wr_ = wchunk("attn_w_in", j, 0, 128, 16, 2048, 512)
                    for sub in range(4):
                        b = nextps()
                        for kt in range(16):
                            P.op("tensor", lambda e, b=b, sub=sub, kt=kt, wc=wc: e.matmul(ps[b][:], lhsT=xT[:, kt, sub * 128:(sub + 1) * 128], rhs=wc[:, kt, :], start=(kt == 0), stop=(kt == 15)),
                                 ["xT", wr_], ["ps%d" % b])
                        evac_copy(vst[:, sub, :], ps[b][:], ["ps%d" % b], ["vst"])
                    dma("gpsimd", V_d[s, tk, :].rearrange("(u p) f -> p u f", p=128), vst, ["vst"], ["V_d"], "vst")
                    wc, wr_ = wchunk("attn_w_in", j, 0, 128, 16, 2560, 512)
                for m in range(4):
                    b = nextps()
                    for kt in range(16):
                        P.op("tensor", lambda e, b=b, m=m, kt=kt, wc=wc: e.matmul(ps[b][:], lhsT=wc[:, kt, m * 128:(m + 1) * 128], rhs=xT[:, kt, :], start=(kt == 0), stop=(kt == 15)),
                             ["xT", wr_], ["ps%d" % b])
                    evac_copy(qst[:, m, :], ps[b][:], ["ps%d" % b], ["qst"])
                dma("gpsimd", QM_d[s, :, :, tk].rearrange("f p t -> p f t"), qst, ["qst"], ["QM_d"], "qst")
            if stop == 'A':
                return finish()
            if is_ssm:
                P.barrier()
                HA.reset(0)
                unat = HA.alloc([1, L], BF16)[:, 0, :]
                uperm = HA.alloc([TC, NCH], BF16)
                znat = HA.alloc([1, L], BF16)[:, 0, :]
                Qt = HA.alloc([2, 32, 2, 128], BF16)
                Pt = HA.alloc([3, 2, 32, 2, 32], BF16)
                Kt = HA.alloc([2, 32, 96], BF16)
                SC = HA.alloc([2, 2, 7, 48], F32)
                X = [HA.alloc([3, 2, 2, NCH], F32) for _ in range(2)]
                TM = [HA.alloc([3, 2, 2, NCH], F32) for _ in range(2)]
                Hb = HA.alloc([3, 2, 2, NCH], BF16)
                g1 = HA.alloc([1, 2048], F32)[:, 0, :]
                g2t = HA.alloc([1, 2048], F32)[:, 0, :]
                dma("sync", SC[:, 0], SC_d[j, 0], [], ["SC"], "SC")
                dma("sync", SC[:, 1], SC_d[j, 1], [], ["SC"], "SC")
                NBK = L // 512
                TPB = 512 // NCH
                for ft in range(16):
                    g0 = 3 * ft
                    dma("sync", unat[0:96, :], U_d[s, ft], ["U_d"], ["unat"], "unat")
                    dma("sync", Qt[0:96].rearrange("q a b c d -> q (a b c d)"), Qd[j, ft], ["Qd"], ["Qt"], "Qt")
                    dma("sync", Pt.rearrange("q a b c d e -> q (a b c d e)"), Pd[j, ft], ["Pd"], ["Pt"], "Pt")
                    dma("sync", Kt[0:96].rearrange("q a b c -> q (a b c)"), Kd[j, ft], ["Kd"], ["Kt"], "Kt")
                    P.op("vector", lambda e: e.tensor_copy(out=uperm[0:96], in_=unat[0:96, :].rearrange("p (k t) -> p t k", t=TC)), ["unat"], ["uperm"])
                    if stop == 'B0':
                        return finish()
                    for gl in range(3):
                        for d in range(2):
                            for c in range(2):
                                b = gl
                                col = (d * 2 + c) * NCH
                                for sg in range(TC):
                                    e_ = (31 - sg) if d == 0 else sg
                                    P.op("tensor", lambda e, b=b, col=col, gl=gl, d=d, c=c, sg=sg, e_=e_: e.matmul(
                                        ps[b][:, col:col + NCH], lhsT=Qt[32 * gl:32 * gl + 32, d, e_, c, :], rhs=uperm[32 * gl:32 * gl + 32, sg, :],
                                        start=(sg == 0), stop=(sg == TC - 1), skip_group_check=True), ["Qt", "uperm"], ["ps%d" % b])
                    for gl in range(3):
                        xo = X[0][:, gl].rearrange("q d c k -> q (d c) k")
                        evac_copy(xo, ps[gl][:, 0:4 * NCH].rearrange("q (a k) -> q a k", k=NCH), ["ps%d" % gl], ["X0d0", "X0d1"])
                    if stop == 'B1':
                        return finish()
                    cur = 0
                    for s_ in range(NSTEP):
                        sh = 2 ** s_
                        n_ = NCH - sh
                        Xi, Xo = X[cur], X[1 - cur]
                        ar = lambda d, c, s_=s_, g0=g0, n_=n_: SC[:, d, c, s_, g0:g0 + 3].unsqueeze(2).to_broadcast([128, 3, n_])
                        for d in range(2):
                            if d == 0:
                                dsl, ssl, ksl = slice(sh, NCH), slice(0, n_), slice(0, sh)
                            else:
                                dsl, ssl, ksl = slice(0, n_), slice(sh, NCH), slice(n_, NCH)
                            eng = "vector" if d == 0 else "gpsimd"
                            t0_, t1_ = TM[0][:, :, d, 0, 0:n_], TM[1][:, :, d, 0, 0:n_]
                            u0_, u1_ = TM[0][:, :, d, 1, 0:n_], TM[1][:, :, d, 1, 0:n_]
                            tn = "TM%d" % d
                            ci, co_ = "X%dd%d" % (cur, d), "X%dd%d" % (1 - cur, d)
                            xre, xim = Xi[:, :, d, 0, ssl], Xi[:, :, d, 1, ssl]
                            P.op(eng, lambda e, t0_=t0_, xre=xre, d=d, ar=ar: e.tensor_tensor(out=t0_, in0=xre, in1=ar(d, 0), op=ALU.mult), [ci, "SC"], [tn + "a"])
                            P.op(eng, lambda e, u0_=u0_, xim=xim, d=d, ar=ar: e.tensor_tensor(out=u0_, in0=xim, in1=ar(d, 0), op=ALU.mult), [ci, "SC"], [tn + "c"])
                            P.op(eng, lambda e, t1_=t1_, xim=xim, d=d, ar=ar: e.tensor_tensor(out=t1_, in0=xim, in1=ar(d, 1), op=ALU.mult), [ci, "SC"], [tn + "b"])
                            P.op(eng, lambda e, u1_=u1_, xre=xre, d=d, ar=ar: e.tensor_tensor(out=u1_, in0=xre, in1=ar(d, 1), op=ALU.mult), [ci, "SC"], [tn + "d"])
                            P.op(eng, lambda e, t0_=t0_, t1_=t1_: e.tensor_tensor(out=t0_, in0=t0_, in1=t1_, op=ALU.subtract), [tn + "a", tn + "b"], [tn + "a"])
                            P.op(eng, lambda e, u0_=u0_, u1_=u1_: e.tensor_tensor(out=u0_, in0=u0_, in1=u1_, op=ALU.add), [tn + "c", tn + "d"], [tn + "c"])
                            P.op(eng, lambda e, Xo=Xo, Xi=Xi, d=d, dsl=dsl, t0_=t0_: e.tensor_tensor(out=Xo[:, :, d, 0, dsl], in0=Xi[:, :, d, 0, dsl], in1=t0_, op=ALU.add), [ci, tn + "a"], [co_])
                            P.op(eng, lambda e, Xo=Xo, Xi=Xi, d=d, dsl=dsl, u0_=u0_: e.tensor_tensor(out=Xo[:, :, d, 1, dsl], in0=Xi[:, :, d, 1, dsl], in1=u0_, op=ALU.add), [ci, tn + "c"], [co_])
                            P.op("scalar", lambda e, Xo=Xo, Xi=Xi, d=d, ksl=ksl: e.copy(out=Xo[:, :, d, :, ksl], in_=Xi[:, :, d, :, ksl]), [ci], [co_])
                        cur = 1 - cur
                    P.op("gpsimd", lambda e: e.memset(Hb, 0.0), [], ["Hb"])
                    P.op("vector", lambda e, cur=cur: e.tensor_copy(out=Hb[:, :, 0, :, 1:NCH], in_=X[cur][:, :, 0, :, 0:NCH - 1]), ["X%dd0" % cur, "Hb"], ["Hb"])
                    P.op("vector", lambda e, cur=cur: e.tensor_copy(out=Hb[:, :, 1, :, 0:NCH - 1], in_=X[cur][:, :, 1, :, 1:NCH]), ["X%dd1" % cur, "Hb"], ["Hb"])
                    if stop == 'B2':
                        return finish()
                    for b in range(NBK):
                        t_lo = b * TPB
                        first = True
                        yb = ps[b][0:96, :].rearrange("q (t k) -> q t k", k=NCH)
                        for d in range(2):
                            for l in range(min(TC, t_lo + TPB)):
                                if d == 0:
                                    ta = max(l, t_lo); tb_ = t_lo + TPB
                                    if ta >= tb_:
                                        continue
                                    out_ = yb[:, ta - t_lo:tb_ - t_lo, :]
                                    rhs_ = uperm[0:96, ta - l:tb_ - l, :]
                                else:
                                    continue
                                P.op("tensor", lambda e, out_=out_, rhs_=rhs_, d=d, l=l, first=first: e.matmul(out_, lhsT=Kt[0:96, d, l, :], rhs=rhs_, start=first, stop=False, skip_group_check=True),
                                     ["Kt", "uperm"], ["ps%d" % b])
                                first = False
                        for l in range(TC):
                            ta = t_lo; tb_ = min(t_lo + TPB, TC - l)
                            if ta >= tb_:
                                continue
                            out_ = yb[:, ta - t_lo:tb_ - t_lo, :]
                            rhs_ = uperm[0:96, ta + l:tb_ + l, :]
                            P.op("tensor", lambda e, out_=out_, rhs_=rhs_, l=l: e.matmul(out_, lhsT=Kt[0:96, 1, l, :], rhs=rhs_, start=False, stop=False, skip_group_check=True),
                                 ["Kt", "uperm"], ["ps%d" % b])
                        if stop == 'B3':
                            return finish()
                        for tau in range(t_lo, t_lo + TPB):
                            for gl in range(3):
                                for d in range(2):
                                    e_ = tau if d == 0 else 31 - tau
                                    for c in range(2):
                                        out_ = yb[32 * gl:32 * gl + 32, tau - t_lo, :]
                                        rhs_ = Hb[:, gl, d, c, :]
                                        last = (tau == t_lo + TPB - 1 and gl == 2 and d == 1 and c == 1)
                                        P.op("tensor", lambda e, out_=out_, rhs_=rhs_, gl=gl, d=d, e_=e_, c=c, last=last: e.matmul(out_, lhsT=Pt[:, gl, d, e_, c, :], rhs=rhs_, start=False, stop=last, skip_group_check=True),
                                             ["Pt", "Hb"], ["ps%d" % b])
                        if stop == 'B4':
                            return finish()
                        ysb = g1[0:96, 0:512]
                        tt = g2t[0:96, 0:512]
                        P.op("vector", lambda e, b=b, ysb=ysb: e.tensor_copy(out=ysb, in_=ps[b][0:96, :]), ["ps%d" % b], ["g1"])
                        P.op("gpsimd", lambda e, ysb=ysb, tt=tt: e.tensor_tensor(out=tt, in0=ysb, in1=ysb, op=ALU.mult), ["g1"], ["g2"])
                        P.op("gpsimd", lambda e, tt=tt: e.tensor_scalar(out=tt, in0=tt, scalar1=0.044715, scalar2=1.0, op0=ALU.mult, op1=ALU.add), ["g2"], ["g2"])
                        P.op("gpsimd", lambda e, ysb=ysb, tt=tt: e.tensor_tensor(out=tt, in0=tt, in1=ysb, op=ALU.mult), ["g1", "g2"], ["g2"])
                        P.op("scalar", lambda e, tt=tt: e.activation(out=tt, in_=tt, func=AF.Sigmoid, scale=float(2.0 * np.sqrt(2.0 / np.pi))), ["g2"], ["g2"])
                        zv = znat[0:96, :].rearrange("p (k t) -> p t k", t=TC)[:, t_lo:t_lo + TPB, :]
                        P.op("vector", lambda e, zv=zv, ysb=ysb, tt=tt: e.tensor_tensor(out=zv, in0=ysb.rearrange("p (t k) -> p t k", k=NCH), in1=tt.rearrange("p (t k) -> p t k", k=NCH), op=ALU.mult), ["g1", "g2"], ["znat"])
                    dma("gpsimd", Z_d[s, ft], znat[0:96, :], ["znat"], ["Z_d"], "znat")
            if stop == 'B':
                return finish()
            P.barrier()
            for t in range(NT):
                tk = slice(t * 512, (t + 1) * 512)
                HA.reset(H_OFF)
                ymixT = HA.alloc([12, 512], BF16)
                ymemT = HA.alloc([4, 512], BF16)
                qmT = HA.alloc([4, 512], BF16)
                smm = [HA.alloc([1, 256], F32)[:, 0, :] for _ in range(4)]
                pm = [HA.alloc([1, 256], BF16)[:, 0, :] for _ in range(4)]
                pmT = [HA.alloc([2, 128], BF16) for _ in range(4)]
                dma("sync", qmT, QM_d[s, :, :, tk].rearrange("f p t -> p f t"), ["QM_d"], ["qmT", "hT0", "hT1", "hT2", "hT3"], "qmT")
                if is_ssm:
                    zT = HA.alloc([16, 512], BF16)
                    sg_ = HA.alloc([1, 512], F32)[:, 0, :]
                    dma("sync", zT[0:96], Z_d[s, :, :, tk].rearrange("f p t -> p f t"), ["Z_d"], ["zT", "hT0", "hT1", "hT2", "hT3"], "zT")
                    for c in range(3):
                        wa, wra = wchunk("ssm_w_glu", j, 0, 96, 16, 512 * c, 512)
                        wg, wrg = wchunk("ssm_w_glu", j, 0, 96, 16, 1536 + 512 * c, 512)
                        for mi in range(4):
                            ba = nextps(); bg = nextps()
                            for kt in range(16):
                                P.op("tensor", lambda e, ba=ba, mi=mi, kt=kt, wa=wa: e.matmul(ps[ba][:], lhsT=wa[0:96, kt, mi * 128:(mi + 1) * 128], rhs=zT[0:96, kt, :], start=(kt == 0), stop=(kt == 15)),
                                     ["zT", wra], ["ps%d" % ba])
                            for kt in range(16):
                                P.op("tensor", lambda e, bg=bg, mi=mi, kt=kt, wg=wg: e.matmul(ps[bg][:], lhsT=wg[0:96, kt, mi * 128:(mi + 1) * 128], rhs=zT[0:96, kt, :], start=(kt == 0), stop=(kt == 15)),
                                     ["zT", wrg], ["ps%d" % bg])
                            P.op("scalar", lambda e, bg=bg: e.activation(out=sg_, in_=ps[bg][:], func=AF.Sigmoid), ["ps%d" % bg], ["sg"])
                            P.op("vector", lambda e, ba=ba, c=c, mi=mi: e.tensor_tensor(out=ymixT[:, 4 * c + mi, :], in0=ps[ba][:], in1=sg_, op=ALU.mult), ["ps%d" % ba, "sg"], ["ymixT"])
                else:
                    qT = HA.alloc([12, 512], BF16)
                    kT = HA.alloc([4, 768], BF16)
                    vv = HA.alloc([6, 512], BF16)
                    k0 = max(0, t * 512 - 128); k1 = min(L, t * 512 + 640)
                    ko = 128 - (t * 512 - k0)
                    dma("sync", qT, QT_d[s, :, :, tk].rearrange("f p t -> p f t"), ["QT_d"], ["qT", "hT0", "hT1", "hT2", "hT3"], "qT")
                    dma("sync", kT[:, :, ko:ko + (k1 - k0)], KT_d[s, :, :, k0:k1].rearrange("f p t -> p f t"), ["KT_d"], ["kT", "hT0", "hT1", "hT2", "hT3"], "kT")
                    dma("sync", vv[:, ko // 128: ko // 128 + (k1 - k0) // 128, :], V_d[s, k0:k1, :].rearrange("(u p) f -> p u f", p=128), ["V_d"], ["vv", "hT0", "hT1", "hT2", "hT3"], "vv")
                    sm = [HA.alloc([1, 384], F32)[:, 0, :] for _ in range(3)]
                    pp = [HA.alloc([1, 384], BF16)[:, 0, :] for _ in range(3)]
                    pT = [HA.alloc([3, 128], BF16) for _ in range(3)]
                    for kvh in range(4):
                        ybk = [nextps(3, 3) for _ in range(3)]
                        for qb in range(4):
                            gq = t * 4 + qb
                            kb0 = 0 if gq > 0 else 1
                            kb1 = 3 if gq < NB - 1 else 2
                            nk = (kb1 - kb0) * 128
                            kcol = qb * 128 + kb0 * 128
                            for hh in range(3):
                                h = kvh * 3 + hh
                                b = hh
                                so = 32 + 8 * hh
                                P.op("tensor", lambda e, b=b, h=h, qb=qb, kcol=kcol, nk=nk, kvh=kvh: e.matmul(ps[b][:, 0:nk], lhsT=qT[:, h, qb * 128:(qb + 1) * 128], rhs=kT[:, kvh, kcol:kcol + nk], start=True, stop=True),
                                     ["qT", "kT"], ["ps%d" % b])
                                P.op("vector", lambda e, b=b, hh=hh, nk=nk, kb0=kb0: e.tensor_tensor(out=sm[hh][:, 0:nk], in0=ps[b][:, 0:nk], in1=maskt[:, kb0 * 128:kb0 * 128 + nk], op=ALU.add), ["ps%d" % b, "mask"], ["sm%d" % hh])
                                P.op("vector", lambda e, hh=hh, nk=nk, so=so: e.reduce_max(out=stats[:, so:so + 1], in_=sm[hh][:, 0:nk], axis=mybir.AxisListType.X), ["sm%d" % hh], ["st%da" % hh])
                                P.op("vector", lambda e, so=so, h=h: e.scalar_tensor_tensor(out=stats[:, so + 1:so + 2], in0=stats[:, so:so + 1], scalar=float(HD ** -0.5), in1=sinkt[:, h:h + 1], op0=ALU.mult, op1=ALU.max), ["st%da" % hh, "sink"], ["st%db" % hh])
                                P.op("vector", lambda e, so=so: e.tensor_scalar(out=stats[:, so + 2:so + 3], in0=stats[:, so + 1:so + 2], scalar1=-1.0, scalar2=None, op0=ALU.mult), ["st%db" % hh], ["st%dc" % hh])
                                P.op("scalar", lambda e, hh=hh, nk=nk, so=so: e.activation(out=pp[hh][:, 0:nk], in_=sm[hh][:, 0:nk], func=AF.Exp, bias=stats[:, so + 2:so + 3], scale=float(HD ** -0.5), accum_out=stats[:, so + 3:so + 4]),
                                     ["sm%d" % hh, "st%dc" % hh], ["pp%d" % hh, "st%dd" % hh])
                                P.op("scalar", lambda e, so=so, h=h: e.activation(out=stats[:, so + 4:so + 5], in_=stats[:, so + 2:so + 3], func=AF.Exp, bias=sinkt[:, h:h + 1], scale=1.0), ["st%dc" % hh, "sink"], ["st%de" % hh])
                                P.op("vector", lambda e, so=so: e.tensor_tensor(out=stats[:, so + 5:so + 6], in0=stats[:, so + 3:so + 4], in1=stats[:, so + 4:so + 5], op=ALU.add), ["st%dd" % hh, "st%de" % hh], ["st%df" % hh])
                                P.op("vector", lambda e, so=so: e.reciprocal(out=stats[:, so + 6:so + 7], in_=stats[:, so + 5:so + 6]), ["st%df" % hh], ["st%dg" % hh])
                                P.op("vector", lambda e, hh=hh, nk=nk, so=so: e.tensor_scalar(out=pp[hh][:, 0:nk], in0=pp[hh][:, 0:nk], scalar1=stats[:, so + 6:so + 7], scalar2=None, op0=ALU.mult), ["pp%d" % hh, "st%dg" % hh], ["pp%d" % hh])
                                bt = 6 + hh % 2
                                pb = ps[bt][:].bitcast(BF16)
                                for kb in range(kb1 - kb0):
                                    P.op("tensor", lambda e, pb=pb, hh=hh, kb=kb: e.transpose(out=pb[:, (hh // 2) * 512 + kb * 128:(hh // 2) * 512 + kb * 128 + 128], in_=pp[hh][:, kb * 128:(kb + 1) * 128], identity=identb),
                                         ["pp%d" % hh, "identb"], ["ps%d" % bt])
                                nkb = kb1 - kb0
                                evac_copy(pT[hh][:, 0:nkb, :], pb[:, (hh // 2) * 512:(hh // 2) * 512 + nkb * 128].rearrange("p (k t) -> p k t", k=nkb), ["ps%d" % bt], ["pT%d" % hh])
                                yb_ = ybk[hh]
                                for kb in range(nkb):
                                    vb = qb + kb0 + kb
                                    P.op("tensor", lambda e, yb_=yb_, qb=qb, kb=kb, vb=vb, hh=hh, kvh=kvh, nkb=nkb: e.matmul(ps[yb_][:, qb * 128:(qb + 1) * 128], lhsT=vv[:, vb, kvh * 128:(kvh + 1) * 128], rhs=pT[hh][:, kb, :],
                                                                                                                        start=(kb == 0), stop=(kb == nkb - 1), skip_group_check=True), ["vv", "pT%d" % hh], ["ps%d" % yb_])
                        for hh in range(3):
                            evac_copy(ymixT[:, kvh * 3 + hh, :], ps[ybk[hh]][:], ["ps%d" % ybk[hh]], ["ymixT"])
                for qb in range(4):
                    for h in range(4):
                        b = h % 2
                        so = 32 + 8 * h
                        hn = "m%d" % h
                        P.op("tensor", lambda e, b=b, h=h, qb=qb: e.matmul(ps[b][:, 0:256], lhsT=qmT[:, h, qb * 128:(qb + 1) * 128], rhs=kmemT[:, h, :], start=True, stop=True), ["qmT", "kmemT"], ["ps%d" % b])
                        P.op("vector", lambda e, b=b, h=h: e.tensor_copy(out=smm[h], in_=ps[b][:, 0:256]), ["ps%d" % b], [hn + "s"])
                        P.op("vector", lambda e, h=h, so=so: e.reduce_max(out=stats[:, so:so + 1], in_=smm[h], axis=mybir.AxisListType.X), [hn + "s"], [hn + "a"])
                        P.op("vector", lambda e, so=so: e.tensor_scalar(out=stats[:, so + 2:so + 3], in0=stats[:, so:so + 1], scalar1=float(-(HD ** -0.5)), scalar2=None, op0=ALU.mult), [hn + "a"], [hn + "c"])
                        P.op("scalar", lambda e, h=h, so=so: e.activation(out=pm[h], in_=smm[h], func=AF.Exp, bias=stats[:, so + 2:so + 3], scale=float(HD ** -0.5), accum_out=stats[:, so + 3:so + 4]), [hn + "s", hn + "c"], [hn + "p", hn + "d"])
                        P.op("vector", lambda e, so=so: e.reciprocal(out=stats[:, so + 6:so + 7], in_=stats[:, so + 3:so + 4]), [hn + "d"], [hn + "g"])
                        P.op("vector", lambda e, h=h, so=so: e.tensor_scalar(out=pm[h], in0=pm[h], scalar1=stats[:, so + 6:so + 7], scalar2=None, op0=ALU.mult), [hn + "p", hn + "g"], [hn + "p"])
                        bt = 6 + h % 2
                        pb = ps[bt][:].bitcast(BF16)
                        for kb in range(2):
                            P.op("tensor", lambda e, pb=pb, h=h, kb=kb: e.transpose(out=pb[:, (h // 2) * 256 + kb * 128:(h // 2) * 256 + kb * 128 + 128], in_=pm[h][:, kb * 128:(kb + 1) * 128], identity=identb), [hn + "p", "identb"], ["ps%d" % bt])
                        evac_copy(pmT[h], pb[:, (h // 2) * 256:(h // 2) * 256 + 256].rearrange("p (k t) -> p k t", k=2), ["ps%d" % bt], [hn + "T"])
                        for kb in range(2):
                            P.op("tensor", lambda e, h=h, kb=kb, qb=qb: e.matmul(ps[2 + h][:, qb * 128:(qb + 1) * 128], lhsT=vmem[:, kb, h * 128:(h + 1) * 128], rhs=pmT[h][:, kb, :],
                                                                              start=(kb == 0), stop=(kb == 1), skip_group_check=True), [hn + "T", "vmem"], ["ps%d" % (2 + h)])
                for h in range(4):
                    evac_copy(ymemT[:, h, :], ps[2 + h][:], ["ps%d" % (2 + h)], ["ymemT"])
                dma("sync", xr, src[s, tk, :].rearrange("(u p) d -> p u d", p=128), [], ["xr0", "xr1", "xr2", "xr3"], "xr")
                dma("sync", lnp[:, 0, :], w["ln1_g"][li].partition_broadcast(128), [], ["lnp"], "lnp")
                dma("sync", lnp[:, 1, :], w["ln1_b"][li].partition_broadcast(128), [], ["lnp"], "lnp")
                for fc in range(4):
                    wc, wr_ = wchunk("w_out", li, 0, 128, 16, 512 * fc, 512)
                    for sub in range(4):
                        b = nextps(3, 0)
                        for kt in range(16):
                            lhs = ymixT[:, kt, sub * 128:(sub + 1) * 128] if kt < 12 else ymemT[:, kt - 12, sub * 128:(sub + 1) * 128]
                            P.op("tensor", lambda e, b=b, lhs=lhs, kt=kt, wc=wc: e.matmul(ps[b][:], lhsT=lhs, rhs=wc[:, kt, :], start=(kt == 0), stop=(kt == 15)), ["ymixT", "ymemT", wr_], ["ps%d" % b])
                        P.op("vector", lambda e, b=b, sub=sub, fc=fc: e.scalar_tensor_tensor(out=xr[:, sub, 512 * fc:512 * fc + 512], in0=xr[:, sub, 512 * fc:512 * fc + 512], scalar=ALPHA, in1=ps[b][:], op0=ALU.mult, op1=ALU.add),
                             ["ps%d" % b, "xr%d" % sub], ["xr%d" % sub])
                layer_norm_all()
                HA.reset(H_OFF + 49152)
                xb2 = HA.alloc([4, 2048], BF16)
                P.barrier()
                for sub in range(4):
                    eng = ("vector", "scalar", "gpsimd", "vector")[sub]
                    if eng == "scalar":
                        P.op(eng, lambda e, sub=sub, xb2=xb2: e.copy(out=xb2[:, sub, :], in_=xr[:, sub, :]), ["xr%d" % sub], ["xb%d" % sub])
                    else:
                        P.op(eng, lambda e, sub=sub, xb2=xb2: e.tensor_copy(out=xb2[:, sub, :], in_=xr[:, sub, :]), ["xr%d" % sub], ["xb%d" % sub])
                transposes_to_xT(xb2, ["xb0", "xb1", "xb2", "xb3"])
                dma("sync", lnp[:, 0, :], w["ln2_g"][li].partition_broadcast(128), [], ["lnp"], "lnp")
                dma("sync", lnp[:, 1, :], w["ln2_b"][li].partition_broadcast(128), [], ["lnp"], "lnp")
                P.barrier()
                rtmp = [rtmp0, rtmp1]
                for c in range(16):
                    wc, wr_ = wchunk("w_ff1", li, 0, 128, 16, 512 * c, 512)
                    for fi in range(4):
                        f = 4 * c + fi
                        b = 4 + f % 2
                        for kt in range(16):
                            P.op("tensor", lambda e, b=b, fi=fi, kt=kt, wc=wc: e.matmul(ps[b][:], lhsT=wc[:, kt, fi * 128:(fi + 1) * 128], rhs=xT[:, kt, :], start=(kt == 0), stop=(kt == 15)), ["xT", wr_], ["ps%d" % b])
                        r_ = rtmp[f % 2]
                        P.op("scalar", lambda e, b=b, r_=r_: e.activation(out=r_, in_=ps[b][:], func=AF.Relu), ["ps%d" % b], ["rt%d" % (f % 2)])
                        P.op("gpsimd", lambda e, f=f, r_=r_: e.tensor_tensor(out=hT[:, f, :], in0=r_, in1=r_, op=ALU.mult), ["rt%d" % (f % 2)], ["hT%d" % (f // 16)])
                for fc in range(4):
                    for c4 in range(4):
                        v = wb["w_ff2"][li, c4 * 2048:(c4 + 1) * 2048, 512 * fc:512 * fc + 512].rearrange("(k p) n -> p k n", p=128)
                        wc, wr_ = wget(v, 128, 16, 512)
                        for sub in range(4):
                            for kt in range(16):
                                P.op("tensor", lambda e, sub=sub, kt=kt, c4=c4, wc=wc: e.matmul(ps[sub][:], lhsT=hT[:, c4 * 16 + kt, sub * 128:(sub + 1) * 128], rhs=wc[:, kt, :],
                                                                                          start=(c4 == 0 and kt == 0), stop=(c4 == 3 and kt == 15), skip_group_check=True),
                                     ["hT%d" % c4, wr_], ["ps%d" % sub])
                    for sub in range(4):
                        P.op("vector", lambda e, sub=sub, fc=fc: e.scalar_tensor_tensor(out=xr[:, sub, 512 * fc:512 * fc + 512], in0=xr[:, sub, 512 * fc:512 * fc + 512], scalar=ALPHA, in1=ps[sub][:], op0=ALU.mult, op1=ALU.add),
                             ["ps%d" % sub, "xr%d" % sub], ["xr%d" % sub])
                layer_norm_all()
                dma("gpsimd", dst[s, tk, :].rearrange("(u p) d -> p u d", p=128), xr, ["xr0", "xr1", "xr2", "xr3"], ["out"], "xout")
    P.barrier()
    P.op("sync", None, reads=["out"])
    P.run_block()
    st.close()
    return nc


def _consts(L):
    inv = (10000.0 ** (-np.arange(0, 128, 2, dtype=np.float32) / np.float32(128))).astype(np.float32)
    ang = (np.arange(L, dtype=np.float32)[:, None] * inv[None, :]).astype(np.float32)
    i = np.arange(128)[:, None]
    jj = np.arange(384)[None, :]
    mask = np.where(np.abs(jj - 128 - i) <= 128, 0.0, -1e30).astype(np.float32)
    msk2 = np.zeros((128, 2), np.float32)
    msk2[:64, 0] = 1
    msk2[64:, 1] = 1
    return {"c_ident": np.eye(128, dtype=np.float32), "c_cos": np.cos(ang).astype(np.float32), "c_sin": np.sin(ang).astype(np.float32),
            "c_mask": mask, "c_msk2": msk2, "c_ev": np.tile(np.array(EV, np.float32)[None, :], (128, 1))}


def kernel(**inputs):
    L, NS, DEPTH = 4096, 2, 4
    xp = np.asarray(inputs["x_prompt"], np.float32)
    xs = np.asarray(inputs["x_sample"], np.float32)
    mp = np.asarray(inputs["mem_prompt"], np.float32)
    ms = np.asarray(inputs["mem_sample"], np.float32)
    nc = build(L, NS, DEPTH)
    cs = _consts(L)
    wts = {k: np.ascontiguousarray(np.asarray(inputs[k], np.float32)) for k in WSHAPES}
    in_maps = []
    for c in range(8):
        x1 = xs[c] if c < 4 else xp[c]
        m1 = ms[c] if c < 4 else mp[c]
        m = {"x": np.ascontiguousarray(np.stack([xp[c], x1])), "mem": np.ascontiguousarray(np.stack([mp[c], m1]))}
        m.update(wts)
        m.update(cs)
        in_maps.append(m)
    res = run_bass_kernel_spmd(nc, in_maps, core_ids=list(range(8)))
    yp = np.stack([np.asarray(res.results[c]["y"][0], np.float32) for c in range(8)])
    ysm = np.stack([np.asarray(res.results[c]["y"][1], np.float32) for c in range(4)])
    return (yp, ysm)
```

```python
import contextlib
import numpy as np
import concourse.bass as bass
import concourse.mybir as mybir
from concourse.bass_utils import run_bass_kernel_spmd

F32 = mybir.dt.float32
BF16 = mybir.dt.bfloat16
I32 = mybir.dt.int32
AF = mybir.ActivationFunctionType
ALU = mybir.AluOpType

D = 2048
MIX = 1536
HD = 128
DFF = 8192
TC = 32
ALPHA = float(8 ** 0.25)
LN_EPS = 1e-5
EV = list(range(33)) + [64, 128, 256, 512, 1024, 2048]
NE = len(EV)
ENGINES = ("sync", "scalar", "vector", "gpsimd", "tensor")

WSHAPES = {
    "ssm_w_in": [2, 2048, 2048], "ssm_lam_re": [2, 2, 96, 64], "ssm_lam_im": [2, 2, 96, 64],
    "ssm_log_dt": [2, 2, 96], "ssm_b_re": [2, 2, 96, 64, 16], "ssm_b_im": [2, 2, 96, 64, 16],
    "ssm_c_re": [2, 2, 96, 16, 64], "ssm_c_im": [2, 2, 96, 16, 64], "ssm_d": [2, 1536],
    "ssm_w_glu": [2, 1536, 3072], "attn_w_in": [2, 2048, 3072], "attn_sink": [2, 12],
    "w_mem_kv": [4, 2048, 1024], "w_out": [4, 2048, 2048], "ln1_g": [4, 2048], "ln1_b": [4, 2048],
    "w_ff1": [4, 2048, 8192], "w_ff2": [4, 8192, 2048], "ln2_g": [4, 2048], "ln2_b": [4, 2048],
}
BIGW = ["ssm_w_in", "ssm_w_glu", "attn_w_in", "w_mem_kv", "w_out", "w_ff1", "w_ff2"]


class Prog:
    def __init__(self, nc):
        self.nc = nc
        self.ops = []
        self.default_skip = ()

    def op(self, eng, fn, reads=(), writes=(), dma=None):
        self.ops.append((eng, fn, tuple(reads), tuple(writes), dma))

    def barrier(self, skip=()):
        self.ops.append(("BAR", None, tuple(skip) + tuple(self.default_skip), (), None))

    def emit(self):
        ops = self.ops
        n = len(ops)
        last_w, readers = {}, {}
        deps = [None] * n
        last_eng = {}
        dma_since = []
        pending_bar = {}
        for i, (eng, fn, rd, wr, dma) in enumerate(ops):
            if eng == "BAR":
                keepd = [x for x in dma_since if ops[x][4] in rd]
                bd = set(last_eng.values()) | set(x for x in dma_since if ops[x][4] not in rd)
                dma_since = keepd
                pending_bar = {e: bd for e in ENGINES}
                deps[i] = set()
                continue
            d = set()
            for r in rd:
                if r in last_w:
                    d.add(last_w[r])
            for w in wr:
                if w in last_w:
                    d.add(last_w[w])
                for x in readers.get(w, ()):
                    d.add(x)
            if eng in pending_bar:
                d |= pending_bar.pop(eng)
            d.discard(i)
            deps[i] = d
            for w in wr:
                last_w[w] = i
                readers[w] = []
            for r in rd:
                if r not in wr:
                    readers.setdefault(r, []).append(i)
            if dma is None:
                if fn is not None:
                    last_eng[eng] = i
            else:
                dma_since.append(i)
        need_sig = [False] * n
        fdeps = [()] * n
        for i, (eng, fn, rd, wr, dma) in enumerate(ops):
            if eng == "BAR":
                continue
            keep = []
            srd = set(rd)
            for j in deps[i]:
                ej, fj, rdj, wrj, dmaj = ops[j]
                if fj is None:
                    continue
                if dmaj is None and dma is None and ej == eng:
                    if eng == "tensor":
                        continue
                    if not (set(wrj) & srd):
                        continue
                keep.append(j)
            fdeps[i] = keep
            for j in keep:
                need_sig[j] = True
        counts, sig, semkeys = {}, [None] * n, {}
        for i, (eng, fn, rd, wr, dma) in enumerate(ops):
            if not need_sig[i]:
                continue
            key = ("dma", dma) if dma is not None else ("eng", eng)
            inc = 16 if dma is not None else 1
            counts[key] = counts.get(key, 0) + inc
            sig[i] = (key, counts[key], inc)
            semkeys[key] = None
        waited = {e: {} for e in ENGINES}
        streams = {e: [] for e in ENGINES}
        for i, (eng, fn, rd, wr, dma) in enumerate(ops):
            if eng == "BAR":
                continue
            ws = {}
            for j in fdeps[i]:
                key, cnt, _ = sig[j]
                if waited[eng].get(key, 0) >= cnt:
                    continue
                ws[key] = max(ws.get(key, 0), cnt)
            for key, cnt in ws.items():
                waited[eng][key] = cnt
            streams[eng].append((tuple(ws.items()), fn, sig[i]))
        self.semkeys = list(semkeys.keys())
        self.counts = counts
        return streams

    def run_block(self):
        nc = self.nc
        streams = self.emit()
        with contextlib.ExitStack() as st:
            sems = {}
            for n_, k in enumerate(self.semkeys):
                sems[k] = st.enter_context(nc.semaphore("s%d" % n_))
            block = st.enter_context(nc.Block())

            def mk(ename):
                def body(eng):
                    for ws, fn, sg in streams[ename]:
                        for key, cnt in ws:
                            eng.wait_ge(sems[key], cnt)
                        if fn is None:
                            continue
                        ins = fn(eng)
                        if sg is not None:
                            ins.then_inc(sems[sg[0]], sg[2])
                return body

            for ename in ENGINES:
                if streams[ename]:
                    getattr(block, ename)(mk(ename))


class Arena:
    def __init__(self, ap, nbytes):
        self.ap, self.n, self.off = ap, nbytes, 0

    def reset(self, off=0):
        self.off = off

    def alloc(self, shape, dt, parts=128):
        ne = int(np.prod(shape))
        nb = ne * (2 if dt == BF16 else 4)
        a = self.ap[0:parts, self.off // 2:(self.off + nb) // 2]
        self.off += (nb + 63) // 64 * 64
        assert self.off <= self.n, (self.off, self.n)
        if dt != BF16:
            a = a.bitcast(dt)
        if len(shape) == 2:
            return a.rearrange("p (a b) -> p a b", a=shape[0])
        if len(shape) == 3:
            return a.rearrange("p (a b c) -> p a b c", a=shape[0], b=shape[1])
        if len(shape) == 4:
            return a.rearrange("p (a b c d) -> p a b c d", a=shape[0], b=shape[1], c=shape[2])
        if len(shape) == 5:
            return a.rearrange("p (a b c d e) -> p a b c d e", a=shape[0], b=shape[1], c=shape[2], d=shape[3])
        return a


def build(L, NS, DEPTH, dbg=False, stop=None):
    nc = bass.Bass("TRN2", target_bir_lowering=False)
    NT = L // 512
    NB = L // 128
    NCH = L // TC
    NSTEP = int(np.log2(NCH))
    assert 2 ** NSTEP == NCH

    def din(name, shape, dt=F32):
        return nc.dram_tensor(name, list(shape), dt, kind="ExternalInput").ap()

    def scr(name, shape, dt):
        return nc.dram_tensor(name, list(shape), dt, kind="Internal").ap()

    x_in = din("x", [NS, L, D])
    mem_in = din("mem", [NS, 256, D])
    w = {k: din(k, v) for k, v in WSHAPES.items()}
    c_ident = din("c_ident", [128, 128])
    c_cos = din("c_cos", [L, 64])
    c_sin = din("c_sin", [L, 64])
    c_mask = din("c_mask", [128, 384])
    c_msk2 = din("c_msk2", [128, 2])
    c_ev = din("c_ev", [128, NE])
    y_out = nc.dram_tensor("y", [NS, L, D], F32, kind="ExternalOutput").ap()

    wb = {k: scr("wb_" + k, WSHAPES[k], BF16) for k in BIGW}
    xbuf = scr("xbuf", [NS, L, D], F32)
    U_d = scr("U_d", [NS, 16, 96, L], BF16)
    Z_d = scr("Z_d", [NS, 16, 96, L], BF16)
    QM_d = scr("QM_d", [NS, 4, 128, L], BF16)
    QT_d = scr("QT_d", [NS, 12, 128, L], BF16)
    KT_d = scr("KT_d", [NS, 4, 128, L], BF16)
    V_d = scr("V_d", [NS, L, 512], BF16)
    Kd = scr("Kd", [2, 16, 96, 2 * 32 * 96], BF16)
    Qd = scr("Qd", [2, 16, 96, 2 * 32 * 2 * 128], BF16)
    Pd = scr("Pd", [2, 16, 128, 3 * 2 * 32 * 2 * 32], BF16)
    SC_d = scr("SC_d", [2, 2, 128, 2, 7, 48], F32)

    st = contextlib.ExitStack()
    ARENA_BYTES = 206 * 1024
    arena_t = st.enter_context(nc.sbuf_tensor("arena", [128, ARENA_BYTES // 2], BF16))
    A = Arena(arena_t, ARENA_BYTES)
    ps = [st.enter_context(nc.psum_tensor("ps%d" % i, [128, 512], F32)) for i in range(8)]
    P = Prog(nc)
    TWO_PI = float(2 * np.pi)

    def finish():
        P.default_skip = ()
        P.barrier()
        P.op("sync", None, reads=["out"])
        P.run_block()
        st.close()
        return nc

    def dma(eng, out, in_, reads, writes, key, slow=False):
        if slow:
            P.op(eng, lambda e: e.dma_start(out=out, in_=in_, allow_slow_non_contiguous=True), reads, writes, dma=key)
        else:
            P.op(eng, lambda e: e.dma_start(out=out, in_=in_), reads, writes, dma=key)

    psi = [0]

    def nextps(nb=6, base=0):
        i = base + psi[0] % nb
        psi[0] += 1
        return i

    PERS_OFF = 192 * 1024
    o = PERS_OFF // 2
    ident = arena_t[:, o:o + 256].bitcast(F32); o += 256
    identb = arena_t[:, o:o + 128]; o += 128
    maskt = arena_t[:, o:o + 768].bitcast(F32); o += 768
    msk2 = arena_t[:, o:o + 32].bitcast(F32)[:, 0:2]; o += 32
    kmemT = arena_t[:, o:o + 1024].rearrange("p (h t) -> p h t", h=4); o += 1024
    vmem = arena_t[:, o:o + 1024].rearrange("p (m f) -> p m f", m=2); o += 1024
    sinkt = arena_t[:, o:o + 32].bitcast(F32)[:, 0:12]; o += 32
    cst = arena_t[:, o:o + 1024].bitcast(F32).rearrange("p (a s f) -> p a s f", a=2, s=4); o += 1024
    stats = arena_t[:, o:o + 128].bitcast(F32); o += 128
    smalls = arena_t[:, o:o + 512].bitcast(F32); o += 512
    rtmp0 = arena_t[:, o:o + 512]; o += 512
    rtmp1 = arena_t[:, o:o + 512]; o += 512
    assert o * 2 <= ARENA_BYTES

    dma("sync", ident, c_ident, [], ["ident"], "c0")
    dma("sync", maskt, c_mask, [], ["mask"], "c1")
    dma("sync", msk2, c_msk2, [], ["msk2"], "c2")
    P.op("vector", lambda e: e.tensor_copy(out=identb, in_=ident), ["ident"], ["identb"])

    def use_layer(name, l):
        if name in ("ssm_w_in", "ssm_w_glu"):
            return 2 * l
        if name == "attn_w_in":
            return 2 * l + 1
        return l

    for ul in range(4):
        for name in BIGW:
            shp = WSHAPES[name]
            rows_per = max(1, (1 << 20) // shp[2])
            for l in range(shp[0]):
                if use_layer(name, l) != ul:
                    continue
                for r0 in range(0, shp[1], rows_per):
                    r1 = min(shp[1], r0 + rows_per)
                    dma("gpsimd", wb[name][l, r0:r1, :], w[name][l, r0:r1, :], [], ["wb:%s:%d" % (name, l)], "cast%d" % ul)
    P.default_skip = ("cast0", "cast1", "cast2", "cast3")
    if stop == 'cast':
        return finish()

    def rr_sin(out, arg, tmp_r, tmp_i, tmp_f, shift, rd, wrn):
        P.op("vector", lambda e: e.tensor_scalar(out=tmp_r, in0=arg, scalar1=float(1.0 / TWO_PI), scalar2=float(shift),
                                                 op0=ALU.mult, op1=ALU.add), rd, [wrn + "r"])
        P.op("vector", lambda e: e.tensor_copy(out=tmp_i, in_=tmp_r), [wrn + "r"], [wrn + "i"])
        P.op("vector", lambda e: e.tensor_copy(out=tmp_f, in_=tmp_i), [wrn + "i"], [wrn + "f"])
        P.op("vector", lambda e: e.tensor_tensor(out=tmp_r, in0=tmp_r, in1=tmp_f, op=ALU.subtract),
             [wrn + "r", wrn + "f"], [wrn + "r"])
        P.op("scalar", lambda e: e.activation(out=out, in_=tmp_r, func=AF.Sin, scale=TWO_PI), [wrn + "r"], [wrn])

    n_ssm = (DEPTH + 1) // 2
    for j in range(n_ssm):
        for d in range(2):
            A.reset(0)

            def a2(n, dt=F32):
                t = A.alloc([1, n], dt)
                return t[:, 0, :]
            lamre = a2(48); lamim = a2(48); dtt = a2(48); evt = a2(NE)
            aa = a2(48); th = a2(48)
            BR = A.alloc([48, 16], F32); BI = A.alloc([48, 16], F32)
            CR = A.alloc([48, 16], F32); CI = A.alloc([48, 16], F32)
            cn = A.alloc([1, 128], F32)[:, 0, :]
            LR = A.alloc([NE, 48], F32); LI = A.alloc([NE, 48], F32); MG = A.alloc([NE, 48], F32)
            T1 = A.alloc([NE, 48], F32); T2i = A.alloc([NE, 48], I32); T3 = A.alloc([NE, 48], F32)
            crr = a2(48); cii = a2(48); s1 = a2(48); s2 = a2(48); s3 = a2(48)
            BBR = A.alloc([48, 16], F32); BBI = A.alloc([48, 16], F32)
            dcol = a2(16)
            pre = "pc%d%d" % (j, d)
            flat = lambda ap_: ap_.rearrange("a b -> (a b)")
            dma("sync", lamre, flat(w["ssm_lam_re"][j, d]).rearrange("(g q) -> q g", q=128), [], [pre + "lamre"], "p0", slow=True)
            dma("sync", lamim, flat(w["ssm_lam_im"][j, d]).rearrange("(g q) -> q g", q=128), [], [pre + "lamim"], "p1", slow=True)
            ldt2 = w["ssm_log_dt"][j, d].rearrange("(g t) -> t g", t=2)
            for g2 in range(2):
                dma("sync", dtt[64 * g2:64 * g2 + 64, :], ldt2[g2:g2 + 1, :].to_broadcast([64, 48]), [], [pre + "dt%d" % g2], "p2%d" % g2, slow=True)
            dma("sync", evt, c_ev, [], [pre + "ev"], "p3")
            dma("sync", BR, w["ssm_b_re"][j, d].rearrange("g p c -> (g p c)").rearrange("(g q c) -> q g c", q=128, c=16), [], [pre + "BR"], "p4", slow=True)
            dma("sync", BI, w["ssm_b_im"][j, d].rearrange("g p c -> (g p c)").rearrange("(g q c) -> q g c", q=128, c=16), [], [pre + "BI"], "p5", slow=True)
            dma("sync", dcol[0:96, :], w["ssm_d"][j].rearrange("(f p) -> p f", p=96), [], [pre + "dcol"], "p6", slow=True)
            for ri, (src, dst) in enumerate(((w["ssm_c_re"], CR), (w["ssm_c_im"], CI))):
                cflat = src[j, d].rearrange("g c p -> (g c) p")
                for i in range(12):
                    dma("sync", cn[:, 0:64], cflat[128 * i:128 * i + 128, :], [], [pre + "cn"], "p7")
                    dma("sync", cn[:, 64:128], cflat[128 * i:128 * i + 128, :], [], [pre + "cn"], "p7")
                    b = nextps()
                    P.op("tensor", lambda e, b=b: e.transpose(out=ps[b][:, 0:128], in_=cn, identity=ident), [pre + "cn", "ident"], ["ps%d" % b])
                    tv = ps[b][:, 0:128].rearrange("q (l t c) -> q l t c", l=4, t=2)
                    P.op("vector", lambda e, tv=tv, dst=dst, i=i: e.tensor_copy(out=dst[0:64, 4 * i:4 * i + 4, :], in_=tv[0:64, :, 0, :]), ["ps%d" % b], [pre + "C%d" % ri])
                    P.op("scalar", lambda e, tv=tv, dst=dst, i=i: e.copy(out=dst[64:128, 4 * i:4 * i + 4, :], in_=tv[64:128, :, 1, :]), ["ps%d" % b], [pre + "C%d" % ri])
            P.op("scalar", lambda e: e.activation(out=dtt, in_=dtt, func=AF.Exp), [pre + "dt0", pre + "dt1"], [pre + "dt"])
            P.op("vector", lambda e: e.tensor_tensor(out=aa, in0=dtt, in1=lamre, op=ALU.mult), [pre + "dt", pre + "lamre"], [pre + "aa"])
            P.op("vector", lambda e: e.tensor_tensor(out=th, in0=dtt, in1=lamim, op=ALU.mult), [pre + "dt", pre + "lamim"], [pre + "th"])
            bc_g = lambda t: t.unsqueeze(1).to_broadcast([128, NE, 48])
            bc_e = lambda t: t.unsqueeze(2).to_broadcast([128, NE, 48])
            P.op("vector", lambda e: e.tensor_tensor(out=T1, in0=bc_g(th), in1=bc_e(evt), op=ALU.mult), [pre + "th", pre + "ev"], [pre + "ARG"])
            rr_sin(LI, T1, T3, T2i, MG, 0.0, [pre + "ARG"], pre + "LI")
            rr_sin(LR, T1, T3, T2i, MG, 0.25, [pre + "ARG", pre + "LI"], pre + "LR")
            P.op("vector", lambda e: e.tensor_tensor(out=T1, in0=bc_g(aa), in1=bc_e(evt), op=ALU.mult), [pre + "aa", pre + "ev", pre + "LR", pre + "LI", pre + "ARG"], [pre + "ARG"])
            P.op("scalar", lambda e: e.activation(out=MG, in_=T1, func=AF.Exp), [pre + "ARG", pre + "LRf", pre + "LIf", pre + "LR", pre + "LI"], [pre + "MG"])
            P.op("vector", lambda e: e.tensor_tensor(out=LR, in0=LR, in1=MG, op=ALU.mult), [pre + "LR", pre + "MG"], [pre + "LR"])
            P.op("vector", lambda e: e.tensor_tensor(out=LI, in0=LI, in1=MG, op=ALU.mult), [pre + "LI", pre + "MG"], [pre + "LI"])
            L1R = LR[:, 1, :]; L1I = LI[:, 1, :]
            V = "vector"
            P.op(V, lambda e: e.tensor_scalar(out=s1, in0=L1R, scalar1=-1.0, scalar2=None, op0=ALU.add), [pre + "LR"], [pre + "s1"])
            P.op(V, lambda e: e.tensor_tensor(out=s2, in0=lamre, in1=lamre, op=ALU.mult), [pre + "lamre"], [pre + "s2"])
            P.op(V, lambda e: e.tensor_tensor(out=s3, in0=lamim, in1=lamim, op=ALU.mult), [pre + "lamim"], [pre + "s3"])
            P.op(V, lambda e: e.tensor_tensor(out=s2, in0=s2, in1=s3, op=ALU.add), [pre + "s2", pre + "s3"], [pre + "s2"])
            P.op(V, lambda e: e.reciprocal(out=s2, in_=s2), [pre + "s2"], [pre + "s2"])
            P.op(V, lambda e: e.tensor_tensor(out=crr, in0=s1, in1=lamre, op=ALU.mult), [pre + "s1", pre + "lamre"], [pre + "crr"])
            P.op(V, lambda e: e.tensor_tensor(out=s3, in0=L1I, in1=lamim, op=ALU.mult), [pre + "LI", pre + "lamim", pre + "s2"], [pre + "s3"])
            P.op(V, lambda e: e.tensor_tensor(out=crr, in0=crr, in1=s3, op=ALU.add), [pre + "crr", pre + "s3"], [pre + "crr"])
            P.op(V, lambda e: e.tensor_tensor(out=crr, in0=crr, in1=s2, op=ALU.mult), [pre + "crr", pre + "s2"], [pre + "crr"])
            P.op(V, lambda e: e.tensor_tensor(out=cii, in0=L1I, in1=lamre, op=ALU.mult), [pre + "LI", pre + "lamre"], [pre + "cii"])
            P.op(V, lambda e: e.tensor_tensor(out=s3, in0=s1, in1=lamim, op=ALU.mult), [pre + "s1", pre + "lamim", pre + "crr"], [pre + "s3"])
            P.op(V, lambda e: e.tensor_tensor(out=cii, in0=cii, in1=s3, op=ALU.subtract), [pre + "cii", pre + "s3"], [pre + "cii"])
            P.op(V, lambda e: e.tensor_tensor(out=cii, in0=cii, in1=s2, op=ALU.mult), [pre + "cii", pre + "s2"], [pre + "cii"])
            bc_c = lambda t: t.unsqueeze(2).to_broadcast([128, 48, 16])
            TB = A.alloc([48, 16], F32)
            P.op(V, lambda e: e.tensor_tensor(out=BBR, in0=BR, in1=bc_c(crr), op=ALU.mult), [pre + "BR", pre + "crr"], [pre + "BBR"])
            P.op(V, lambda e: e.tensor_tensor(out=TB, in0=BI, in1=bc_c(cii), op=ALU.mult), [pre + "BI", pre + "cii"], [pre + "TB"])
            P.op(V, lambda e: e.tensor_tensor(out=BBR, in0=BBR, in1=TB, op=ALU.subtract), [pre + "BBR", pre + "TB"], [pre + "BBR"])
            P.op(V, lambda e: e.tensor_tensor(out=BBI, in0=BI, in1=bc_c(crr), op=ALU.mult), [pre + "BI", pre + "crr"], [pre + "BBI"])
            P.op(V, lambda e: e.tensor_tensor(out=TB, in0=BR, in1=bc_c(cii), op=ALU.mult), [pre + "BR", pre + "cii", pre + "BBR"], [pre + "TB"])
            P.op(V, lambda e: e.tensor_tensor(out=BBI, in0=BBI, in1=TB, op=ALU.add), [pre + "BBI", pre + "TB"], [pre + "BBI"])
            base_off = A.off
            for ft in range(16):
                A.reset(base_off)
                g0 = 3 * ft
                XR = A.alloc([32, 3, 16], F32); XI = A.alloc([32, 3, 16], F32); XT = A.alloc([32, 3, 16], F32)
                DQ = [A.alloc([32, 96], F32), A.alloc([32, 96], F32)]
                VR = A.alloc([33, 3, 16], F32); VI = A.alloc([33, 3, 16], F32); VT = A.alloc([33, 3, 16], F32)
                EE = [A.alloc([33, 3, 32], F32), A.alloc([33, 3, 32], F32)]
                Qt = A.alloc([32, 2, 128], BF16)
                Pt = A.alloc([3, 32, 2, 32], BF16)
                Kt = A.alloc([32, 96], BF16)
                K0 = A.alloc([1, 96], F32)[:, 0, :]
                fp = pre + "f"
                lrb = lambda t, ne, g0=g0: t[:, 0:ne, g0:g0 + 3].unsqueeze(3).to_broadcast([128, ne, 3, 16])
                bb = lambda t, ne, g0=g0: t[:, g0:g0 + 3, :].unsqueeze(1).to_broadcast([128, ne, 3, 16])
                P.op(V, lambda e, XR=XR, lrb=lrb, bb=bb: e.tensor_tensor(out=XR, in0=lrb(LR, 32), in1=bb(BBR, 32), op=ALU.mult), [pre + "LR", pre + "BBR"], [fp + "XR"])
                P.op("gpsimd", lambda e, XT=XT, lrb=lrb, bb=bb: e.tensor_tensor(out=XT, in0=lrb(LI, 32), in1=bb(BBI, 32), op=ALU.mult), [pre + "LI", pre + "BBI"], [fp + "XT"])
                P.op(V, lambda e, XR=XR, XT=XT: e.tensor_tensor(out=XR, in0=XR, in1=XT, op=ALU.subtract), [fp + "XR", fp + "XT"], [fp + "XR"])
                P.op(V, lambda e, XI=XI, lrb=lrb, bb=bb: e.tensor_tensor(out=XI, in0=lrb(LR, 32), in1=bb(BBI, 32), op=ALU.mult), [pre + "LR", pre + "BBI"], [fp + "XI"])
                P.op("gpsimd", lambda e, XT=XT, lrb=lrb, bb=bb: e.tensor_tensor(out=XT, in0=lrb(LI, 32), in1=bb(BBR, 32), op=ALU.mult), [pre + "LI", pre + "BBR", fp + "XR"], [fp + "XT"])
                P.op(V, lambda e, XI=XI, XT=XT: e.tensor_tensor(out=XI, in0=XI, in1=XT, op=ALU.add), [fp + "XI", fp + "XT"], [fp + "XI"])
                m2 = lambda n: msk2.unsqueeze(1).unsqueeze(3).to_broadcast([128, n, 2, 16])
                for c, Xc in enumerate((XR, XI)):
                    xin = Xc.rearrange("q e g c -> q (e g) c").unsqueeze(2).to_broadcast([128, 96, 2, 16])
                    dqo = DQ[c].rearrange("q e (g t c) -> q (e g) t c", g=3, t=2)
                    P.op(V if c == 0 else "gpsimd", lambda e, dqo=dqo, xin=xin, m2=m2: e.tensor_tensor(out=dqo, in0=xin, in1=m2(96), op=ALU.mult),
                         [fp + ("XR" if c == 0 else "XI"), "msk2"], [fp + "DQ%d" % c])
                for e2 in range(0, 32, 2):
                    b = nextps()
                    for ee in range(2):
                        for c in range(2):
                            sl = (ee * 2 + c) * 128
                            P.op("tensor", lambda e, b=b, sl=sl, c=c, e_=e2 + ee, DQ=DQ: e.transpose(out=ps[b][0:96, sl:sl + 128], in_=DQ[c][:, e_, :], identity=ident),
                                 [fp + "DQ%d" % c, "ident"], ["ps%d" % b])
                    pv = ps[b][0:96, :].rearrange("q (a c n) -> q a c n", a=2, c=2)
                    P.op(V if (e2 // 2) % 2 == 0 else "scalar",
                         (lambda e, pv=pv, e2=e2, Qt=Qt: e.tensor_copy(out=Qt[0:96, e2:e2 + 2, :, :], in_=pv)) if (e2 // 2) % 2 == 0 else
                         (lambda e, pv=pv, e2=e2, Qt=Qt: e.copy(out=Qt[0:96, e2:e2 + 2, :, :], in_=pv)),
                         ["ps%d" % b], [fp + "Qt"])
                dma("gpsimd", Qd[j, ft, :, d * 8192:(d + 1) * 8192], Qt[0:96].rearrange("q e c n -> q (e c n)"), [fp + "Qt"], ["Qd"], "pq")
                cb = lambda t, ne, g0=g0: t[:, g0:g0 + 3, :].unsqueeze(1).to_broadcast([128, ne, 3, 16])
                P.op(V, lambda e, VR=VR, lrb=lrb, cb=cb: e.tensor_tensor(out=VR, in0=lrb(LR, 33), in1=cb(CR, 33), op=ALU.mult), [pre + "LR", pre + "C0"], [fp + "VR"])
                P.op("gpsimd", lambda e, VT=VT, lrb=lrb, cb=cb: e.tensor_tensor(out=VT, in0=lrb(LI, 33), in1=cb(CI, 33), op=ALU.mult), [pre + "LI", pre + "C1"], [fp + "VT"])
                P.op(V, lambda e, VR=VR, VT=VT: e.tensor_tensor(out=VR, in0=VR, in1=VT, op=ALU.subtract), [fp + "VR", fp + "VT"], [fp + "VR"])
                P.op(V, lambda e, VI=VI, lrb=lrb, cb=cb: e.tensor_tensor(out=VI, in0=lrb(LR, 33), in1=cb(CI, 33), op=ALU.mult), [pre + "LR", pre + "C1"], [fp + "VI"])
                P.op("gpsimd", lambda e, VT=VT, lrb=lrb, cb=cb: e.tensor_tensor(out=VT, in0=lrb(LI, 33), in1=cb(CR, 33), op=ALU.mult), [pre + "LI", pre + "C0", fp + "VR"], [fp + "VT"])
                P.op(V, lambda e, VI=VI, VT=VT: e.scalar_tensor_tensor(out=VI, in0=VI, scalar=-1.0, in1=VT, op0=ALU.mult, op1=ALU.subtract), [fp + "VI", fp + "VT"], [fp + "VI"])
                for c, Vc in enumerate((VR, VI)):
                    vin = Vc.rearrange("q e g c -> q (e g) c").unsqueeze(2).to_broadcast([128, 99, 2, 16])
                    eo = EE[c].rearrange("q e g (t c) -> q (e g) t c", t=2)
                    P.op(V if c == 0 else "gpsimd", lambda e, eo=eo, vin=vin, m2=m2: e.tensor_tensor(out=eo, in0=vin, in1=m2(99), op=ALU.mult),
                         [fp + ("VR" if c == 0 else "VI"), "msk2"], [fp + "EE%d" % c])
                    for gl in range(3):
                        P.op("scalar", lambda e, gl=gl, c=c, Pt=Pt, EE=EE: e.copy(out=Pt[:, gl, :, c, :], in_=EE[c][:, 1:33, gl, :]), [fp + "EE%d" % c], [fp + "Pt"])
                dma("gpsimd", Pd[j, ft].rearrange("q (g d r) -> q g d r", g=3, d=2)[:, :, d, :], Pt.rearrange("q g e c n -> q g (e c n)"), [fp + "Pt"], ["Pd"], "pp")
                for bk in range(8):
                    b = nextps()
                    P.op(V, lambda e, b=b: e.memset(ps[b][0:96, 0:384], 0.0), [], ["ps%d" % b])
                    pk = ps[b][0:96, 0:384].rearrange("q (l n) -> q l n", l=4)
                    for gl in range(3):
                        for c in range(2):
                            P.op("tensor", lambda e, gl=gl, c=c, bk=bk, pk=pk, DQ=DQ, EE=EE: e.matmul(
                                pk[32 * gl:32 * gl + 32, :, 32 * gl:32 * gl + 32], lhsT=DQ[c][:, 0, 32 * gl:32 * gl + 32],
                                rhs=EE[c][:, 4 * bk:4 * bk + 4, gl, :], start=False, stop=(gl == 2 and c == 1), skip_group_check=True),
                                [fp + "DQ%d" % c, fp + "EE%d" % c, "ps%d" % b], ["ps%d" % b])
                    P.op(V, lambda e, pk=pk, bk=bk, Kt=Kt: e.tensor_copy(out=Kt[0:96, 4 * bk:4 * bk + 4, :], in_=pk), ["ps%d" % b], [fp + "Kt"])
                    if bk == 0 and d == 0:
                        P.op(V, lambda e, pk=pk, K0=K0, ft=ft: e.scalar_tensor_tensor(out=K0[0:96, :], in0=ident[0:96, 0:96], scalar=dcol[0:96, ft:ft + 1], in1=pk[:, 0, :],
                                                                                op0=ALU.mult, op1=ALU.add), ["ps%d" % b, "ident", pre + "dcol"], [fp + "K0"])
                        P.op(V, lambda e, K0=K0, Kt=Kt: e.tensor_copy(out=Kt[0:96, 0, :], in_=K0[0:96, :]), [fp + "K0", fp + "Kt"], [fp + "Kt"])
                dma("gpsimd", Kd[j, ft, :, d * 3072:(d + 1) * 3072], Kt[0:96].rearrange("q l n -> q (l n)"), [fp + "Kt"], ["Kd"], "pk")
            dma("gpsimd", SC_d[j, d, :, 0, :, :], LR[:, 32:32 + 7, :], [pre + "LR"], ["SC_d"], "psc0")
            dma("gpsimd", SC_d[j, d, :, 1, :, :], LI[:, 32:32 + 7, :], [pre + "LI"], ["SC_d"], "psc1")
            P.barrier(skip=("cast",))
    if stop == 'pre':
        return finish()
    RING = 4
    ring_i = [0]

    def wget(src3, rows, nkt, ncols, wres="wb"):
        slot = ring_i[0] % RING
        ring_i[0] += 1
        dst = arena_t[0:rows, slot * 8192: slot * 8192 + nkt * ncols].rearrange("p (k n) -> p k n", k=nkt)
        dma("sync", dst, src3, [wres], ["ring%d" % slot], "ring%d" % slot)
        return dst, "ring%d" % slot

    def wchunk(name, l, r0, nrows_p, nkt, c0, ncols):
        v = wb[name][l, r0:r0 + nrows_p * nkt, c0:c0 + ncols].rearrange("(k p) n -> p k n", p=nrows_p)
        return wget(v, nrows_p, nkt, ncols, "wb:%s:%d" % (name, l))

    H_OFF = 64 * 1024
    XR_OFF = 128 * 1024
    XT_OFF = 160 * 1024
    LNP_OFF = 176 * 1024
    xr = arena_t[:, XR_OFF // 2:(XR_OFF + 32768) // 2].bitcast(F32).rearrange("p (s d) -> p s d", s=4)
    xT = arena_t[:, XT_OFF // 2:(XT_OFF + 16384) // 2].rearrange("p (k t) -> p k t", k=16)
    lnp = arena_t[:, LNP_OFF // 2:(LNP_OFF + 16384) // 2].bitcast(F32).rearrange("p (a d) -> p a d", a=2)
    hT = arena_t[:, H_OFF // 2:(H_OFF + 65536) // 2].rearrange("p (k t) -> p k t", k=64)
    HA = Arena(arena_t, 192 * 1024)
    evc = [0]

    def evac_copy(out, in_, rd, wr):
        evc[0] += 1
        if evc[0] % 2:
            P.op("vector", lambda e: e.tensor_copy(out=out, in_=in_), rd, wr)
        else:
            P.op("scalar", lambda e: e.copy(out=out, in_=in_), rd, wr)

    def transposes_to_xT(xb, rd):
        for kp in range(8):
            b = 6 + kp % 2
            pb = ps[b][:].bitcast(BF16)
            for kk in range(2):
                kt = 2 * kp + kk
                for sub in range(4):
                    P.op("tensor", lambda e, pb=pb, kk=kk, sub=sub, kt=kt: e.transpose(out=pb[:, kk * 512 + sub * 128: kk * 512 + sub * 128 + 128],
                                                                                  in_=xb[:, sub, kt * 128:(kt + 1) * 128], identity=identb), rd + ["identb"], ["ps%d" % b])
            evac_copy(xT[:, 2 * kp:2 * kp + 2, :], pb.rearrange("p (k t) -> p k t", k=2), ["ps%d" % b], ["xT"])

    def layer_norm_all():
        V = "vector"
        sc = lambda sub, c0, c1: smalls[:, 8 + 32 * sub + c0: 8 + 32 * sub + c1]
        for c in range(4):
            for sub in range(4):
                P.op(V, lambda e, c=c, sub=sub: e.bn_stats(out=sc(sub, 6 * c, 6 * c + 6), in_=xr[:, sub, 512 * c:512 * c + 512]), ["xr%d" % sub], ["bst%d" % sub])
        for sub in range(4):
            P.op(V, lambda e, sub=sub: e.bn_aggr(out=sc(sub, 24, 26), in_=sc(sub, 0, 24)), ["bst%d" % sub], ["bag%d" % sub])
        for sub in range(4):
            P.op("scalar", lambda e, sub=sub: e.activation(out=sc(sub, 26, 27), in_=sc(sub, 25, 26), func=AF.Sqrt, bias=epst, scale=1.0), ["bag%d" % sub, "eps"], ["bsd%d" % sub])
        for sub in range(4):
            P.op(V, lambda e, sub=sub: e.reciprocal(out=sc(sub, 27, 28), in_=sc(sub, 26, 27)), ["bsd%d" % sub], ["brs%d" % sub])
        for sub in range(4):
            P.op(V, lambda e, sub=sub: e.scalar_tensor_tensor(out=sc(sub, 28, 29), in0=sc(sub, 24, 25), scalar=-1.0, in1=sc(sub, 27, 28), op0=ALU.mult, op1=ALU.mult), ["bag%d" % sub, "brs%d" % sub], ["bnb%d" % sub])
        for sub in range(4):
            P.op("scalar", lambda e, sub=sub: e.activation(out=xr[:, sub, :], in_=xr[:, sub, :], func=AF.Identity, bias=sc(sub, 28, 29), scale=sc(sub, 27, 28)), ["xr%d" % sub, "brs%d" % sub, "bnb%d" % sub], ["xr%d" % sub])
        for sub in range(4):
            eng = V if sub % 2 == 0 else "gpsimd"
            P.op(eng, lambda e, sub=sub: e.tensor_tensor(out=xr[:, sub, :], in0=xr[:, sub, :], in1=lnp[:, 0, :], op=ALU.mult), ["xr%d" % sub, "lnp"], ["xr%d" % sub])
        for sub in range(4):
            eng = "gpsimd" if sub % 2 == 0 else V
            P.op(eng, lambda e, sub=sub: e.tensor_tensor(out=xr[:, sub, :], in0=xr[:, sub, :], in1=lnp[:, 1, :], op=ALU.add), ["xr%d" % sub, "lnp"], ["xr%d" % sub])

    epst = smalls[:, 0:1]
    P.op("vector", lambda e: e.memset(epst, LN_EPS), [], ["eps"])

    for s in range(NS):
        for li in range(DEPTH):
            j = li // 2
            is_ssm = (li % 2 == 0)
            src = x_in if li == 0 else xbuf
            dst = y_out if li == DEPTH - 1 else xbuf
            lp = "s%dl%d" % (s, li)
            P.barrier()
            HA.reset(H_OFF)
            memb = HA.alloc([2, 2048], BF16)
            memT = HA.alloc([16, 256], BF16)
            dma("sync", xr[:, 0:2, :], mem_in[s].rearrange("(m p) d -> p m d", p=128), [], ["xr0", "xr1"], "xr")
            P.op("vector", lambda e: e.tensor_copy(out=memb[:, 0, :], in_=xr[:, 0, :]), ["xr0"], ["memb"])
            P.op("scalar", lambda e: e.copy(out=memb[:, 1, :], in_=xr[:, 1, :]), ["xr1"], ["memb"])
            for kp in range(4):
                b = 6 + kp % 2
                pb = ps[b][:].bitcast(BF16)
                for kk in range(4):
                    kt = 4 * kp + kk
                    for m in range(2):
                        P.op("tensor", lambda e, pb=pb, kk=kk, m=m, kt=kt: e.transpose(out=pb[:, kk * 256 + m * 128: kk * 256 + m * 128 + 128], in_=memb[:, m, kt * 128:(kt + 1) * 128], identity=identb),
                             ["memb", "identb"], ["ps%d" % b])
                evac_copy(memT[:, 4 * kp:4 * kp + 4, :], pb.rearrange("p (k t) -> p k t", k=4), ["ps%d" % b], ["memT"])
            wc, wr_ = wchunk("w_mem_kv", li, 0, 128, 16, 0, 512)
            for h in range(4):
                b = nextps()
                for kt in range(16):
                    P.op("tensor", lambda e, b=b, h=h, kt=kt, wc=wc: e.matmul(ps[b][:, 0:256], lhsT=wc[:, kt, h * 128:(h + 1) * 128], rhs=memT[:, kt, :], start=(kt == 0), stop=(kt == 15)),
                         ["memT", wr_], ["ps%d" % b])
                evac_copy(kmemT[:, h, :], ps[b][:, 0:256], ["ps%d" % b], ["kmemT"])
            wc, wr_ = wchunk("w_mem_kv", li, 0, 128, 16, 512, 512)
            for m in range(2):
                b = nextps()
                for kt in range(16):
                    P.op("tensor", lambda e, b=b, m=m, kt=kt, wc=wc: e.matmul(ps[b][:], lhsT=memT[:, kt, m * 128:(m + 1) * 128], rhs=wc[:, kt, :], start=(kt == 0), stop=(kt == 15)),
                         ["memT", wr_], ["ps%d" % b])
                evac_copy(vmem[:, m, :], ps[b][:], ["ps%d" % b], ["vmem"])
            if not is_ssm:
                dma("sync", sinkt, w["attn_sink"][j].partition_broadcast(128), [], ["sink"], "sink")
            if stop == 'mem':
                return finish()
            P.barrier()
            HA.reset(H_OFF)
            xb = HA.alloc([4, 2048], BF16)
            if is_ssm:
                ust = HA.alloc([16, 512], BF16)
                qst = HA.alloc([4, 512], BF16)
            else:
                qTst = HA.alloc([16, 512], BF16)
                qst = HA.alloc([4, 512], BF16)
                vst = HA.alloc([4, 512], BF16)
                rk = HA.alloc([4, 4, 128], BF16)
                rt = [HA.alloc([4, 64], F32) for _ in range(4)]
            for t in range(NT):
                tk = slice(t * 512, (t + 1) * 512)
                dma("sync", xr, src[s, tk, :].rearrange("(u p) d -> p u d", p=128), [], ["xr0", "xr1", "xr2", "xr3"], "xr")
                for sub in range(4):
                    eng = ("vector", "scalar", "gpsimd", "vector")[sub]
                    if eng == "scalar":
                        P.op(eng, lambda e, sub=sub, xb=xb: e.copy(out=xb[:, sub, :], in_=xr[:, sub, :]), ["xr%d" % sub], ["xb%d" % sub])
                    else:
                        P.op(eng, lambda e, sub=sub, xb=xb: e.tensor_copy(out=xb[:, sub, :], in_=xr[:, sub, :]), ["xr%d" % sub], ["xb%d" % sub])
                transposes_to_xT(xb, ["xb0", "xb1", "xb2", "xb3"])
                if is_ssm:
                    for c in range(4):
                        wc, wr_ = wchunk("ssm_w_in", j, 0, 128, 16, 384 * c, 384)
                        for fi in range(4):
                            b = nextps()
                            for kt in range(16):
                                P.op("tensor", lambda e, b=b, fi=fi, kt=kt, wc=wc: e.matmul(ps[b][0:96, :], lhsT=wc[:, kt, fi * 96:(fi + 1) * 96], rhs=xT[:, kt, :], start=(kt == 0), stop=(kt == 15)),
                                     ["xT", wr_], ["ps%d" % b])
                            evac_copy(ust[0:96, 4 * c + fi, :], ps[b][0:96, :], ["ps%d" % b], ["ust"])
                    dma("gpsimd", U_d[s, :, :, tk].rearrange("f p t -> p f t"), ust[0:96], ["ust"], ["U_d"], "ust")
                    wc, wr_ = wchunk("ssm_w_in", j, 0, 128, 16, 1536, 512)
                else:
                    dma("sync", cst[:, 0], c_cos[tk, :].rearrange("(u p) f -> p u f", p=128), [], ["cst"], "cst")
                    dma("sync", cst[:, 1], c_sin[tk, :].rearrange("(u p) f -> p u f", p=128), [], ["cst"], "cst")
                    for c in range(4):
                        wc, wr_ = wchunk("attn_w_in", j, 0, 128, 16, 512 * c, 512)
                        for sub in range(4):
                            b = nextps()
                            for kt in range(16):
                                P.op("tensor", lambda e, b=b, sub=sub, kt=kt, wc=wc: e.matmul(ps[b][:], lhsT=xT[:, kt, sub * 128:(sub + 1) * 128], rhs=wc[:, kt, :], start=(kt == 0), stop=(kt == 15)),
                                     ["xT", wr_], ["ps%d" % b])
                            pv = ps[b][:].rearrange("p (h two f) -> p h two f", h=4, two=2)
                            cosb = cst[:, 0, sub, :].unsqueeze(1).to_broadcast([128, 4, 64])
                            sinb = cst[:, 1, sub, :].unsqueeze(1).to_broadcast([128, 4, 64])
                            rn = "rt%d" % sub
                            P.op("vector", lambda e, pv=pv, cosb=cosb: e.tensor_tensor(out=rt[0], in0=pv[:, :, 0, :], in1=cosb, op=ALU.mult), ["ps%d" % b, "cst"], ["rt0"])
                            P.op("vector", lambda e, pv=pv, sinb=sinb: e.tensor_tensor(out=rt[1], in0=pv[:, :, 1, :], in1=sinb, op=ALU.mult), ["ps%d" % b, "cst"], ["rt1"])
                            P.op("vector", lambda e, pv=pv, cosb=cosb: e.tensor_tensor(out=rt[2], in0=pv[:, :, 1, :], in1=cosb, op=ALU.mult), ["ps%d" % b, "cst"], ["rt2"])
                            P.op("vector", lambda e, pv=pv, sinb=sinb: e.tensor_tensor(out=rt[3], in0=pv[:, :, 0, :], in1=sinb, op=ALU.mult), ["ps%d" % b, "cst"], ["rt3"])
                            P.op("gpsimd", lambda e, sub=sub: e.tensor_tensor(out=rk[:, sub, :, 0:64], in0=rt[0], in1=rt[1], op=ALU.subtract), ["rt0", "rt1"], ["rk%d" % sub])
                            P.op("gpsimd", lambda e, sub=sub: e.tensor_tensor(out=rk[:, sub, :, 64:128], in0=rt[2], in1=rt[3], op=ALU.add), ["rt2", "rt3"], ["rk%d" % sub])
                        for hp in range(2):
                            b = 6 + hp % 2
                            pb = ps[b][:].bitcast(BF16)
                            for hh in range(2):
                                for sub in range(4):
                                    P.op("tensor", lambda e, pb=pb, hh=hh, sub=sub, hp=hp: e.transpose(out=pb[:, hh * 512 + sub * 128: hh * 512 + sub * 128 + 128], in_=rk[:, sub, 2 * hp + hh, :], identity=identb),
                                         ["rk%d" % sub, "identb"], ["ps%d" % b])
                            evac_copy(qTst[:, 4 * c + 2 * hp: 4 * c + 2 * hp + 2, :], pb.rearrange("p (k t) -> p k t", k=2), ["ps%d" % b], ["qTst"])
                    dma("gpsimd", QT_d[s, :, :, tk].rearrange("f p t -> p f t"), qTst[:, 0:12, :], ["qTst"], ["QT_d"], "qTst")
                    dma("gpsimd", KT_d[s, :, :, tk].rearrange("f p t -> p f t"), qTst[:, 12:16, :], ["qTst"], ["KT_d"], "kTst")
                    wc, wr_ = wchunk("attn_w_in", j, 0, 128, 16, 2048, 512)
                    for sub in range(4):
                        b = nextps()
                        for kt in range(16):
                            P.op("tensor", lambda e, b=b, sub=sub, kt=kt, wc=wc: e.matmul(ps[b][:], lhsT=xT[:, kt, sub * 128:(sub + 1) * 128], rhs=wc[:, kt, :], start=(kt == 0), stop=(kt == 15)),
                                 ["xT", wr_], ["ps%d" % b])
                        evac_copy(vst[:, sub, :], ps[b][:], ["ps%d" % b], ["vst"])
                    dma("gpsimd", V_d[s, tk, :].rearrange("(u p) f -> p u f", p=128), vst, ["vst"], ["V_d"], "vst")
                    wc, wr_ = wchunk("attn_w_in", j, 0, 128, 16, 2560, 512)
                for m in range(4):
                    b = nextps()
                    for kt in range(16):
                        P.op("tensor", lambda e, b=b, m=m, kt=kt, wc=wc: e.matmul(ps[b][:], lhsT=wc[:, kt, m * 128:(m + 1) * 128], rhs=xT[:, kt, :], start=(kt == 0), stop=(kt == 15)),
                             ["xT", wr_], ["ps%d" % b])
                    evac_copy(qst[:, m, :], ps[b][:], ["ps%d" % b], ["qst"])
                dma("gpsimd", QM_d[s, :, :, tk].rearrange("f p t -> p f t"), qst, ["qst"], ["QM_d"], "qst")
            if stop == 'A':
                return finish()
            if is_ssm:
                P.barrier()
                HA.reset(0)
                unat = HA.alloc([1, L], BF16)[:, 0, :]
                uperm = HA.alloc([TC, NCH], BF16)
                znat = HA.alloc([1, L], BF16)[:, 0, :]
                Qt = HA.alloc([2, 32, 2, 128], BF16)
                Pt = HA.alloc([3, 2, 32, 2, 32], BF16)
                Kt = HA.alloc([2, 32, 96], BF16)
                SC = HA.alloc([2, 2, 7, 48], F32)
                X = [HA.alloc([3, 2, 2, NCH], F32) for _ in range(2)]
                TM = [HA.alloc([3, 2, 2, NCH], F32) for _ in range(2)]
                Hb = HA.alloc([3, 2, 2, NCH], BF16)
                g1 = HA.alloc([1, 2048], F32)[:, 0, :]
                g2t = HA.alloc([1, 2048], F32)[:, 0, :]
                dma("sync", SC[:, 0], SC_d[j, 0], [], ["SC"], "SC")
                dma("sync", SC[:, 1], SC_d[j, 1], [], ["SC"], "SC")
                NBK = L // 512
                TPB = 512 // NCH
                for ft in range(16):
                    g0 = 3 * ft
                    dma("sync", unat[0:96, :], U_d[s, ft], ["U_d"], ["unat"], "unat")
                    dma("sync", Qt[0:96].rearrange("q a b c d -> q (a b c d)"), Qd[j, ft], ["Qd"], ["Qt"], "Qt")
                    dma("sync", Pt.rearrange("q a b c d e -> q (a b c d e)"), Pd[j, ft], ["Pd"], ["Pt"], "Pt")
                    dma("sync", Kt[0:96].rearrange("q a b c -> q (a b c)"), Kd[j, ft], ["Kd"], ["Kt"], "Kt")
                    P.op("vector", lambda e: e.tensor_copy(out=uperm[0:96], in_=unat[0:96, :].rearrange("p (k t) -> p t k", t=TC)), ["unat"], ["uperm"])
                    if stop == 'B0':
                        return finish()
                    for gl in range(3):
                        for d in range(2):
                            for c in range(2):
                                b = gl
                                col = (d * 2 + c) * NCH
                                for sg in range(TC):
                                    e_ = (31 - sg) if d == 0 else sg
                                    P.op("tensor", lambda e, b=b, col=col, gl=gl, d=d, c=c, sg=sg, e_=e_: e.matmul(
                                        ps[b][:, col:col + NCH], lhsT=Qt[32 * gl:32 * gl + 32, d, e_, c, :], rhs=uperm[32 * gl:32 * gl + 32, sg, :],
                                        start=(sg == 0), stop=(sg == TC - 1), skip_group_check=True), ["Qt", "uperm"], ["ps%d" % b])
                    for gl in range(3):
                        xo = X[0][:, gl].rearrange("q d c k -> q (d c) k")
                        evac_copy(xo, ps[gl][:, 0:4 * NCH].rearrange("q (a k) -> q a k", k=NCH), ["ps%d" % gl], ["X0d0", "X0d1"])
                    if stop == 'B1':
                        return finish()
                    cur = 0
                    for s_ in range(NSTEP):
                        sh = 2 ** s_
                        n_ = NCH - sh
                        Xi, Xo = X[cur], X[1 - cur]
                        ar = lambda d, c, s_=s_, g0=g0, n_=n_: SC[:, d, c, s_, g0:g0 + 3].unsqueeze(2).to_broadcast([128, 3, n_])
                        for d in range(2):
                            if d == 0:
                                dsl, ssl, ksl = slice(sh, NCH), slice(0, n_), slice(0, sh)
                            else:
                                dsl, ssl, ksl = slice(0, n_), slice(sh, NCH), slice(n_, NCH)
                            eng = "vector" if d == 0 else "gpsimd"
                            t0_, t1_ = TM[0][:, :, d, 0, 0:n_], TM[1][:, :, d, 0, 0:n_]
                            u0_, u1_ = TM[0][:, :, d, 1, 0:n_], TM[1][:, :, d, 1, 0:n_]
                            tn = "TM%d" % d
                            ci, co_ = "X%dd%d" % (cur, d), "X%dd%d" % (1 - cur, d)
                            xre, xim = Xi[:, :, d, 0, ssl], Xi[:, :, d, 1, ssl]
                            P.op(eng, lambda e, t0_=t0_, xre=xre, d=d, ar=ar: e.tensor_tensor(out=t0_, in0=xre, in1=ar(d, 0), op=ALU.mult), [ci, "SC"], [tn + "a"])
                            P.op(eng, lambda e, u0_=u0_, xim=xim, d=d, ar=ar: e.tensor_tensor(out=u0_, in0=xim, in1=ar(d, 0), op=ALU.mult), [ci, "SC"], [tn + "c"])
                            P.op(eng, lambda e, t1_=t1_, xim=xim, d=d, ar=ar: e.tensor_tensor(out=t1_, in0=xim, in1=ar(d, 1), op=ALU.mult), [ci, "SC"], [tn + "b"])
                            P.op(eng, lambda e, u1_=u1_, xre=xre, d=d, ar=ar: e.tensor_tensor(out=u1_, in0=xre, in1=ar(d, 1), op=ALU.mult), [ci, "SC"], [tn + "d"])
                            P.op(eng, lambda e, t0_=t0_, t1_=t1_: e.tensor_tensor(out=t0_, in0=t0_, in1=t1_, op=ALU.subtract), [tn + "a", tn + "b"], [tn + "a"])
                            P.op(eng, lambda e, u0_=u0_, u1_=u1_: e.tensor_tensor(out=u0_, in0=u0_, in1=u1_, op=ALU.add), [tn + "c", tn + "d"], [tn + "c"])
                            P.op(eng, lambda e, Xo=Xo, Xi=Xi, d=d, dsl=dsl, t0_=t0_: e.tensor_tensor(out=Xo[:, :, d, 0, dsl], in0=Xi[:, :, d, 0, dsl], in1=t0_, op=ALU.add), [ci, tn + "a"], [co_])
                            P.op(eng, lambda e, Xo=Xo, Xi=Xi, d=d, dsl=dsl, u0_=u0_: e.tensor_tensor(out=Xo[:, :, d, 1, dsl], in0=Xi[:, :, d, 1, dsl], in1=u0_, op=ALU.add), [ci, tn + "c"], [co_])
                            P.op("scalar", lambda e, Xo=Xo, Xi=Xi, d=d, ksl=ksl: e.copy(out=Xo[:, :, d, :, ksl], in_=Xi[:, :, d, :, ksl]), [ci], [co_])
                        cur = 1 - cur
                    P.op("gpsimd", lambda e: e.memset(Hb, 0.0), [], ["Hb"])
                    P.op("vector", lambda e, cur=cur: e.tensor_copy(out=Hb[:, :, 0, :, 1:NCH], in_=X[cur][:, :, 0, :, 0:NCH - 1]), ["X%dd0" % cur, "Hb"], ["Hb"])
                    P.op("vector", lambda e, cur=cur: e.tensor_copy(out=Hb[:, :, 1, :, 0:NCH - 1], in_=X[cur][:, :, 1, :, 1:NCH]), ["X%dd1" % cur, "Hb"], ["Hb"])
                    if stop == 'B2':
                        return finish()
                    for b in range(NBK):
                        t_lo = b * TPB
                        first = True
                        yb = ps[b][0:96, :].rearrange("q (t k) -> q t k", k=NCH)
                        for d in range(2):
                            for l in range(min(TC, t_lo + TPB)):
                                if d == 0:
                                    ta = max(l, t_lo); tb_ = t_lo + TPB
                                    if ta >= tb_:
                                        continue
                                    out_ = yb[:, ta - t_lo:tb_ - t_lo, :]
                                    rhs_ = uperm[0:96, ta - l:tb_ - l, :]
                                else:
                                    continue
                                P.op("tensor", lambda e, out_=out_, rhs_=rhs_, d=d, l=l, first=first: e.matmul(out_, lhsT=Kt[0:96, d, l, :], rhs=rhs_, start=first, stop=False, skip_group_check=True),
                                     ["Kt", "uperm"], ["ps%d" % b])
                                first = False
                        for l in range(TC):
                            ta = t_lo; tb_ = min(t_lo + TPB, TC - l)
                            if ta >= tb_:
                                continue
                            out_ = yb[:, ta - t_lo:tb_ - t_lo, :]
                            rhs_ = uperm[0:96, ta + l:tb_ + l, :]
                            P.op("tensor", lambda e, out_=out_, rhs_=rhs_, l=l: e.matmul(out_, lhsT=Kt[0:96, 1, l, :], rhs=rhs_, start=False, stop=False, skip_group_check=True),
                                 ["Kt", "uperm"], ["ps%d" % b])
                        if stop == 'B3':
                            return finish()
                        for tau in range(t_lo, t_lo + TPB):
                            for gl in range(3):
                                for d in range(2):
                                    e_ = tau if d == 0 else 31 - tau
                                    for c in range(2):
                                        out_ = yb[32 * gl:32 * gl + 32, tau - t_lo, :]
                                        rhs_ = Hb[:, gl, d, c, :]
                                        last = (tau == t_lo + TPB - 1 and gl == 2 and d == 1 and c == 1)
                                        P.op("tensor", lambda e, out_=out_, rhs_=rhs_, gl=gl, d=d, e_=e_, c=c, last=last: e.matmul(out_, lhsT=Pt[:, gl, d, e_, c, :], rhs=rhs_, start=False, stop=last, skip_group_check=True),
                                             ["Pt", "Hb"], ["ps%d" % b])
                        if stop == 'B4':
                            return finish()
                        ysb = g1[0:96, 0:512]
                        tt = g2t[0:96, 0:512]
                        P.op("vector", lambda e, b=b, ysb=ysb: e.tensor_copy(out=ysb, in_=ps[b][0:96, :]), ["ps%d" % b], ["g1"])
                        P.op("gpsimd", lambda e, ysb=ysb, tt=tt: e.tensor_tensor(out=tt, in0=ysb, in1=ysb, op=ALU.mult), ["g1"], ["g2"])
                        P.op("gpsimd", lambda e, tt=tt: e.tensor_scalar(out=tt, in0=tt, scalar1=0.044715, scalar2=1.0, op0=ALU.mult, op1=ALU.add), ["g2"], ["g2"])
                        P.op("gpsimd", lambda e, ysb=ysb, tt=tt: e.tensor_tensor(out=tt, in0=tt, in1=ysb, op=ALU.mult), ["g1", "g2"], ["g2"])
                        P.op("scalar", lambda e, tt=tt: e.activation(out=tt, in_=tt, func=AF.Sigmoid, scale=float(2.0 * np.sqrt(2.0 / np.pi))), ["g2"], ["g2"])
                        zv = znat[0:96, :].rearrange("p (k t) -> p t k", t=TC)[:, t_lo:t_lo + TPB, :]
                        P.op("vector", lambda e, zv=zv, ysb=ysb, tt=tt: e.tensor_tensor(out=zv, in0=ysb.rearrange("p (t k) -> p t k", k=NCH), in1=tt.rearrange("p (t k) -> p t k", k=NCH), op=ALU.mult), ["g1", "g2"], ["znat"])
                    dma("gpsimd", Z_d[s, ft], znat[0:96, :], ["znat"], ["Z_d"], "znat")
            if stop == 'B':
                return finish()
            P.barrier()
            for t in range(NT):
                tk = slice(t * 512, (t + 1) * 512)
                HA.reset(H_OFF)
                ymixT = HA.alloc([12, 512], BF16)
                ymemT = HA.alloc([4, 512], BF16)
                qmT = HA.alloc([4, 512], BF16)
                smm = [HA.alloc([1, 256], F32)[:, 0, :] for _ in range(4)]
                pm = [HA.alloc([1, 256], BF16)[:, 0, :] for _ in range(4)]
                pmT = [HA.alloc([2, 128], BF16) for _ in range(4)]
                dma("sync", qmT, QM_d[s, :, :, tk].rearrange("f p t -> p f t"), ["QM_d"], ["qmT", "hT0", "hT1", "hT2", "hT3"], "qmT")
                if is_ssm:
                    zT = HA.alloc([16, 512], BF16)
                    sg_ = HA.alloc([1, 512], F32)[:, 0, :]
                    dma("sync", zT[0:96], Z_d[s, :, :, tk].rearrange("f p t -> p f t"), ["Z_d"], ["zT", "hT0", "hT1", "hT2", "hT3"], "zT")
                    for c in range(3):
                        wa, wra = wchunk("ssm_w_glu", j, 0, 96, 16, 512 * c, 512)
                        wg, wrg = wchunk("ssm_w_glu", j, 0, 96, 16, 1536 + 512 * c, 512)
                        for mi in range(4):
                            ba = nextps(); bg = nextps()
                            for kt in range(16):
                                P.op("tensor", lambda e, ba=ba, mi=mi, kt=kt, wa=wa: e.matmul(ps[ba][:], lhsT=wa[0:96, kt, mi * 128:(mi + 1) * 128], rhs=zT[0:96, kt, :], start=(kt == 0), stop=(kt == 15)),
                                     ["zT", wra], ["ps%d" % ba])
                            for kt in range(16):
                                P.op("tensor", lambda e, bg=bg, mi=mi, kt=kt, wg=wg: e.matmul(ps[bg][:], lhsT=wg[0:96, kt, mi * 128:(mi + 1) * 128], rhs=zT[0:96, kt, :], start=(kt == 0), stop=(kt == 15)),
                                     ["zT", wrg], ["ps%d" % bg])
                            P.op("scalar", lambda e, bg=bg: e.activation(out=sg_, in_=ps[bg][:], func=AF.Sigmoid), ["ps%d" % bg], ["sg"])
                            P.op("vector", lambda e, ba=ba, c=c, mi=mi: e.tensor_tensor(out=ymixT[:, 4 * c + mi, :], in0=ps[ba][:], in1=sg_, op=ALU.mult), ["ps%d" % ba, "sg"], ["ymixT"])
                else:
                    qT = HA.alloc([12, 512], BF16)
                    kT = HA.alloc([4, 768], BF16)
                    vv = HA.alloc([6, 512], BF16)
                    k0 = max(0, t * 512 - 128); k1 = min(L, t * 512 + 640)
                    ko = 128 - (t * 512 - k0)
                    dma("sync", qT, QT_d[s, :, :, tk].rearrange("f p t -> p f t"), ["QT_d"], ["qT", "hT0", "hT1", "hT2", "hT3"], "qT")
                    dma("sync", kT[:, :, ko:ko + (k1 - k0)], KT_d[s, :, :, k0:k1].rearrange("f p t -> p f t"), ["KT_d"], ["kT", "hT0", "hT1", "hT2", "hT3"], "kT")
                    dma("sync", vv[:, ko // 128: ko // 128 + (k1 - k0) // 128, :], V_d[s, k0:k1, :].rearrange("(u p) f -> p u f", p=128), ["V_d"], ["vv", "hT0", "hT1", "hT2", "hT3"], "vv")
                    sm = [HA.alloc([1, 384], F32)[:, 0, :] for _ in range(3)]
                    pp = [HA.alloc([1, 384], BF16)[:, 0, :] for _ in range(3)]
                    pT = [HA.alloc([3, 128], BF16) for _ in range(3)]
                    for kvh in range(4):
                        ybk = [nextps(3, 3) for _ in range(3)]
                        for qb in range(4):
                            gq = t * 4 + qb
                            kb0 = 0 if gq > 0 else 1
                            kb1 = 3 if gq < NB - 1 else 2
                            nk = (kb1 - kb0) * 128
                            kcol = qb * 128 + kb0 * 128
                            for hh in range(3):
                                h = kvh * 3 + hh
                                b = hh
                                so = 32 + 8 * hh
                                P.op("tensor", lambda e, b=b, h=h, qb=qb, kcol=kcol, nk=nk, kvh=kvh: e.matmul(ps[b][:, 0:nk], lhsT=qT[:, h, qb * 128:(qb + 1) * 128], rhs=kT[:, kvh, kcol:kcol + nk], start=True, stop=True),
                                     ["qT", "kT"], ["ps%d" % b])
                                P.op("vector", lambda e, b=b, hh=hh, nk=nk, kb0=kb0: e.tensor_tensor(out=sm[hh][:, 0:nk], in0=ps[b][:, 0:nk], in1=maskt[:, kb0 * 128:kb0 * 128 + nk], op=ALU.add), ["ps%d" % b, "mask"], ["sm%d" % hh])
                                P.op("vector", lambda e, hh=hh, nk=nk, so=so: e.reduce_max(out=stats[:, so:so + 1], in_=sm[hh][:, 0:nk], axis=mybir.AxisListType.X), ["sm%d" % hh], ["st%da" % hh])
                                P.op("vector", lambda e, so=so, h=h: e.scalar_tensor_tensor(out=stats[:, so + 1:so + 2], in0=stats[:, so:so + 1], scalar=float(HD ** -0.5), in1=sinkt[:, h:h + 1], op0=ALU.mult, op1=ALU.max), ["st%da" % hh, "sink"], ["st%db" % hh])
                                P.op("vector", lambda e, so=so: e.tensor_scalar(out=stats[:, so + 2:so + 3], in0=stats[:, so + 1:so + 2], scalar1=-1.0, scalar2=None, op0=ALU.mult), ["st%db" % hh], ["st%dc" % hh])
                                P.op("scalar", lambda e, hh=hh, nk=nk, so=so: e.activation(out=pp[hh][:, 0:nk], in_=sm[hh][:, 0:nk], func=AF.Exp, bias=stats[:, so + 2:so + 3], scale=float(HD ** -0.5), accum_out=stats[:, so + 3:so + 4]),
                                     ["sm%d" % hh, "st%dc" % hh], ["pp%d" % hh, "st%dd" % hh])
                                P.op("scalar", lambda e, so=so, h=h: e.activation(out=stats[:, so + 4:so + 5], in_=stats[:, so + 2:so + 3], func=AF.Exp, bias=sinkt[:, h:h + 1], scale=1.0), ["st%dc" % hh, "sink"], ["st%de" % hh])
                                P.op("vector", lambda e, so=so: e.tensor_tensor(out=stats[:, so + 5:so + 6], in0=stats[:, so + 3:so + 4], in1=stats[:, so + 4:so + 5], op=ALU.add), ["st%dd" % hh, "st%de" % hh], ["st%df" % hh])
                                P.op("vector", lambda e, so=so: e.reciprocal(out=stats[:, so + 6:so + 7], in_=stats[:, so + 5:so + 6]), ["st%df" % hh], ["st%dg" % hh])
                                P.op("vector", lambda e, hh=hh, nk=nk, so=so: e.tensor_scalar(out=pp[hh][:, 0:nk], in0=pp[hh][:, 0:nk], scalar1=stats[:, so + 6:so + 7], scalar2=None, op0=ALU.mult), ["pp%d" % hh, "st%dg" % hh], ["pp%d" % hh])
                                bt = 6 + hh % 2
                                pb = ps[bt][:].bitcast(BF16)
                                for kb in range(kb1 - kb0):
                                    P.op("tensor", lambda e, pb=pb, hh=hh, kb=kb: e.transpose(out=pb[:, (hh // 2) * 512 + kb * 128:(hh // 2) * 512 + kb * 128 + 128], in_=pp[hh][:, kb * 128:(kb + 1) * 128], identity=identb),
                                         ["pp%d" % hh, "identb"], ["ps%d" % bt])
                                nkb = kb1 - kb0
                                evac_copy(pT[hh][:, 0:nkb, :], pb[:, (hh // 2) * 512:(hh // 2) * 512 + nkb * 128].rearrange("p (k t) -> p k t", k=nkb), ["ps%d" % bt], ["pT%d" % hh])
                                yb_ = ybk[hh]
                                for kb in range(nkb):
                                    vb = qb + kb0 + kb
                                    P.op("tensor", lambda e, yb_=yb_, qb=qb, kb=kb, vb=vb, hh=hh, kvh=kvh, nkb=nkb: e.matmul(ps[yb_][:, qb * 128:(qb + 1) * 128], lhsT=vv[:, vb, kvh * 128:(kvh + 1) * 128], rhs=pT[hh][:, kb, :],
                                                                                                                        start=(kb == 0), stop=(kb == nkb - 1), skip_group_check=True), ["vv", "pT%d" % hh], ["ps%d" % yb_])
                        for hh in range(3):
                            evac_copy(ymixT[:, kvh * 3 + hh, :], ps[ybk[hh]][:], ["ps%d" % ybk[hh]], ["ymixT"])
                for qb in range(4):
                    for h in range(4):
                        b = h % 2
                        so = 32 + 8 * h
                        hn = "m%d" % h
                        P.op("tensor", lambda e, b=b, h=h, qb=qb: e.matmul(ps[b][:, 0:256], lhsT=qmT[:, h, qb * 128:(qb + 1) * 128], rhs=kmemT[:, h, :], start=True, stop=True), ["qmT", "kmemT"], ["ps%d" % b])
                        P.op("vector", lambda e, b=b, h=h: e.tensor_copy(out=smm[h], in_=ps[b][:, 0:256]), ["ps%d" % b], [hn + "s"])
                        P.op("vector", lambda e, h=h, so=so: e.reduce_max(out=stats[:, so:so + 1], in_=smm[h], axis=mybir.AxisListType.X), [hn + "s"], [hn + "a"])
                        P.op("vector", lambda e, so=so: e.tensor_scalar(out=stats[:, so + 2:so + 3], in0=stats[:, so:so + 1], scalar1=float(-(HD ** -0.5)), scalar2=None, op0=ALU.mult), [hn + "a"], [hn + "c"])
                        P.op("scalar", lambda e, h=h, so=so: e.activation(out=pm[h], in_=smm[h], func=AF.Exp, bias=stats[:, so + 2:so + 3], scale=float(HD ** -0.5), accum_out=stats[:, so + 3:so + 4]), [hn + "s", hn + "c"], [hn + "p", hn + "d"])
                        P.op("vector", lambda e, so=so: e.reciprocal(out=stats[:, so + 6:so + 7], in_=stats[:, so + 3:so + 4]), [hn + "d"], [hn + "g"])
                        P.op("vector", lambda e, h=h, so=so: e.tensor_scalar(out=pm[h], in0=pm[h], scalar1=stats[:, so + 6:so + 7], scalar2=None, op0=ALU.mult), [hn + "p", hn + "g"], [hn + "p"])
                        bt = 6 + h % 2
                        pb = ps[bt][:].bitcast(BF16)
                        for kb in range(2):
                            P.op("tensor", lambda e, pb=pb, h=h, kb=kb: e.transpose(out=pb[:, (h // 2) * 256 + kb * 128:(h // 2) * 256 + kb * 128 + 128], in_=pm[h][:, kb * 128:(kb + 1) * 128], identity=identb), [hn + "p", "identb"], ["ps%d" % bt])
                        evac_copy(pmT[h], pb[:, (h // 2) * 256:(h // 2) * 256 + 256].rearrange("p (k t) -> p k t", k=2), ["ps%d" % bt], [hn + "T"])
                        for kb in range(2):
                            P.op("tensor", lambda e, h=h, kb=kb, qb=qb: e.matmul(ps[2 + h][:, qb * 128:(qb + 1) * 128], lhsT=vmem[:, kb, h * 128:(h + 1) * 128], rhs=pmT[h][:, kb, :],
                                                                              start=(kb == 0), stop=(kb == 1), skip_group_check=True), [hn + "T", "vmem"], ["ps%d" % (2 + h)])
                for h in range(4):
                    evac_copy(ymemT[:, h, :], ps[2 + h][:], ["ps%d" % (2 + h)], ["ymemT"])
                dma("sync", xr, src[s, tk, :].rearrange("(u p) d -> p u d", p=128), [], ["xr0", "xr1", "xr2", "xr3"], "xr")
                dma("sync", lnp[:, 0, :], w["ln1_g"][li].partition_broadcast(128), [], ["lnp"], "lnp")
                dma("sync", lnp[:, 1, :], w["ln1_b"][li].partition_broadcast(128), [], ["lnp"], "lnp")
                for fc in range(4):
                    wc, wr_ = wchunk("w_out", li, 0, 128, 16, 512 * fc, 512)
                    for sub in range(4):
                        b = nextps(3, 0)
                        for kt in range(16):
                            lhs = ymixT[:, kt, sub * 128:(sub + 1) * 128] if kt < 12 else ymemT[:, kt - 12, sub * 128:(sub + 1) * 128]
                            P.op("tensor", lambda e, b=b, lhs=lhs, kt=kt, wc=wc: e.matmul(ps[b][:], lhsT=lhs, rhs=wc[:, kt, :], start=(kt == 0), stop=(kt == 15)), ["ymixT", "ymemT", wr_], ["ps%d" % b])
                        P.op("vector", lambda e, b=b, sub=sub, fc=fc: e.scalar_tensor_tensor(out=xr[:, sub, 512 * fc:512 * fc + 512], in0=xr[:, sub, 512 * fc:512 * fc + 512], scalar=ALPHA, in1=ps[b][:], op0=ALU.mult, op1=ALU.add),
                             ["ps%d" % b, "xr%d" % sub], ["xr%d" % sub])
                layer_norm_all()
                HA.reset(H_OFF + 49152)
                xb2 = HA.alloc([4, 2048], BF16)
                P.barrier()
                for sub in range(4):
                    eng = ("vector", "scalar", "gpsimd", "vector")[sub]
                    if eng == "scalar":
                        P.op(eng, lambda e, sub=sub, xb2=xb2: e.copy(out=xb2[:, sub, :], in_=xr[:, sub, :]), ["xr%d" % sub], ["xb%d" % sub])
                    else:
                        P.op(eng, lambda e, sub=sub, xb2=xb2: e.tensor_copy(out=xb2[:, sub, :], in_=xr[:, sub, :]), ["xr%d" % sub], ["xb%d" % sub])
                transposes_to_xT(xb2, ["xb0", "xb1", "xb2", "xb3"])
                dma("sync", lnp[:, 0, :], w["ln2_g"][li].partition_broadcast(128), [], ["lnp"], "lnp")
                dma("sync", lnp[:, 1, :], w["ln2_b"][li].partition_broadcast(128), [], ["lnp"], "lnp")
                P.barrier()
                rtmp = [rtmp0, rtmp1]
                for c in range(16):
                    wc, wr_ = wchunk("w_ff1", li, 0, 128, 16, 512 * c, 512)
                    for fi in range(4):
                        f = 4 * c + fi
                        b = 4 + f % 2
                        for kt in range(16):
                            P.op("tensor", lambda e, b=b, fi=fi, kt=kt, wc=wc: e.matmul(ps[b][:], lhsT=wc[:, kt, fi * 128:(fi + 1) * 128], rhs=xT[:, kt, :], start=(kt == 0), stop=(kt == 15)), ["xT", wr_], ["ps%d" % b])
                        r_ = rtmp[f % 2]
                        P.op("scalar", lambda e, b=b, r_=r_: e.activation(out=r_, in_=ps[b][:], func=AF.Relu), ["ps%d" % b], ["rt%d" % (f % 2)])
                        P.op("gpsimd", lambda e, f=f, r_=r_: e.tensor_tensor(out=hT[:, f, :], in0=r_, in1=r_, op=ALU.mult), ["rt%d" % (f % 2)], ["hT%d" % (f // 16)])
                for fc in range(4):
                    for c4 in range(4):
                        v = wb["w_ff2"][li, c4 * 2048:(c4 + 1) * 2048, 512 * fc:512 * fc + 512].rearrange("(k p) n -> p k n", p=128)
                        wc, wr_ = wget(v, 128, 16, 512, "wb:w_ff2:%d" % li)
                        for sub in range(4):
                            for kt in range(16):
                                P.op("tensor", lambda e, sub=sub, kt=kt, c4=c4, wc=wc: e.matmul(ps[sub][:], lhsT=hT[:, c4 * 16 + kt, sub * 128:(sub + 1) * 128], rhs=wc[:, kt, :],
                                                                                          start=(c4 == 0 and kt == 0), stop=(c4 == 3 and kt == 15), skip_group_check=True),
                                     ["hT%d" % c4, wr_], ["ps%d" % sub])
                    for sub in range(4):
                        P.op("vector", lambda e, sub=sub, fc=fc: e.scalar_tensor_tensor(out=xr[:, sub, 512 * fc:512 * fc + 512], in0=xr[:, sub, 512 * fc:512 * fc + 512], scalar=ALPHA, in1=ps[sub][:], op0=ALU.mult, op1=ALU.add),
                             ["ps%d" % sub, "xr%d" % sub], ["xr%d" % sub])
                layer_norm_all()
                dma("gpsimd", dst[s, tk, :].rearrange("(u p) d -> p u d", p=128), xr, ["xr0", "xr1", "xr2", "xr3"], ["out"], "xout")
    P.default_skip = ()
    P.barrier()
    P.op("sync", None, reads=["out"])
    P.run_block()
    st.close()
    return nc


def _consts(L):
    inv = (10000.0 ** (-np.arange(0, 128, 2, dtype=np.float32) / np.float32(128))).astype(np.float32)
    ang = (np.arange(L, dtype=np.float32)[:, None] * inv[None, :]).astype(np.float32)
    i = np.arange(128)[:, None]
    jj = np.arange(384)[None, :]
    mask = np.where(np.abs(jj - 128 - i) <= 128, 0.0, -1e30).astype(np.float32)
    msk2 = np.zeros((128, 2), np.float32)
    msk2[:64, 0] = 1
    msk2[64:, 1] = 1
    return {"c_ident": np.eye(128, dtype=np.float32), "c_cos": np.cos(ang).astype(np.float32), "c_sin": np.sin(ang).astype(np.float32),
            "c_mask": mask, "c_msk2": msk2, "c_ev": np.tile(np.array(EV, np.float32)[None, :], (128, 1))}


def kernel(**inputs):
    L, NS, DEPTH = 4096, 2, 4
    xp = np.asarray(inputs["x_prompt"], np.float32)
    xs = np.asarray(inputs["x_sample"], np.float32)
    mp = np.asarray(inputs["mem_prompt"], np.float32)
    ms = np.asarray(inputs["mem_sample"], np.float32)
    nc = build(L, NS, DEPTH)
    cs = _consts(L)
    wts = {k: np.ascontiguousarray(np.asarray(inputs[k], np.float32)) for k in WSHAPES}
    in_maps = []
    for c in range(8):
        x1 = xs[c] if c < 4 else xp[c]
        m1 = ms[c] if c < 4 else mp[c]
        m = {"x": np.ascontiguousarray(np.stack([xp[c], x1])), "mem": np.ascontiguousarray(np.stack([mp[c], m1]))}
        m.update(wts)
        m.update(cs)
        in_maps.append(m)
    res = run_bass_kernel_spmd(nc, in_maps, core_ids=list(range(8)))
    yp = np.stack([np.asarray(res.results[c]["y"][0], np.float32) for c in range(8)])
    ysm = np.stack([np.asarray(res.results[c]["y"][1], np.float32) for c in range(4)])
    return (yp, ysm)
```
